# Optimizing a Trainium2 kernel written in Bass

```python
import jax, jax.numpy as jnp
from jax import lax
import numpy as np

D_MODEL = 1024
BATCH = 32
SEQ = 2048
DEPTH = 4

N_MIXERS = 2
N_HEADS = 16
N_KV_HEADS = 2
HEAD_DIM = 64
GROUP = N_HEADS // N_KV_HEADS
WINDOW = 128
BLOCK = 128
SPAN = BLOCK + WINDOW
Q_DIM = N_HEADS * HEAD_DIM
KV_DIM = N_KV_HEADS * HEAD_DIM
QKV_DIM = Q_DIM + 2 * KV_DIM
CONV_WIDTH = 31
CONV_DIM = D_MODEL
D_FF = 4 * D_MODEL
N_MOD = 6
EPS = 1e-6
N_ATTN_LAYERS = (DEPTH + 1) // 2
N_CONV_LAYERS = DEPTH // 2

kernel_name = "interleaved_swa_sink_conformer_conv_adaln"


def alibi_slopes(n_heads):
    return jnp.asarray(np.array([2.0 ** (-8.0 * (h + 1) / n_heads) for h in range(n_heads)], dtype=np.float32))


def rmsnorm(x, g):
    xf = x.astype(jnp.float32)
    r = lax.rsqrt(jnp.mean(xf * xf, axis=-1, keepdims=True) + EPS)
    return (xf * r * g.astype(jnp.float32)).astype(x.dtype)


def layernorm(x, g, b):
    xf = x.astype(jnp.float32)
    mu = jnp.mean(xf, axis=-1, keepdims=True)
    var = jnp.mean(jnp.square(xf - mu), axis=-1, keepdims=True)
    y = (xf - mu) * lax.rsqrt(var + EPS) * g.astype(jnp.float32) + b.astype(jnp.float32)
    return y.astype(x.dtype)


def modulate(h, shift, scale):
    return h * (1 + scale[:, None, :]) + shift[:, None, :]


def sliding_window_attention(h, w_qkv, b_qkv, w_o, b_o, sinks):
    B, S, _ = h.shape
    qkv = h @ w_qkv + b_qkv
    q, k, v = jnp.split(qkv, [Q_DIM, Q_DIM + KV_DIM], axis=-1)
    q = q.reshape(B, S, N_KV_HEADS, GROUP, HEAD_DIM) * (HEAD_DIM ** -0.5)
    k = k.reshape(B, S, N_KV_HEADS, HEAD_DIM)
    v = v.reshape(B, S, N_KV_HEADS, HEAD_DIM)
    pad = ((0, 0), (WINDOW, 0), (0, 0), (0, 0))
    k_pad = jnp.pad(k, pad)
    v_pad = jnp.pad(v, pad)
    q_idx = jnp.arange(BLOCK)[:, None] + WINDOW
    k_idx = jnp.arange(SPAN)[None, :]
    dist = q_idx - k_idx
    band = (dist >= 0) & (dist < WINDOW)
    slopes = alibi_slopes(N_HEADS).reshape(N_KV_HEADS, GROUP)
    alibi = -slopes[:, :, None, None] * dist.astype(jnp.float32)[None, None]
    sink = sinks.astype(jnp.float32).reshape(N_KV_HEADS, GROUP)[None, :, :, None, None]

    def one_block(i):
        start = i * BLOCK
        qb = lax.dynamic_slice_in_dim(q, start, BLOCK, axis=1)
        kb = lax.dynamic_slice_in_dim(k_pad, start, SPAN, axis=1)
        vb = lax.dynamic_slice_in_dim(v_pad, start, SPAN, axis=1)
        s = jnp.einsum('bqkgd,bskd->bkgqs', qb, kb).astype(jnp.float32) + alibi
        mask = band & ((start - WINDOW + k_idx) >= 0)
        s = jnp.where(mask[None, None, None], s, -jnp.inf)
        m = jnp.maximum(jnp.max(s, axis=-1, keepdims=True), sink)
        p = jnp.exp(s - m)
        denom = jnp.sum(p, axis=-1, keepdims=True) + jnp.exp(sink - m)
        p = (p / denom).astype(vb.dtype)
        return jnp.einsum('bkgqs,bskd->bqkgd', p, vb)

    out = lax.map(one_block, jnp.arange(S // BLOCK))
    out = jnp.moveaxis(out, 0, 1).reshape(B, S, Q_DIM)
    return out @ w_o + b_o


def conformer_conv(h, w_pw1, b_pw1, w_dw, b_dw, ln_g, ln_b, w_pw2, b_pw2):
    u = h @ w_pw1 + b_pw1
    a, g = jnp.split(u, 2, axis=-1)
    u = a * jax.nn.sigmoid(g)
    u = jnp.pad(u, ((0, 0), (CONV_WIDTH - 1, 0), (0, 0)))
    u = lax.conv_general_dilated(u, w_dw[:, None, :].astype(u.dtype), window_strides=(1,), padding='VALID',
                                 dimension_numbers=('NWC', 'WIO', 'NWC'),
                                 feature_group_count=CONV_DIM) + b_dw
    u = jax.nn.silu(layernorm(u, ln_g, ln_b))
    return u @ w_pw2 + b_pw2


def squared_relu_mlp(h, w_up, w_down):
    return jnp.square(jax.nn.relu(h @ w_up)) @ w_down


def setup_inputs(seed: int = 0) -> dict:
    key = jax.random.key(seed)
    ks = jax.random.split(key, 32)
    D = D_MODEL
    nrm = lambda k, shape, s: jax.random.normal(k, shape, jnp.float32) * s
    return {
        "x": nrm(ks[0], (BATCH, SEQ, D), 1.0),
        "c": nrm(ks[1], (BATCH, D), 1.0),
        "w_mod": nrm(ks[2], (DEPTH, D, N_MOD * D), 0.5 * D ** -0.5),
        "b_mod": nrm(ks[3], (DEPTH, N_MOD * D), 0.01),
        "norm_mix": 1.0 + nrm(ks[4], (DEPTH, D), 0.02),
        "norm_mlp": 1.0 + nrm(ks[5], (DEPTH, D), 0.02),
        "w_qkv": nrm(ks[6], (N_ATTN_LAYERS, D, QKV_DIM), D ** -0.5),
        "b_qkv": nrm(ks[7], (N_ATTN_LAYERS, QKV_DIM), 0.01),
        "w_o": nrm(ks[8], (N_ATTN_LAYERS, Q_DIM, D), Q_DIM ** -0.5),
        "b_o": nrm(ks[9], (N_ATTN_LAYERS, D), 0.01),
        "sinks": nrm(ks[10], (N_ATTN_LAYERS, N_HEADS), 0.5),
        "w_pw1": nrm(ks[11], (N_CONV_LAYERS, D, 2 * CONV_DIM), D ** -0.5),
        "b_pw1": nrm(ks[12], (N_CONV_LAYERS, 2 * CONV_DIM), 0.01),
        "w_dw": nrm(ks[13], (N_CONV_LAYERS, CONV_WIDTH, CONV_DIM), CONV_WIDTH ** -0.5),
        "b_dw": nrm(ks[14], (N_CONV_LAYERS, CONV_DIM), 0.01),
        "conv_ln_g": 1.0 + nrm(ks[15], (N_CONV_LAYERS, CONV_DIM), 0.02),
        "conv_ln_b": nrm(ks[16], (N_CONV_LAYERS, CONV_DIM), 0.01),
        "w_pw2": nrm(ks[17], (N_CONV_LAYERS, CONV_DIM, D), CONV_DIM ** -0.5),
        "b_pw2": nrm(ks[18], (N_CONV_LAYERS, D), 0.01),
        "w_up": nrm(ks[19], (DEPTH, D, D_FF), D ** -0.5),
        "w_down": nrm(ks[20], (DEPTH, D_FF, D), D_FF ** -0.5),
        "final_norm": 1.0 + nrm(ks[21], (D,), 0.02),
    }


def reference(x, c, w_mod, b_mod, norm_mix, norm_mlp, w_qkv, b_qkv, w_o, b_o, sinks,
              w_pw1, b_pw1, w_dw, b_dw, conv_ln_g, conv_ln_b, w_pw2, b_pw2,
              w_up, w_down, final_norm):
    cs = jax.nn.silu(c)
    for i in range(DEPTH):
        mod = cs @ w_mod[i] + b_mod[i]
        sh1, sc1, g1, sh2, sc2, g2 = jnp.split(mod, N_MOD, axis=-1)
        h = modulate(rmsnorm(x, norm_mix[i]), sh1, sc1)
        j = i // N_MIXERS
        if i % N_MIXERS == 0:
            y = sliding_window_attention(h, w_qkv[j], b_qkv[j], w_o[j], b_o[j], sinks[j])
        else:
            y = conformer_conv(h, w_pw1[j], b_pw1[j], w_dw[j], b_dw[j], conv_ln_g[j], conv_ln_b[j],
                               w_pw2[j], b_pw2[j])
        x = x + g1[:, None, :] * y
        h = modulate(rmsnorm(x, norm_mlp[i]), sh2, sc2)
        x = x + g2[:, None, :] * squared_relu_mlp(h, w_up[i], w_down[i])
    return rmsnorm(x, final_norm)
```

```python
import contextlib
import numpy as np
import concourse.bass as bass
import concourse.mybir as mybir
from concourse.bass_utils import run_bass_kernel_spmd

F32 = mybir.dt.float32
BF16 = mybir.dt.bfloat16
AF = mybir.ActivationFunctionType
ALU = mybir.AluOpType

ENGS = ['pe', 'act', 'dve', 'pool', 'sp']
CENGS = ['pe', 'act', 'dve', 'pool']
SEM_W = 4000
DMA_RING = 8

D = 1024
NH = 16
HD = 64
NMOD = 6
DFF = 4096
CW = 31
EPS = 1e-6
T = 512
NT = 4
NSLOT = 3
SLAB = 8192
NEG = -1.0e6
OPROJ_INTERLEAVE = False


import os


class StopBuild(Exception):
    pass


def _stage(name):
    PHASE[0] = 'after_' + name
    if os.environ.get('KSTOP') == name:
        raise StopBuild(name)


PHASE = ['init']


class Sched:
    def __init__(self, nc):
        self.nc = nc
        self.ops = {e: [] for e in ENGS}
        self.tags = {}
        self.known = {e: {f: -1 for f in CENGS} for e in ENGS}
        self.known_dma = {e: set() for e in ENGS}
        self.ndma = {}

    def op(self, eng, fn, reads=(), writes=(), dma=False):
        idx = len(self.ops[eng])
        me = (eng, idx)
        raw, oth = set(), set()
        for t in reads:
            st = self.tags.get(t)
            if st is not None and st[0] is not None:
                raw.add(st[0])
        for t in writes:
            st = self.tags.get(t)
            if st is not None:
                if st[0] is not None:
                    oth.add(st[0])
                oth.update(st[1])
        deps = set(raw)
        for d in oth:
            if d[0] == eng and eng == 'pe' and not dma:
                continue
            deps.add(d)
        deps.discard(me)
        waits = []
        kn = self.known[eng]
        kd = self.known_dma[eng]
        for d in sorted(deps, key=lambda z: -z[1]):
            f, j = d
            o = self.ops[f][j]
            if o['dma']:
                if d in kd:
                    continue
                kd.add(d)
            else:
                if kn[f] >= j:
                    continue
                kn[f] = j
            waits.append(d)
            o['needs_inc'] = True
            for g, v in o['known'].items():
                if v > kn[g]:
                    kn[g] = v
        rec = dict(fn=fn, waits=waits, needs_inc=False, dma=dma, known=dict(kn), dma_k=None, phase=PHASE[0])
        if dma:
            rec['dma_k'] = self.ndma.get(eng, 0)
            self.ndma[eng] = rec['dma_k'] + 1
        self.ops[eng].append(rec)
        for t in reads:
            st = self.tags.setdefault(t, [None, []])
            st[1].append(me)
        for t in writes:
            self.tags[t] = [me, []]
        return me

    def emit(self):
        nc = self.nc
        with contextlib.ExitStack() as es:
            for e in CENGS:
                n_inc = sum(1 for o in self.ops[e] if o['needs_inc'] and not o['dma'])
                nsem = max(1, (n_inc + SEM_W - 1) // SEM_W)
                sems = [es.enter_context(nc.semaphore(f"s_{e}_{i}")) for i in range(nsem)]
                c = 0
                for o in self.ops[e]:
                    if o['needs_inc'] and not o['dma']:
                        o['sem'] = (sems[c // SEM_W], c % SEM_W + 1)
                        c += 1
            dma_by_k = {}
            for e in ENGS:
                dl = [o for o in self.ops[e] if o['dma']]
                if not dl:
                    continue
                dsems = [es.enter_context(nc.semaphore(f"s_dma_{e}_{i}")) for i in range(DMA_RING)]
                for o in dl:
                    k = o['dma_k']
                    o['sem'] = (dsems[k % DMA_RING], 16 * (k // DMA_RING + 1))
                    dma_by_k[(e, k)] = o
            block = es.enter_context(nc.Block())

            def run(eng_name):
                def body(eng):
                    for o in self.ops[eng_name]:
                        for (f, j) in o['waits']:
                            s, v = self.ops[f][j]['sem']
                            eng.wait_ge(s, v)
                        if o['dma']:
                            k = o['dma_k']
                            if k >= DMA_RING:
                                s, v = dma_by_k[(eng_name, k - DMA_RING)]['sem']
                                eng.wait_ge(s, v)
                            ins = o['fn'](eng)
                            ins.then_inc(o['sem'][0], 16)
                        else:
                            ins = o['fn'](eng)
                            if o['needs_inc']:
                                ins.then_inc(o['sem'][0], 1)
                return body

            block.tensor(run('pe'))
            block.scalar(run('act'))
            block.vector(run('dve'))
            block.gpsimd(run('pool'))
            block.sync(run('sp'))

    def stats(self):
        return {e: (len(self.ops[e]), sum(len(o['waits']) for o in self.ops[e])) for e in ENGS}


def col_layout(depth):
    lay = {}
    n = 0

    def add(name, w):
        nonlocal n
        lay[name] = n
        n += w

    for L in range(depth):
        add(('nmix', L), 8)
        add(('nmlp', L), 8)
        add(('bmod', L), 48)
        if L % 2 == 0:
            add(('bq', L), 8)
            add(('bk2', L), 2)
            add(('sink', L), 8)
        else:
            add(('b1a', L), 8)
            add(('b1g', L), 8)
            add(('wdw', L), CW * 8)
            add(('bdw', L), 8)
            add(('lng', L), 8)
            add(('lnb', L), 8)
    add(('fn',), 8)
    return lay, n


def colsT(v):
    v = np.asarray(v, dtype=np.float32)
    return np.ascontiguousarray(v.reshape(-1, 128).T)


def alibi_slopes():
    return [float(np.float32(2.0 ** (-8.0 * (h + 1) / NH))) for h in range(NH)]


def host_constants():
    s = np.arange(128)[:, None]
    t = np.arange(128)[None, :]
    prev = np.where(t < s, -(128 + t - s), NEG).astype(np.float32)
    cur = np.where(t >= s, -(t - s), NEG).astype(np.float32)
    dm = np.concatenate([prev, cur], axis=1).astype(np.float32)
    ident = np.eye(128, dtype=np.float32)
    return dm, ident


def prep_shared(inp, depth):
    lay, ncol = col_layout(depth)
    cols = np.zeros((128, ncol), np.float32)
    rows = np.zeros((128, 2, 1152), np.float32)
    for L in range(depth):
        j = L // 2
        cols[:, lay[('nmix', L)]:lay[('nmix', L)] + 8] = colsT(inp['norm_mix'][L])
        cols[:, lay[('nmlp', L)]:lay[('nmlp', L)] + 8] = colsT(inp['norm_mlp'][L])
        cols[:, lay[('bmod', L)]:lay[('bmod', L)] + 48] = colsT(inp['b_mod'][L])
        pr, ri = 32 * (L % 3), L // 3
        if L % 2 == 0:
            bqkv = np.asarray(inp['b_qkv'][j], np.float32)
            cols[:, lay[('bq', L)]:lay[('bq', L)] + 8] = colsT(bqkv[:1024])
            k0, k1 = bqkv[1024:1088], bqkv[1088:1152]
            cols[:, lay[('bk2', L)]:lay[('bk2', L)] + 2] = colsT(np.concatenate([k0, k0, k1, k1]))
            sk = np.asarray(inp['sinks'][j], np.float32)
            cols[:, lay[('sink', L)]:lay[('sink', L)] + 8] = colsT(np.repeat(sk, 64))
            rows[pr, ri, 0:128] = bqkv[1152:1280]
            rows[pr, ri, 128:1152] = np.asarray(inp['b_o'][j], np.float32)
        else:
            b1 = np.asarray(inp['b_pw1'][j], np.float32)
            cols[:, lay[('b1a', L)]:lay[('b1a', L)] + 8] = colsT(b1[:1024])
            cols[:, lay[('b1g', L)]:lay[('b1g', L)] + 8] = colsT(b1[1024:])
            wd = np.asarray(inp['w_dw'][j], np.float32)
            cols[:, lay[('wdw', L)]:lay[('wdw', L)] + CW * 8] = colsT(wd.reshape(-1))
            cols[:, lay[('bdw', L)]:lay[('bdw', L)] + 8] = colsT(inp['b_dw'][j])
            cols[:, lay[('lng', L)]:lay[('lng', L)] + 8] = colsT(inp['conv_ln_g'][j])
            cols[:, lay[('lnb', L)]:lay[('lnb', L)] + 8] = colsT(inp['conv_ln_b'][j])
            rows[pr, ri, 0:1024] = np.asarray(inp['b_pw2'][j], np.float32)
    cols[:, lay[('fn',)]:lay[('fn',)] + 8] = colsT(inp['final_norm'])
    return cols, rows


def build_program(NSEQ=4, SEQ=2048, DEPTH=4, NXB=1):
    NU = SEQ // T
    NA = (DEPTH + 1) // 2
    NCV = DEPTH // 2
    lay, NCOL = col_layout(DEPTH)
    slopes = alibi_slopes()
    nc = bass.Bass("TRN2", target_bir_lowering=False)

    def din(name, shape, dt=F32):
        return nc.dram_tensor(name, shape, dt, kind="ExternalInput").ap()

    x_d = din("x", [NSEQ, SEQ, D])
    cT_d = din("cT", [128, 8 * NSEQ])
    cols_d = din("cols", [128, NCOL])
    rows_d = din("rows", [128, 2 * 1152])
    dm_d = din("dm", [128, 256])
    ident_d = din("ident", [128, 128])
    w_mod_d = din("w_mod", [DEPTH, D, NMOD * D])
    w_qkv_d = din("w_qkv", [NA, D, 1280])
    w_o_d = din("w_o", [NA, D, D])
    w_pw1_d = din("w_pw1", [max(NCV, 1), D, 2 * D])
    w_pw2_d = din("w_pw2", [max(NCV, 1), D, D])
    w_up_d = din("w_up", [DEPTH, D, DFF])
    w_down_d = din("w_down", [DEPTH, DFF, D])
    out_d = nc.dram_tensor("out", [NSEQ, SEQ, D], F32, kind="ExternalOutput").ap()
    def layer_slabs(L):
        if L % 2 == 0:
            names = ['Q', 'KV', 'O']
        else:
            names = ['P10', 'DG0', 'DG1', 'P11', 'DG2', 'DG3', 'P2']
        return names + [f'U{s_}' for s_ in range(4)] + [f'D{s_}' for s_ in range(4)]

    SID = {}
    for L_ in range(DEPTH):
        for nm_ in layer_slabs(L_):
            SID[(L_, nm_)] = len(SID)
    for L_ in range(DEPTH):
        for s_ in range(6):
            SID[(L_, f'M{s_}')] = len(SID)
    wsc = nc.dram_tensor("wsc", [len(SID), 128, SLAB], BF16, kind="Internal").ap()
    gsc = nc.dram_tensor("gsc", [DEPTH * NSEQ * 2, 128, D], F32, kind="Internal").ap()

    S = Sched(nc)
    with contextlib.ExitStack() as es:
        def sb(name, shape, dt):
            return es.enter_context(nc.sbuf_tensor(name, shape, dt))

        xs = sb("xs", [128, NXB * NT, D], F32)
        hT = sb("hT", [128, 8, T], BF16)
        aT = sb("aT", [128, 32, T], BF16)
        aTf = aT[:].rearrange("p a t -> p (a t)").bitcast(F32)
        ntm = sb("ntm", [128, NT, D], F32)
        ring = sb("ring", [128, NSLOT, SLAB], BF16)
        Gb = sb("Gb", [128, 2, 2, D], F32)
        FN = sb("FN", [128, D], F32)
        qT = sb("qT", [128, 8, T], BF16)
        kT = [sb(f"kT{a}", [128, 2, T + 128], BF16) for a in range(NA)]
        Vt = [sb(f"V{a}", [128, NT + 1, 128], BF16) for a in range(NA)]
        DM = sb("DM", [128, 256], F32)
        TF = sb("TF", [128, 6, T], F32)
        TB = sb("TB", [128, 4, T], BF16)
        ub8 = sb("ub8", [128, 8, T + 32], BF16)
        ucar = [sb(f"ucar{a}", [128, 8, CW - 1], BF16) for a in range(max(NCV, 1))]
        identb = sb("identb", [128, 128], BF16)
        colsb = sb("colsb", [128, NCOL], F32)
        rowsb = sb("rowsb", [128, 2, 1152], BF16)
        identf = sb("identf", [128, 128], F32)
        onesf = sb("onesf", [128, 128], F32)
        onesb = sb("onesb", [128, 128], BF16)
        rep4 = TF[:, 4:6, :]
        cTf = sb("cTf", [128, 8 * NSEQ], F32)
        csT = sb("csT", [128, 8, NSEQ], BF16)
        modT = sb("modT", [128, DEPTH * 6, 8, NSEQ], F32)
        A1 = sb("A1", [128, DEPTH, 8, NSEQ], F32)
        A2 = sb("A2", [128, DEPTH, 8, NSEQ], F32)
        bq8 = sb("bq8", [128, NA, 8], F32)
        esink = sb("esink", [128, NA, 8], F32)
        ss = sb("ss", [128, 8], F32)
        sd = sb("sd", [128, 8], F32)
        rs = sb("rs", [128, 8], F32)
        ps = es.enter_context(nc.psum_tensor("ps", [128, 8, 512], F32))

        rot = {'A': [0, 1, 2, 3], 'B': [4, 5], 'C': [6, 7]}
        rpos = {'A': 0, 'B': 0, 'C': 0}

        def bank(cls):
            b = rot[cls][rpos[cls] % len(rot[cls])]
            rpos[cls] += 1
            return b

        tfpos = {'lo': 0, 'hi': 0}

        def tf_lo():
            i = tfpos['lo'] % 2
            tfpos['lo'] += 1
            return i

        def tf_hi():
            i = 2 + tfpos['hi'] % 2
            tfpos['hi'] += 1
            return i

        tbpos = [0]

        def tb_next(n=1):
            i = tbpos[0] % (4 // n)
            tbpos[0] += 1
            return i * n

        def col(name, j=0):
            c0 = lay[name] + j
            return colsb[:, c0:c0 + 1]

        def slab_id(L, k):
            return SID[(L, k)]

        seq = []
        for s in range(6):
            seq.append(('wmod', 0, s))
        for un in range(NSEQ * NU):
            for L in range(DEPTH):
                for k in layer_slabs(L):
                    if un == 0 and k == 'U0' and L + 1 < DEPTH:
                        for s in range(6):
                            seq.append(('wmod', L + 1, s))
                    seq.append(('w', L, k))
        wstate = {'loaded': 0, 'cur': 0}

        def slab_len(L, k):
            if k == 'KV':
                return 8 * 384
            if k.startswith('DG'):
                return 62 * 128
            return SLAB

        def slab_tags(L, k):
            sid = slab_id(L, k)
            if k == 'KV':
                return [f"wsc{sid}_{d}" for d in (0, 64, 128, 192, 256)]
            if k in ('P10', 'P11'):
                return [f"wsc{sid}_a", f"wsc{sid}_g"]
            return [f"wsc{sid}"]

        def record_load(i):
            ent = seq[i]
            slot = i % NSLOT
            if ent[0] == 'wmod':
                _, L, s = ent
                sid = slab_id(L, f'M{s}')
                S.op('sp', lambda e: e.dma_start(out=ring[:, slot, :], in_=wsc[sid]),
                     reads=[f"wsc{sid}"], writes=[f"ring{slot}"], dma=True)
            else:
                _, L, k = ent
                n = slab_len(L, k)
                sid = slab_id(L, k)
                S.op('sp', lambda e: e.dma_start(out=ring[:, slot, 0:n], in_=wsc[sid][:, 0:n]),
                     reads=slab_tags(L, k), writes=[f"ring{slot}"], dma=True)

        def prefetch():
            while wstate['loaded'] < len(seq) and wstate['loaded'] < wstate['cur'] + NSLOT:
                record_load(wstate['loaded'])
                wstate['loaded'] += 1

        def acquire(expect):
            i = wstate['cur']
            assert seq[i] == expect, (seq[i], expect)
            prefetch()
            return i % NSLOT

        def release():
            wstate['cur'] += 1
            prefetch()

        def load_x(b, u, xb):
            for tt in range(NT):
                S.op('act', lambda e, tt=tt: e.dma_start(
                    out=xs[:, xb * NT + tt, :], in_=x_d[b, u * T + tt * 128:u * T + (tt + 1) * 128, :]),
                    writes=[f"x{xb}_{tt}_{q}" for q in range(4)], dma=True)

        out_tags = []

        def _body():
            load_x(0, 0, 0)
            S.op('sp', lambda e: e.dma_start(out=colsb[:], in_=cols_d), writes=['colsb'], dma=True)
            S.op('sp', lambda e: e.dma_start(out=identf[:], in_=ident_d), writes=['identf'], dma=True)
            S.op('sp', lambda e: e.dma_start(out=DM[:], in_=dm_d), writes=['DM'], dma=True)
            S.op('sp', lambda e: e.dma_start(out=cTf[:], in_=cT_d), writes=['cTf'], dma=True)
            S.op('pool', lambda e: e.dma_start(out=rowsb[:].rearrange("p a n -> p (a n)"), in_=rows_d),
                 writes=['rowsb'], dma=True)
            _stage('dma0')

            def conv_slab(sid, src3, n):
                dst = wsc[sid][:, 0:8 * n].rearrange("p (kc n) -> p kc n", kc=8)
                S.op('pool', lambda e: e.dma_start(out=dst, in_=src3), writes=[f"wsc{sid}"], dma=True)

            def r3(ap2):
                return ap2.rearrange("(kc p) n -> p kc n", p=128)

            def convert_layer(L):
                j = L // 2
                if L % 2 == 0:
                    conv_slab(slab_id(L, 'Q'), r3(w_qkv_d[j][:, 0:1024]), 1024)
                    sid = slab_id(L, 'KV')
                    dst3 = wsc[sid][:, 0:8 * 384].rearrange("p (kc n) -> p kc n", kc=8)
                    pieces = [(0, 1024), (64, 1024), (128, 1088), (192, 1088)]
                    tags = [f"wsc{sid}"]
                    for (d0, s0) in pieces:
                        S.op('pool', lambda e, d0=d0, s0=s0: e.dma_start(
                            out=dst3[:, :, d0:d0 + 64], in_=r3(w_qkv_d[j][:, s0:s0 + 64])),
                            writes=[f"wsc{sid}_{d0}"], dma=True)
                    S.op('pool', lambda e: e.dma_start(out=dst3[:, :, 256:384], in_=r3(w_qkv_d[j][:, 1152:1280])),
                         writes=[f"wsc{sid}_256"], dma=True)
                    conv_slab(slab_id(L, 'O'), r3(w_o_d[j]), 1024)
                else:
                    for i in range(2):
                        sid = slab_id(L, f'P1{i}')
                        dst3 = wsc[sid].rearrange("p (kc n) -> p kc n", kc=8)
                        S.op('pool', lambda e, i=i, dst3=dst3: e.dma_start(
                            out=dst3[:, :, 0:512], in_=r3(w_pw1_d[j][:, i * 512:(i + 1) * 512])),
                            writes=[f"wsc{sid}_a"], dma=True)
                        S.op('pool', lambda e, i=i, dst3=dst3: e.dma_start(
                            out=dst3[:, :, 512:1024], in_=r3(w_pw1_d[j][:, 1024 + i * 512:1024 + (i + 1) * 512])),
                            writes=[f"wsc{sid}_g"], dma=True)
                    conv_slab(slab_id(L, 'P2'), r3(w_pw2_d[j]), 1024)
                for s in range(4):
                    conv_slab(slab_id(L, f'U{s}'), r3(w_up_d[L][:, s * 1024:(s + 1) * 1024]), 1024)
                for s in range(4):
                    sid = slab_id(L, f'D{s}')
                    dst = wsc[sid].rearrange("p (kc n) -> p kc n", kc=32)
                    src = w_down_d[L][:, s * 256:(s + 1) * 256].rearrange("(kc p) n -> p kc n", p=128)
                    S.op('pool', lambda e, dst=dst, src=src: e.dma_start(out=dst, in_=src),
                         writes=[f"wsc{sid}"], dma=True)

            def convert_wmod(L):
                for s_ in range(6):
                    conv_slab(slab_id(L, f'M{s_}'), r3(w_mod_d[L][:, s_ * 1024:(s_ + 1) * 1024]), 1024)

            convert_wmod(0)
            S.op('act', lambda e: e.copy(out=identb[:], in_=identf[:]), reads=['identf'], writes=['identb'])
            aTflat = aT[:].rearrange("p a t -> p (a t)")

            def build_diag(L):
                cw0 = lay[('wdw', L)]
                for k in range(4):
                    hb = k % 2
                    stg = aTflat[:, hb * 8192:(hb + 1) * 8192]
                    stags = [f"aT{hb * 16 + q}" for q in range(16)]
                    for ci in range(2):
                        c = 2 * k + ci
                        wcols = colsb[:, cw0 + c:cw0 + c + 8 * (CW - 1) + 1:8]
                        S.op('dve', lambda e, ci=ci, stg=stg, wcols=wcols: e.tensor_tensor(
                            out=stg[:, ci * CW * 128:(ci + 1) * CW * 128].rearrange("p (j n) -> p j n", j=CW),
                            in0=identb[:].unsqueeze(1).to_broadcast([128, CW, 128]),
                            in1=wcols.unsqueeze(2).to_broadcast([128, CW, 128]), op=ALU.mult),
                            reads=['identb', 'colsb'], writes=stags)
                    sid = slab_id(L, f'DG{k}')
                    S.op('pool', lambda e, sid=sid, stg=stg: e.dma_start(out=wsc[sid][:, 0:62 * 128], in_=stg[:, 0:62 * 128]),
                         reads=stags, writes=[f"wsc{sid}"], dma=True)

            for L_ in range(1, DEPTH, 2):
                build_diag(L_)

            convert_layer(0)
            for L_ in range(1, DEPTH):
                convert_wmod(L_)
                convert_layer(L_)
            _stage('conv0')

            S.op('dve', lambda e: e.memset(onesf[:], 1.0), writes=['onesf'])
            S.op('dve', lambda e: e.memset(onesb[:], 1.0), writes=['onesb'])
            S.op('act', lambda e: e.activation(out=csT[:].rearrange("p k b -> p (k b)"), in_=cTf[:], func=AF.Silu),
                 reads=['cTf'], writes=['csT'])
            for a in range(NA):
                L = 2 * a
                c0 = lay[('bq', L)]
                S.op('act', lambda e, a=a, c0=c0: e.mul(out=bq8[:, a, :], in_=colsb[:, c0:c0 + 8], mul=0.125),
                     reads=['colsb'], writes=['bq8'])
                c1 = lay[('sink', L)]
                S.op('act', lambda e, a=a, c1=c1: e.activation(out=esink[:, a, :], in_=colsb[:, c1:c1 + 8], func=AF.Exp),
                     reads=['colsb'], writes=['esink'])
            _stage('small')

            def bcast_tile(dst, dst_tag, colfn4, col_tags):
                for hf in range(2):
                    bk = bank('B')
                    ri = hf
                    S.op('dve', lambda e, hf=hf, ri=ri: e.tensor_tensor(
                        out=rep4[:, ri, :].rearrange("p (j n) -> p j n", j=4),
                        in0=identf[:].unsqueeze(1).to_broadcast([128, 4, 128]),
                        in1=colfn4(hf).unsqueeze(2).to_broadcast([128, 4, 128]), op=ALU.mult),
                        reads=['identf'] + col_tags, writes=[f"TF{4 + ri}"])
                    S.op('pe', lambda e, ri=ri, bk=bk: e.matmul(
                        ps[:, bk, :], lhsT=onesf[:], rhs=rep4[:, ri, :], start=True, stop=True),
                        reads=[f"TF{4 + ri}", 'onesf'], writes=[f"ps{bk}"])
                    S.op('act', lambda e, hf=hf, bk=bk: e.copy(out=dst[:, hf * 512:(hf + 1) * 512], in_=ps[:, bk, :]),
                         reads=[f"ps{bk}"], writes=[f"{dst_tag}{hf}"])

            gcount = [0]

            def mod_layer(L, spar):
                gstage = [(Gb[:, spar, 0, :], f'G{spar}_0_'), (Gb[:, spar, 1, :], f'G{spar}_1_')]
                for s in range(6):
                    slot = acquire(('wmod', L, s))
                    w3 = ring[:, slot, :].rearrange("p (kc n) -> p kc n", kc=8)
                    mi = (L * 6 + s) % 2
                    mrow = TF[0:NSEQ, 2 * mi:2 * mi + 2, :].rearrange("p a t -> p (a t)")
                    mtags = [f"TF{2 * mi}", f"TF{2 * mi + 1}"]
                    for ch in range(2):
                        bka = bank('A')
                        for kc in range(8):
                            S.op('pe', lambda e, ch=ch, kc=kc, w3=w3, bka=bka: e.matmul(
                                ps[0:NSEQ, bka, :], lhsT=csT[:, kc, :], rhs=w3[:, kc, ch * 512:(ch + 1) * 512],
                                start=(kc == 0), stop=(kc == 7)),
                                reads=[f"ring{slot}", 'csT'], writes=[f"ps{bka}"])
                        S.op('act', lambda e, ch=ch, bka=bka, mrow=mrow: e.copy(
                            out=mrow[:, ch * 512:(ch + 1) * 512], in_=ps[0:NSEQ, bka, :]),
                            reads=[f"ps{bka}"], writes=[mtags[ch]])
                    bk = bank('B')
                    for j in range(8):
                        S.op('pe', lambda e, j=j, bk=bk, mrow=mrow: e.transpose(
                            ps[:, bk, j * NSEQ:(j + 1) * NSEQ], mrow[:, j * 128:(j + 1) * 128], identf[0:NSEQ, 0:NSEQ]),
                            reads=mtags + ['identf'], writes=[f"ps{bk}"])
                    c0 = lay[('bmod', L)] + s * 8
                    S.op('dve', lambda e, L=L, s=s, bk=bk, c0=c0: e.tensor_tensor(
                        out=modT[:, L * 6 + s, :, :],
                        in0=ps[:, bk, 0:8 * NSEQ].rearrange("p (j b) -> p j b", j=8),
                        in1=colsb[:, c0:c0 + 8].unsqueeze(2).to_broadcast([128, 8, NSEQ]), op=ALU.add),
                        reads=[f"ps{bk}", 'colsb'], writes=[f"modT{L}_{s}"])
                    release()
                    if s % 2 == 1 and (L * 3 + s // 2 + 1) < DEPTH:
                        pass
                cm = lay[('nmix', L)]
                S.op('dve', lambda e, L=L, cm=cm: e.scalar_tensor_tensor(
                    out=A1[:, L, :, :], in0=modT[:, L * 6 + 1, :, :], scalar=1.0,
                    in1=colsb[:, cm:cm + 8].unsqueeze(2).to_broadcast([128, 8, NSEQ]), op0=ALU.add, op1=ALU.mult),
                    reads=[f"modT{L}_1", 'colsb'], writes=[f"A1_{L}"])
                cm2 = lay[('nmlp', L)]
                S.op('dve', lambda e, L=L, cm2=cm2: e.scalar_tensor_tensor(
                    out=A2[:, L, :, :], in0=modT[:, L * 6 + 4, :, :], scalar=1.0,
                    in1=colsb[:, cm2:cm2 + 8].unsqueeze(2).to_broadcast([128, 8, NSEQ]), op0=ALU.add, op1=ALU.mult),
                    reads=[f"modT{L}_4", 'colsb'], writes=[f"A2_{L}"])
                for b_ in range(NSEQ):
                    for wh, sec in ((0, 2), (1, 5)):
                        stg, stag = gstage[gcount[0] % len(gstage)]
                        gcount[0] += 1
                        bcast_tile(stg, stag, lambda hf, L=L, b_=b_, sec=sec: modT[:, L * 6 + sec, hf * 4:(hf + 1) * 4, b_],
                                   [f"modT{L}_{sec}"])
                        gi = (L * NSEQ + b_) * 2 + wh
                        S.op('act', lambda e, gi=gi, stg=stg: e.dma_start(out=gsc[gi], in_=stg),
                             reads=[f"{stag}0", f"{stag}1"], writes=[f"gsc{gi}"], dma=True)
            mod_layer(0, 1)
            _stage('mod')
            cfn = lay[('fn',)]
            bcast_tile(FN, 'FN', lambda hf: colsb[:, cfn + hf * 4:cfn + (hf + 1) * 4], ['colsb'])

            def load_gates(L, b, par):
                for wh in range(2):
                    gi = (L * NSEQ + b) * 2 + wh
                    S.op('act', lambda e, gi=gi, wh=wh: e.dma_start(out=Gb[:, par, wh, :], in_=gsc[gi]),
                         reads=[f"gsc{gi}"], writes=[f"G{par}_{wh}_0", f"G{par}_{wh}_1"], dma=True)
            _stage('fn')

            def xtags(xb, tt, qs=range(4)):
                return [f"x{xb}_{tt}_{q}" for q in qs]

            def norm_to_hT(xb, Acol, Scol, col_tags):
                for tt in range(NT):
                    xrow = xs[:, xb * NT + tt, :]
                    S.op('act', lambda e, tt=tt, xrow=xrow: e.activation(
                        out=ntm[:, tt, :], in_=xrow, func=AF.Square, accum_out=ss[:, tt:tt + 1]),
                        reads=xtags(xb, tt), writes=[f"ntm{tt}", f"ss{tt}"])
                    S.op('act', lambda e, tt=tt: e.activation(
                        out=sd[:, tt:tt + 1], in_=ss[:, tt:tt + 1], func=AF.Sqrt, scale=1.0 / D, bias=EPS),
                        reads=[f"ss{tt}"], writes=[f"sd{tt}"])
                    S.op('dve', lambda e, tt=tt: e.reciprocal(out=rs[:, tt:tt + 1], in_=sd[:, tt:tt + 1]),
                         reads=[f"sd{tt}"], writes=[f"rs{tt}"])
                    S.op('dve', lambda e, tt=tt, xrow=xrow: e.tensor_scalar(
                        out=ntm[:, tt, :], in0=xrow, scalar1=rs[:, tt:tt + 1], scalar2=None, op0=ALU.mult),
                        reads=xtags(xb, tt) + [f"rs{tt}"], writes=[f"ntm{tt}"])
                for kc in range(8):
                    bk = bank('B')
                    for tt in range(NT):
                        S.op('pe', lambda e, kc=kc, tt=tt, bk=bk: e.transpose(
                            ps[:, bk, tt * 128:(tt + 1) * 128], ntm[:, tt, kc * 128:(kc + 1) * 128], identf[:]),
                            reads=[f"ntm{tt}", 'identf'], writes=[f"ps{bk}"])
                    if kc % 2 == 0:
                        S.op('dve', lambda e, kc=kc, bk=bk: e.tensor_scalar(
                            out=hT[:, kc, :], in0=ps[:, bk, :], scalar1=Acol(kc), scalar2=Scol(kc),
                            op0=ALU.mult, op1=ALU.add),
                            reads=[f"ps{bk}"] + col_tags, writes=[f"hT{kc}"])
                    else:
                        S.op('act', lambda e, kc=kc, bk=bk: e.activation(
                            out=hT[:, kc, :], in_=ps[:, bk, :], func=AF.Identity, scale=Acol(kc), bias=Scol(kc)),
                            reads=[f"ps{bk}"] + col_tags, writes=[f"hT{kc}"])

            def proj_residual(xb, slot, src_tags_fn, lhs_fn, nk, par, bias_row, pr, tts=range(NT)):
                w3 = ring[:, slot, :].rearrange("p (kc n) -> p kc n", kc=nk)
                for tt in tts:
                    for hf in range(2):
                        bk = bank('A')
                        for kc in range(nk):
                            S.op('pe', lambda e, kc=kc, tt=tt, hf=hf, bk=bk: e.matmul(
                                ps[:, bk, :], lhsT=lhs_fn(kc, tt), rhs=w3[:, kc, hf * 512:(hf + 1) * 512],
                                start=(kc == 0), stop=False),
                                reads=[f"ring{slot}"] + src_tags_fn(kc), writes=[f"ps{bk}"])
                        S.op('pe', lambda e, hf=hf, bk=bk: e.matmul(
                            ps[:, bk, :], lhsT=onesb[pr:pr + 1, :], rhs=bias_row(hf), start=False, stop=True),
                            reads=['onesb', 'rowsb'], writes=[f"ps{bk}"])
                        ti = tf_hi()
                        S.op('dve', lambda e, hf=hf, bk=bk, ti=ti: e.tensor_tensor(
                            out=TF[:, ti, :], in0=ps[:, bk, :], in1=Gb[:, par, 0, hf * 512:(hf + 1) * 512], op=ALU.mult),
                            reads=[f"ps{bk}", f"G{par}_0_{hf}"], writes=[f"TF{ti}"])
                        xv = xs[:, xb * NT + tt, hf * 512:(hf + 1) * 512]
                        S.op('pool', lambda e, xv=xv, ti=ti: e.tensor_tensor(out=xv, in0=xv, in1=TF[:, ti, :], op=ALU.add),
                             reads=[f"TF{ti}"] + xtags(xb, tt, (2 * hf, 2 * hf + 1)),
                             writes=xtags(xb, tt, (2 * hf, 2 * hf + 1)))

            def attention(L, b, u, xb, par):
                a = L // 2
                pr, ri = 32 * (L % 3), L // 3
                kTa, Va = kT[a], Vt[a]
                if u > 0:
                    S.op('act', lambda e: e.copy(out=kTa[:, :, 0:128], in_=kTa[:, :, T:T + 128]),
                         reads=[f"kT{a}_3"], writes=[f"kT{a}_c"])
                    S.op('act', lambda e: e.copy(out=Va[:, 0, :], in_=Va[:, NT, :]),
                         reads=[f"V{a}_3"], writes=[f"V{a}_c"])
                slot = acquire(('w', L, 'Q'))
                w3 = ring[:, slot, :].rearrange("p (kc n) -> p kc n", kc=8)
                for c in range(8):
                    bk = bank('A')
                    for kc in range(8):
                        S.op('pe', lambda e, c=c, kc=kc, bk=bk, w3=w3: e.matmul(
                            ps[:, bk, :], lhsT=w3[:, kc, c * 128:(c + 1) * 128], rhs=hT[:, kc, :],
                            start=(kc == 0), stop=(kc == 7)),
                            reads=[f"ring{slot}", f"hT{kc}"], writes=[f"ps{bk}"])
                    S.op('act', lambda e, c=c, bk=bk: e.activation(
                        out=qT[:, c, :], in_=ps[:, bk, :], func=AF.Identity, scale=0.125, bias=bq8[:, a, c:c + 1]),
                        reads=[f"ps{bk}", 'bq8'], writes=[f"qT{c}"])
                release()
                _stage('attQ')
                slot = acquire(('w', L, 'KV'))
                w3 = ring[:, slot, 0:8 * 384].rearrange("p (kc n) -> p kc n", kc=8)
                for c2 in range(2):
                    bk = bank('A')
                    for kc in range(8):
                        S.op('pe', lambda e, c2=c2, kc=kc, bk=bk, w3=w3: e.matmul(
                            ps[:, bk, :], lhsT=w3[:, kc, c2 * 128:(c2 + 1) * 128], rhs=hT[:, kc, :],
                            start=(kc == 0), stop=(kc == 7)),
                            reads=[f"ring{slot}", f"hT{kc}"], writes=[f"ps{bk}"])
                    S.op('act', lambda e, c2=c2, bk=bk: e.activation(
                        out=kTa[:, c2, 128:128 + T], in_=ps[:, bk, :], func=AF.Identity,
                        bias=col(('bk2', L), c2)),
                        reads=[f"ps{bk}", 'colsb'], writes=[f"kT{a}_{i}" for i in range(NT)])
                for tt in range(NT):
                    bk = bank('A')
                    for kc in range(8):
                        S.op('pe', lambda e, tt=tt, kc=kc, bk=bk, w3=w3: e.matmul(
                            ps[:, bk, 0:128], lhsT=hT[:, kc, tt * 128:(tt + 1) * 128], rhs=w3[:, kc, 256:384],
                            start=(kc == 0), stop=False),
                            reads=[f"ring{slot}", f"hT{kc}"], writes=[f"ps{bk}"])
                    S.op('pe', lambda e, bk=bk: e.matmul(
                        ps[:, bk, 0:128], lhsT=onesb[pr:pr + 1, :], rhs=rowsb[pr:pr + 1, ri, 0:128],
                        start=False, stop=True),
                        reads=['onesb', 'rowsb'], writes=[f"ps{bk}"])
                    S.op('act', lambda e, tt=tt, bk=bk: e.copy(out=Va[:, 1 + tt, :], in_=ps[:, bk, 0:128]),
                         reads=[f"ps{bk}"], writes=[f"V{a}_{tt}"])
                release()
                _stage('attKV')
                def ktag(slot_i):
                    return f"kT{a}_c" if slot_i == 0 else f"kT{a}_{slot_i - 1}"

                def vtag(slot_i):
                    return f"V{a}_c" if slot_i == 0 else f"V{a}_{slot_i - 1}"

                pairs = [(i, hf, p2) for i in range(NT) for hf in range(2) for p2 in range(2)]
                pstate = {}
                hstate = {}

                def stage_a(pidx):
                    i, hf, p2 = pairs[pidx]
                    has_prev = not (u == 0 and i == 0)
                    kbs = [0, 1] if has_prev else [1]
                    c0 = hf * 4 + p2 * 2
                    kvh = c0 // 4
                    sbanks = (bank('A'), bank('A'))
                    for cq in range(2):
                        c_ = c0 + cq
                        for hh in range(2):
                            S3 = ps[:, sbanks[hh], :].rearrange("p (q k t) -> p q k t", q=2, k=2)
                            for kb in kbs:
                                S.op('pe', lambda e, hh=hh, kb=kb, c_=c_, cq=cq, S3=S3: e.matmul(
                                    S3[:, cq, kb, :],
                                    lhsT=kTa[hh * 64:(hh + 1) * 64, kvh, (i + kb) * 128:(i + kb + 1) * 128],
                                    rhs=qT[hh * 64:(hh + 1) * 64, c_, i * 128:(i + 1) * 128],
                                    start=True, stop=True),
                                    reads=[ktag(i + kb), f"qT{c_}"], writes=[f"ps{sbanks[hh]}"])
                    info = []
                    for cq in range(2):
                        c = c0 + cq
                        sviews = [ps[:, sbanks[hh], :].rearrange("p (q k t) -> p q k t", q=2, k=2)[:, cq, :, :]
                                  for hh in range(2)]
                        ti = tf_lo()
                        T4 = TF[:, ti, :].rearrange("p (h k t) -> p h k t", h=2, k=2)
                        for hh in range(2):
                            h = 2 * c + hh
                            if has_prev:
                                S.op('dve', lambda e, hh=hh, h=h, sv=sviews[hh], T4=T4: e.scalar_tensor_tensor(
                                    out=T4[:, hh, :, :], in0=DM[:].rearrange("p (k t) -> p k t", k=2),
                                    scalar=slopes[h], in1=sv, op0=ALU.mult, op1=ALU.add),
                                    reads=['DM', f"ps{sbanks[hh]}"], writes=[f"TF{ti}"])
                            else:
                                S.op('dve', lambda e, hh=hh, h=h, sv=sviews[hh], T4=T4: e.scalar_tensor_tensor(
                                    out=T4[:, hh, 1, :], in0=DM[:, 128:256],
                                    scalar=slopes[h], in1=sv[:, 1, :], op0=ALU.mult, op1=ALU.add),
                                    reads=['DM', f"ps{sbanks[hh]}"], writes=[f"TF{ti}"])
                        pi = tb_next(1)
                        P4 = TB[:, pi, :].rearrange("p (h k t) -> p h k t", h=2, k=2)
                        if has_prev:
                            S.op('act', lambda e, ti=ti, pi=pi: e.activation(out=TB[:, pi, :], in_=TF[:, ti, :], func=AF.Exp),
                                 reads=[f"TF{ti}"], writes=[f"TB{pi}"])
                        else:
                            S.op('act', lambda e, T4=T4, P4=P4: e.activation(out=P4[:, :, 1, :], in_=T4[:, :, 1, :], func=AF.Exp),
                                 reads=[f"TF{ti}"], writes=[f"TB{pi}"])
                        info.append((c, pi, P4))
                    pstate[pidx] = (kbs, kvh, info)

                def stage_b(pidx):
                    i, hf, p2 = pairs[pidx]
                    kbs, kvh, info = pstate.pop(pidx)
                    if p2 == 0:
                        hstate[(i, hf)] = (bank('C'), bank('B'))
                    nb, db = hstate[(i, hf)]
                    for (c, pi, P4) in info:
                        cc = c % 4
                        for hh in range(2):
                            for n_i, kb in enumerate(kbs):
                                st, sp_ = (n_i == 0), (n_i == len(kbs) - 1)
                                S.op('pe', lambda e, hh=hh, kb=kb, cc=cc, P4=P4, st=st, sp_=sp_: e.matmul(
                                    ps[hh * 64:(hh + 1) * 64, nb, cc * 128:(cc + 1) * 128],
                                    lhsT=Va[:, i + kb, kvh * 64:(kvh + 1) * 64], rhs=P4[:, hh, kb, :],
                                    start=st, stop=sp_),
                                    reads=[vtag(i + kb), f"TB{pi}"], writes=[f"ps{nb}"])
                                S.op('pe', lambda e, hh=hh, kb=kb, cc=cc, P4=P4, st=st, sp_=sp_: e.matmul(
                                    ps[hh * 64:(hh + 1) * 64, db, cc * 128:(cc + 1) * 128],
                                    lhsT=onesb[:, 0:64], rhs=P4[:, hh, kb, :], start=st, stop=sp_),
                                    reads=['onesb', f"TB{pi}"], writes=[f"ps{db}"])
                    if p2 == 1:
                        ri2 = tf_hi()
                        R3 = TF[:, ri2, :].rearrange("p (c t) -> p c t", c=4)
                        S.op('dve', lambda e, R3=R3: e.tensor_tensor(
                            out=R3, in0=ps[:, db, :].rearrange("p (c t) -> p c t", c=4),
                            in1=esink[:, a, hf * 4:(hf + 1) * 4].unsqueeze(2).to_broadcast([128, 4, 128]), op=ALU.add),
                            reads=[f"ps{db}", 'esink'], writes=[f"TF{ri2}"])
                        S.op('act', lambda e: e.activation(out=TF[:, ri2, :], in_=TF[:, ri2, :], func=AF.Ln),
                             reads=[f"TF{ri2}"], writes=[f"TF{ri2}"])
                        S.op('act', lambda e: e.activation(out=TF[:, ri2, :], in_=TF[:, ri2, :], func=AF.Exp, scale=-1.0),
                             reads=[f"TF{ri2}"], writes=[f"TF{ri2}"])
                        S.op('dve', lambda e, R3=R3: e.tensor_tensor(
                            out=hT[:, hf * 4:(hf + 1) * 4, i * 128:(i + 1) * 128],
                            in0=ps[:, nb, :].rearrange("p (c t) -> p c t", c=4), in1=R3, op=ALU.mult),
                            reads=[f"ps{nb}", f"TF{ri2}"],
                            writes=([f"hT{c_}" for c_ in range(hf * 4, hf * 4 + 4)] if i == 0 else []) +
                                   [f"oT{c_}_{i}" for c_ in range(hf * 4, hf * 4 + 4)])

                oslot = acquire(('w', L, 'O'))

                def oproj(tt_):
                    proj_residual(xb, oslot, lambda kc: [f"hT{kc}", f"oT{kc}_{tt_}"],
                                  lambda kc, tt: hT[:, kc, tt * 128:(tt + 1) * 128], 8, par,
                                  lambda hf: rowsb[pr:pr + 1, ri, 128 + hf * 512:128 + (hf + 1) * 512], pr,
                                  tts=[tt_])

                stage_a(0)
                for pidx in range(len(pairs)):
                    if pidx + 1 < len(pairs):
                        stage_a(pidx + 1)
                    stage_b(pidx)
                    if OPROJ_INTERLEAVE and pidx % 4 == 1 and pidx // 4 >= 1:
                        oproj(pidx // 4 - 1)
                if OPROJ_INTERLEAVE:
                    oproj(NT - 1)
                else:
                    for tt_ in range(NT):
                        oproj(tt_)
                release()
                _stage('attN')

            def conformer(L, b, u, xb, par):
                cv = L // 2
                pr, ri = 32 * (L % 3), L // 3
                uc = ucar[cv]
                sbk1, sbk2 = bank('B'), bank('B')
                pend = []

                def flush_stats():
                    while pend:
                        c, pi = pend.pop(0)
                        S.op('pe', lambda e, c=c, pi=pi: e.matmul(
                            ps[:, sbk1, :], lhsT=onesb[:], rhs=TB[:, pi, :], start=(c == 0), stop=(c == 7)),
                            reads=['onesb', f"TB{pi}"], writes=[f"ps{sbk1}"])
                        S.op('pe', lambda e, c=c, pi=pi: e.matmul(
                            ps[:, sbk2, :], lhsT=onesb[:], rhs=TB[:, pi + 1, :], start=(c == 0), stop=(c == 7)),
                            reads=['onesb', f"TB{pi + 1}"], writes=[f"ps{sbk2}"])

                for i2 in range(2):
                    slot = acquire(('w', L, f'P1{i2}'))
                    w3 = ring[:, slot, :].rearrange("p (kc n) -> p kc n", kc=8)
                    for cc in range(4):
                        c = i2 * 4 + cc
                        bka = bank('A')
                        bkg = bank('A')
                        for kc in range(8):
                            S.op('pe', lambda e, cc=cc, kc=kc, bka=bka, w3=w3: e.matmul(
                                ps[:, bka, :], lhsT=w3[:, kc, cc * 128:(cc + 1) * 128], rhs=hT[:, kc, :],
                                start=(kc == 0), stop=(kc == 7)),
                                reads=[f"ring{slot}", f"hT{kc}"], writes=[f"ps{bka}"])
                        for kc in range(8):
                            S.op('pe', lambda e, cc=cc, kc=kc, bkg=bkg, w3=w3: e.matmul(
                                ps[:, bkg, :], lhsT=w3[:, kc, 512 + cc * 128:512 + (cc + 1) * 128], rhs=hT[:, kc, :],
                                start=(kc == 0), stop=(kc == 7)),
                                reads=[f"ring{slot}", f"hT{kc}"], writes=[f"ps{bkg}"])
                        ti = tf_lo()
                        S.op('act', lambda e, c=c, bkg=bkg, ti=ti: e.activation(
                            out=TF[:, ti, :], in_=ps[:, bkg, :], func=AF.Sigmoid, bias=col(('b1g', L), c)),
                            reads=[f"ps{bkg}", 'colsb'], writes=[f"TF{ti}"])
                        ub = ub8[:, c, :]
                        if u > 0:
                            S.op('dve', lambda e, c=c, ub=ub: e.tensor_copy(out=ub[:, 0:CW - 1], in_=uc[:, c, :]),
                                 reads=[f"ucar{cv}_{c}"], writes=[f"ub{c}"])
                        else:
                            S.op('dve', lambda e, ub=ub: e.memset(ub[:, 0:CW - 1], 0.0), writes=[f"ub{c}"])
                        S.op('dve', lambda e, c=c, ub=ub, bka=bka, ti=ti: e.scalar_tensor_tensor(
                            out=ub[:, CW - 1:CW - 1 + T], in0=ps[:, bka, :], scalar=col(('b1a', L), c),
                            in1=TF[:, ti, :], op0=ALU.add, op1=ALU.mult),
                            reads=[f"ps{bka}", f"TF{ti}", 'colsb'], writes=[f"ub{c}"])
                        S.op('dve', lambda e, c=c, ub=ub: e.tensor_copy(out=uc[:, c, :], in_=ub[:, T:T + CW - 1]),
                             reads=[f"ub{c}"], writes=[f"ucar{cv}_{c}"])
                    release()
                    for kk in range(2):
                        slot = acquire(('w', L, f'DG{2 * i2 + kk}'))
                        dg = ring[:, slot, :]
                        for ci in range(2):
                            c = i2 * 4 + kk * 2 + ci
                            ub = ub8[:, c, :]
                            bk = bank('A')
                            for j in range(CW):
                                m = ci * CW + j
                                S.op('pe', lambda e, j=j, m=m, bk=bk, ub=ub, dg=dg: e.matmul(
                                    ps[:, bk, :], lhsT=dg[:, m * 128:(m + 1) * 128], rhs=ub[:, j:j + T],
                                    start=(j == 0), stop=(j == CW - 1)),
                                    reads=[f"ring{slot}", f"ub{c}"], writes=[f"ps{bk}"])
                            vc = aTf[:, c * T:(c + 1) * T]
                            vtags = [f"aT{2 * c}", f"aT{2 * c + 1}"]
                            S.op('act', lambda e, c=c, bk=bk, vc=vc: e.activation(
                                out=vc, in_=ps[:, bk, :], func=AF.Identity, bias=col(('bdw', L), c)),
                                reads=[f"ps{bk}", 'colsb'], writes=vtags)
                            pi = tb_next(2)
                            S.op('act', lambda e, c=c, bk=bk, pi=pi: e.activation(
                                out=TB[:, pi, :], in_=ps[:, bk, :], func=AF.Identity, bias=col(('bdw', L), c)),
                                reads=[f"ps{bk}", 'colsb'], writes=[f"TB{pi}"])
                            S.op('act', lambda e, c=c, bk=bk, pi=pi: e.activation(
                                out=TB[:, pi + 1, :], in_=ps[:, bk, :], func=AF.Square, bias=col(('bdw', L), c)),
                                reads=[f"ps{bk}", 'colsb'], writes=[f"TB{pi + 1}"])
                            flush_stats()
                            pend.append((c, pi))
                        release()
                flush_stats()
                S.op('act', lambda e: e.activation(out=TF[:, 4, :], in_=ps[:, sbk1, :], func=AF.Copy, scale=1.0 / D),
                     reads=[f"ps{sbk1}"], writes=['TF4'])
                S.op('dve', lambda e: e.tensor_tensor(out=TF[:, 5, :], in0=TF[:, 4, :], in1=TF[:, 4, :], op=ALU.mult),
                     reads=['TF4'], writes=['TF5'])
                S.op('dve', lambda e: e.scalar_tensor_tensor(
                    out=TF[:, 5, :], in0=ps[:, sbk2, :], scalar=1.0 / D, in1=TF[:, 5, :], op0=ALU.mult, op1=ALU.subtract),
                    reads=[f"ps{sbk2}", 'TF5'], writes=['TF5'])
                S.op('act', lambda e: e.activation(out=TF[:, 5, :], in_=TF[:, 5, :], func=AF.Sqrt, bias=EPS),
                     reads=['TF5'], writes=['TF5'])
                S.op('dve', lambda e: e.reciprocal(out=TF[:, 5, :], in_=TF[:, 5, :]), reads=['TF5'], writes=['TF5'])
                for c in range(8):
                    vc = aTf[:, c * T:(c + 1) * T]
                    vtags = [f"aT{2 * c}", f"aT{2 * c + 1}"]
                    zi = tf_hi()
                    S.op('dve', lambda e, vc=vc, zi=zi: e.tensor_tensor(out=TF[:, zi, :], in0=vc, in1=TF[:, 4, :], op=ALU.subtract),
                         reads=vtags + ['TF4'], writes=[f"TF{zi}"])
                    S.op('dve', lambda e, zi=zi: e.tensor_tensor(out=TF[:, zi, :], in0=TF[:, zi, :], in1=TF[:, 5, :], op=ALU.mult),
                         reads=[f"TF{zi}", 'TF5'], writes=[f"TF{zi}"])
                    S.op('act', lambda e, c=c, zi=zi: e.activation(
                        out=hT[:, c, :], in_=TF[:, zi, :], func=AF.Silu, scale=col(('lng', L), c), bias=col(('lnb', L), c)),
                        reads=[f"TF{zi}", 'colsb'], writes=[f"hT{c}"])
                slot = acquire(('w', L, 'P2'))
                proj_residual(xb, slot, lambda kc: [f"hT{kc}"], lambda kc, tt: hT[:, kc, tt * 128:(tt + 1) * 128], 8,
                              par, lambda hf: rowsb[pr:pr + 1, ri, hf * 512:(hf + 1) * 512], pr)
                release()

            def mlp(L, b, u, xb, par):
                for s in range(4):
                    slot = acquire(('w', L, f'U{s}'))
                    w3 = ring[:, slot, :].rearrange("p (kc n) -> p kc n", kc=8)
                    for f in range(8):
                        bk = bank('A')
                        for kc in range(8):
                            S.op('pe', lambda e, f=f, kc=kc, bk=bk, w3=w3: e.matmul(
                                ps[:, bk, :], lhsT=w3[:, kc, f * 128:(f + 1) * 128], rhs=hT[:, kc, :],
                                start=(kc == 0), stop=(kc == 7)),
                                reads=[f"ring{slot}", f"hT{kc}"], writes=[f"ps{bk}"])
                        ti = tf_lo()
                        S.op('act', lambda e, bk=bk, ti=ti: e.activation(out=TF[:, ti, :], in_=ps[:, bk, :], func=AF.Relu),
                             reads=[f"ps{bk}"], writes=[f"TF{ti}"])
                        ch = s * 8 + f
                        S.op('dve', lambda e, ch=ch, ti=ti: e.tensor_tensor(
                            out=aT[:, ch, :], in0=TF[:, ti, :], in1=TF[:, ti, :], op=ALU.mult),
                            reads=[f"TF{ti}"], writes=[f"aT{ch}"])
                    release()
                for s in range(4):
                    slot = acquire(('w', L, f'D{s}'))
                    w3 = ring[:, slot, :].rearrange("p (kc n) -> p kc n", kc=32)
                    for tt in range(NT):
                        bk = bank('A')
                        for kc in range(32):
                            S.op('pe', lambda e, tt=tt, kc=kc, bk=bk, w3=w3: e.matmul(
                                ps[:, bk, 0:256], lhsT=aT[:, kc, tt * 128:(tt + 1) * 128], rhs=w3[:, kc, :],
                                start=(kc == 0), stop=(kc == 31)),
                                reads=[f"ring{slot}", f"aT{kc}"], writes=[f"ps{bk}"])
                        ti = tf_hi()
                        S.op('dve', lambda e, s=s, bk=bk, ti=ti: e.tensor_tensor(
                            out=TF[:, ti, 0:256], in0=ps[:, bk, 0:256], in1=Gb[:, par, 1, s * 256:(s + 1) * 256], op=ALU.mult),
                            reads=[f"ps{bk}", f"G{par}_1_{s // 2}"], writes=[f"TF{ti}"])
                        xv = xs[:, xb * NT + tt, s * 256:(s + 1) * 256]
                        S.op('dve', lambda e, xv=xv, ti=ti: e.tensor_tensor(out=xv, in0=xv, in1=TF[:, ti, 0:256], op=ALU.add),
                             reads=[f"TF{ti}"] + xtags(xb, tt, (s,)), writes=xtags(xb, tt, (s,)))
                    release()

            un = 0
            ul_list = [(L_, b_) for b_ in range(NSEQ) for u_ in range(NU) for L_ in range(DEPTH)]
            ul = 0
            load_gates(ul_list[0][0], ul_list[0][1], 0)
            for b in range(NSEQ):
                for u in range(NU):
                    xb = un % NXB
                    if NXB == 1:
                        if un > 0:
                            load_x(b, u, xb)
                    elif un + 1 < NSEQ * NU:
                        nb_, nu_ = divmod(un + 1, NU)
                        load_x(nb_, nu_, (un + 1) % NXB)
                    for L in range(DEPTH):
                        par = ul % 2
                        nxt = ul_list[ul + 1] if ul + 1 < len(ul_list) else None
                        defer_gates = (un == 0 and L + 1 < DEPTH)
                        if nxt is not None and not defer_gates:
                            load_gates(nxt[0], nxt[1], (ul + 1) % 2)
                        ul += 1
                        _stage('gates')
                        norm_to_hT(xb, lambda kc, L=L, b=b: A1[:, L, kc, b:b + 1],
                                   lambda kc, L=L, b=b: modT[:, L * 6 + 0, kc, b:b + 1],
                                   [f"A1_{L}", f"modT{L}_0"])
                        _stage('norm1')
                        if L % 2 == 0:
                            attention(L, b, u, xb, par)
                        else:
                            conformer(L, b, u, xb, par)
                        if defer_gates:
                            mod_layer(L + 1, ul % 2)
                            load_gates(nxt[0], nxt[1], ul % 2)
                        _stage('mixer')
                        norm_to_hT(xb, lambda kc, L=L, b=b: A2[:, L, kc, b:b + 1],
                                   lambda kc, L=L, b=b: modT[:, L * 6 + 3, kc, b:b + 1],
                                   [f"A2_{L}", f"modT{L}_3"])
                        _stage('norm2')
                        mlp(L, b, u, xb, par)
                        _stage('mlp')
                    for tt in range(NT):
                        xrow = xs[:, xb * NT + tt, :]
                        S.op('act', lambda e, tt=tt, xrow=xrow: e.activation(
                            out=ntm[:, tt, :], in_=xrow, func=AF.Square, accum_out=ss[:, tt:tt + 1]),
                            reads=xtags(xb, tt), writes=[f"ntm{tt}", f"ss{tt}"])
                        S.op('act', lambda e, tt=tt: e.activation(
                            out=sd[:, tt:tt + 1], in_=ss[:, tt:tt + 1], func=AF.Sqrt, scale=1.0 / D, bias=EPS),
                            reads=[f"ss{tt}"], writes=[f"sd{tt}"])
                        S.op('dve', lambda e, tt=tt: e.reciprocal(out=rs[:, tt:tt + 1], in_=sd[:, tt:tt + 1]),
                             reads=[f"sd{tt}"], writes=[f"rs{tt}"])
                        S.op('dve', lambda e, tt=tt, xrow=xrow: e.scalar_tensor_tensor(
                            out=ntm[:, tt, :], in0=xrow, scalar=rs[:, tt:tt + 1], in1=FN[:], op0=ALU.mult, op1=ALU.mult),
                            reads=xtags(xb, tt) + [f"rs{tt}", 'FN0', 'FN1'], writes=[f"ntm{tt}"])
                        ot = f"out{un}_{tt}"
                        out_tags.append(ot)
                        S.op('pool', lambda e, tt=tt, b=b, u=u: e.dma_start(
                            out=out_d[b, u * T + tt * 128:u * T + (tt + 1) * 128, :], in_=ntm[:, tt, :]),
                            reads=[f"ntm{tt}"], writes=[ot], dma=True)
                    un += 1

        stopped = False
        try:
            _body()
        except StopBuild:
            stopped = True
        assert stopped or wstate['cur'] == len(seq), (wstate, len(seq))
        S.op('pool', lambda e: e.nop(), reads=out_tags)
        S.emit()
    return nc, S


_CACHE = {}


def kernel(**inputs):
    NCORES = 8
    inp = {k: np.asarray(v) for k, v in inputs.items()}
    B, SEQ, _ = inp['x'].shape
    DEPTH = inp['w_mod'].shape[0]
    NSEQ = B // NCORES
    key = (NSEQ, SEQ, DEPTH)
    if key not in _CACHE:
        _CACHE[key] = build_program(NSEQ, SEQ, DEPTH)[0]
    nc = _CACHE[key]
    cols, rows = prep_shared(inp, DEPTH)
    dm, ident = host_constants()
    f32c = lambda a: np.ascontiguousarray(np.asarray(a, dtype=np.float32))
    shared = dict(cols=cols, rows=rows.reshape(128, 2 * 1152), dm=dm, ident=ident,
                  w_mod=f32c(inp['w_mod']), w_qkv=f32c(inp['w_qkv']), w_o=f32c(inp['w_o']),
                  w_pw1=f32c(inp['w_pw1']), w_pw2=f32c(inp['w_pw2']),
                  w_up=f32c(inp['w_up']), w_down=f32c(inp['w_down']))
    x = f32c(inp['x'])
    c = f32c(inp['c'])
    in_maps = []
    for ci in range(NCORES):
        cc = c[ci * NSEQ:(ci + 1) * NSEQ]
        cT = np.ascontiguousarray(cc.reshape(NSEQ, 8, 128).transpose(2, 1, 0)).reshape(128, 8 * NSEQ)
        m = dict(shared)
        m['x'] = x[ci * NSEQ:(ci + 1) * NSEQ]
        m['cT'] = cT
        in_maps.append(m)
    res = run_bass_kernel_spmd(nc, in_maps, core_ids=list(range(NCORES)))
    out = np.concatenate([np.asarray(r['out']) for r in res.results], axis=0)
    return out.astype(np.float32, copy=False)
```

```python
import contextlib
import numpy as np
import concourse.bass as bass
import concourse.mybir as mybir
from concourse.bass_utils import run_bass_kernel_spmd

F32 = mybir.dt.float32
BF16 = mybir.dt.bfloat16
AF = mybir.ActivationFunctionType
ALU = mybir.AluOpType

ENGS = ['pe', 'act', 'dve', 'pool', 'sp']
CENGS = ['pe', 'act', 'dve', 'pool']
SEM_W = 4000
DMA_RING = 8

D = 1024
NH = 16
HD = 64
NMOD = 6
DFF = 4096
CW = 31
EPS = 1e-6
T = 512
NT = 4
NSLOT = 3
SLAB = 8192
NEG = -1.0e6
OPROJ_INTERLEAVE = False


import os


class StopBuild(Exception):
    pass


def _stage(name):
    PHASE[0] = 'after_' + name
    if os.environ.get('KSTOP') == name:
        raise StopBuild(name)


PHASE = ['init']


class Sched:
    def __init__(self, nc):
        self.nc = nc
        self.ops = {e: [] for e in ENGS}
        self.tags = {}
        self.known = {e: {f: -1 for f in CENGS} for e in ENGS}
        self.known_dma = {e: set() for e in ENGS}
        self.ndma = {}

    def op(self, eng, fn, reads=(), writes=(), dma=False):
        idx = len(self.ops[eng])
        me = (eng, idx)
        raw, oth = set(), set()
        for t in reads:
            st = self.tags.get(t)
            if st is not None and st[0] is not None:
                raw.add(st[0])
        for t in writes:
            st = self.tags.get(t)
            if st is not None:
                if st[0] is not None:
                    oth.add(st[0])
                oth.update(st[1])
        deps = set(raw)
        for d in oth:
            if d[0] == eng and eng == 'pe' and not dma:
                continue
            deps.add(d)
        deps.discard(me)
        waits = []
        kn = self.known[eng]
        kd = self.known_dma[eng]
        for d in sorted(deps, key=lambda z: -z[1]):
            f, j = d
            o = self.ops[f][j]
            if o['dma']:
                if d in kd:
                    continue
                kd.add(d)
            else:
                if kn[f] >= j:
                    continue
                kn[f] = j
            waits.append(d)
            o['needs_inc'] = True
            for g, v in o['known'].items():
                if v > kn[g]:
                    kn[g] = v
        rec = dict(fn=fn, waits=waits, needs_inc=False, dma=dma, known=dict(kn), dma_k=None, phase=PHASE[0])
        if dma:
            rec['dma_k'] = self.ndma.get(eng, 0)
            self.ndma[eng] = rec['dma_k'] + 1
        self.ops[eng].append(rec)
        for t in reads:
            st = self.tags.setdefault(t, [None, []])
            st[1].append(me)
        for t in writes:
            self.tags[t] = [me, []]
        return me

    def emit(self):
        nc = self.nc
        with contextlib.ExitStack() as es:
            for e in CENGS:
                n_inc = sum(1 for o in self.ops[e] if o['needs_inc'] and not o['dma'])
                nsem = max(1, (n_inc + SEM_W - 1) // SEM_W)
                sems = [es.enter_context(nc.semaphore(f"s_{e}_{i}")) for i in range(nsem)]
                c = 0
                for o in self.ops[e]:
                    if o['needs_inc'] and not o['dma']:
                        o['sem'] = (sems[c // SEM_W], c % SEM_W + 1)
                        c += 1
            dma_by_k = {}
            for e in ENGS:
                dl = [o for o in self.ops[e] if o['dma']]
                if not dl:
                    continue
                dsems = [es.enter_context(nc.semaphore(f"s_dma_{e}_{i}")) for i in range(DMA_RING)]
                for o in dl:
                    k = o['dma_k']
                    o['sem'] = (dsems[k % DMA_RING], 16 * (k // DMA_RING + 1))
                    dma_by_k[(e, k)] = o
            block = es.enter_context(nc.Block())

            def run(eng_name):
                def body(eng):
                    for o in self.ops[eng_name]:
                        for (f, j) in o['waits']:
                            s, v = self.ops[f][j]['sem']
                            eng.wait_ge(s, v)
                        if o['dma']:
                            k = o['dma_k']
                            if k >= DMA_RING:
                                s, v = dma_by_k[(eng_name, k - DMA_RING)]['sem']
                                eng.wait_ge(s, v)
                            ins = o['fn'](eng)
                            ins.then_inc(o['sem'][0], 16)
                        else:
                            ins = o['fn'](eng)
                            if o['needs_inc']:
                                ins.then_inc(o['sem'][0], 1)
                return body

            block.tensor(run('pe'))
            block.scalar(run('act'))
            block.vector(run('dve'))
            block.gpsimd(run('pool'))
            block.sync(run('sp'))

    def stats(self):
        return {e: (len(self.ops[e]), sum(len(o['waits']) for o in self.ops[e])) for e in ENGS}


def col_layout(depth):
    lay = {}
    n = 0

    def add(name, w):
        nonlocal n
        lay[name] = n
        n += w

    for L in range(depth):
        add(('nmix', L), 8)
        add(('nmlp', L), 8)
        add(('bmod', L), 48)
        if L % 2 == 0:
            add(('bq', L), 8)
            add(('bk2', L), 2)
            add(('sink', L), 8)
        else:
            add(('b1a', L), 8)
            add(('b1g', L), 8)
            add(('wdw', L), CW * 8)
            add(('bdw', L), 8)
            add(('lng', L), 8)
            add(('lnb', L), 8)
    add(('fn',), 8)
    return lay, n


def colsT(v):
    v = np.asarray(v, dtype=np.float32)
    return np.ascontiguousarray(v.reshape(-1, 128).T)


def alibi_slopes():
    return [float(np.float32(2.0 ** (-8.0 * (h + 1) / NH))) for h in range(NH)]


def host_constants():
    s = np.arange(128)[:, None]
    t = np.arange(128)[None, :]
    prev = np.where(t < s, -(128 + t - s), NEG).astype(np.float32)
    cur = np.where(t >= s, -(t - s), NEG).astype(np.float32)
    dm = np.concatenate([prev, cur], axis=1).astype(np.float32)
    ident = np.eye(128, dtype=np.float32)
    return dm, ident


def prep_shared(inp, depth):
    lay, ncol = col_layout(depth)
    cols = np.zeros((128, ncol), np.float32)
    rows = np.zeros((128, 2, 1152), np.float32)
    for L in range(depth):
        j = L // 2
        cols[:, lay[('nmix', L)]:lay[('nmix', L)] + 8] = colsT(inp['norm_mix'][L])
        cols[:, lay[('nmlp', L)]:lay[('nmlp', L)] + 8] = colsT(inp['norm_mlp'][L])
        cols[:, lay[('bmod', L)]:lay[('bmod', L)] + 48] = colsT(inp['b_mod'][L])
        pr, ri = 32 * (L % 3), L // 3
        if L % 2 == 0:
            bqkv = np.asarray(inp['b_qkv'][j], np.float32)
            cols[:, lay[('bq', L)]:lay[('bq', L)] + 8] = colsT(bqkv[:1024])
            k0, k1 = bqkv[1024:1088], bqkv[1088:1152]
            cols[:, lay[('bk2', L)]:lay[('bk2', L)] + 2] = colsT(np.concatenate([k0, k0, k1, k1]))
            sk = np.asarray(inp['sinks'][j], np.float32)
            cols[:, lay[('sink', L)]:lay[('sink', L)] + 8] = colsT(np.repeat(sk, 64))
            rows[pr, ri, 0:128] = bqkv[1152:1280]
            rows[pr, ri, 128:1152] = np.asarray(inp['b_o'][j], np.float32)
        else:
            b1 = np.asarray(inp['b_pw1'][j], np.float32)
            cols[:, lay[('b1a', L)]:lay[('b1a', L)] + 8] = colsT(b1[:1024])
            cols[:, lay[('b1g', L)]:lay[('b1g', L)] + 8] = colsT(b1[1024:])
            wd = np.asarray(inp['w_dw'][j], np.float32)
            cols[:, lay[('wdw', L)]:lay[('wdw', L)] + CW * 8] = colsT(wd.reshape(-1))
            cols[:, lay[('bdw', L)]:lay[('bdw', L)] + 8] = colsT(inp['b_dw'][j])
            cols[:, lay[('lng', L)]:lay[('lng', L)] + 8] = colsT(inp['conv_ln_g'][j])
            cols[:, lay[('lnb', L)]:lay[('lnb', L)] + 8] = colsT(inp['conv_ln_b'][j])
            rows[pr, ri, 0:1024] = np.asarray(inp['b_pw2'][j], np.float32)
    cols[:, lay[('fn',)]:lay[('fn',)] + 8] = colsT(inp['final_norm'])
    return cols, rows


def build_program(NSEQ=4, SEQ=2048, DEPTH=4, NXB=1):
    NU = SEQ // T
    NA = (DEPTH + 1) // 2
    NCV = DEPTH // 2
    lay, NCOL = col_layout(DEPTH)
    slopes = alibi_slopes()
    nc = bass.Bass("TRN2", target_bir_lowering=False)

    def din(name, shape, dt=F32):
        return nc.dram_tensor(name, shape, dt, kind="ExternalInput").ap()

    x_d = din("x", [NSEQ, SEQ, D])
    cT_d = din("cT", [128, 8 * NSEQ])
    cols_d = din("cols", [128, NCOL])
    rows_d = din("rows", [128, 2 * 1152])
    dm_d = din("dm", [128, 256])
    ident_d = din("ident", [128, 128])
    w_mod_d = din("w_mod", [DEPTH, D, NMOD * D])
    w_qkv_d = din("w_qkv", [NA, D, 1280])
    w_o_d = din("w_o", [NA, D, D])
    w_pw1_d = din("w_pw1", [max(NCV, 1), D, 2 * D])
    w_pw2_d = din("w_pw2", [max(NCV, 1), D, D])
    w_up_d = din("w_up", [DEPTH, D, DFF])
    w_down_d = din("w_down", [DEPTH, DFF, D])
    out_d = nc.dram_tensor("out", [NSEQ, SEQ, D], F32, kind="ExternalOutput").ap()
    def layer_slabs(L):
        if L % 2 == 0:
            names = ['Q', 'KV', 'O']
        else:
            names = ['P10', 'DG0', 'DG1', 'P11', 'DG2', 'DG3', 'P2']
        return names + [f'U{s_}' for s_ in range(4)] + [f'D{s_}' for s_ in range(4)]

    SID = {}
    for L_ in range(DEPTH):
        for nm_ in layer_slabs(L_):
            SID[(L_, nm_)] = len(SID)
    for L_ in range(DEPTH):
        for s_ in range(6):
            SID[(L_, f'M{s_}')] = len(SID)
    wsc = nc.dram_tensor("wsc", [len(SID), 128, SLAB], BF16, kind="Internal").ap()
    gsc = nc.dram_tensor("gsc", [DEPTH * NSEQ * 2, 128, D], F32, kind="Internal").ap()

    S = Sched(nc)
    with contextlib.ExitStack() as es:
        def sb(name, shape, dt):
            return es.enter_context(nc.sbuf_tensor(name, shape, dt))

        xs = sb("xs", [128, NXB * NT, D], F32)
        hT = sb("hT", [128, 8, T], BF16)
        aT = sb("aT", [128, 32, T], BF16)
        aTf = aT[:].rearrange("p a t -> p (a t)").bitcast(F32)
        ntm = sb("ntm", [128, NT, D], F32)
        ring = sb("ring", [128, NSLOT, SLAB], BF16)
        Gb = sb("Gb", [128, 2, 2, D], F32)
        FN = sb("FN", [128, D], F32)
        qT = sb("qT", [128, 8, T], BF16)
        kT = [sb(f"kT{a}", [128, 2, T + 128], BF16) for a in range(NA)]
        Vt = [sb(f"V{a}", [128, NT + 1, 128], BF16) for a in range(NA)]
        DM = sb("DM", [128, 256], F32)
        TF = sb("TF", [128, 6, T], F32)
        TB = sb("TB", [128, 4, T], BF16)
        ub8 = sb("ub8", [128, 8, T + 32], BF16)
        ucar = [sb(f"ucar{a}", [128, 8, CW - 1], BF16) for a in range(max(NCV, 1))]
        identb = sb("identb", [128, 128], BF16)
        colsb = sb("colsb", [128, NCOL], F32)
        rowsb = sb("rowsb", [128, 2, 1152], BF16)
        identf = sb("identf", [128, 128], F32)
        onesf = sb("onesf", [128, 128], F32)
        onesb = sb("onesb", [128, 128], BF16)
        rep4 = TF[:, 4:6, :]
        cTf = sb("cTf", [128, 8 * NSEQ], F32)
        csT = sb("csT", [128, 8, NSEQ], BF16)
        modT = sb("modT", [128, DEPTH * 6, 8, NSEQ], F32)
        A1 = sb("A1", [128, DEPTH, 8, NSEQ], F32)
        A2 = sb("A2", [128, DEPTH, 8, NSEQ], F32)
        bq8 = sb("bq8", [128, NA, 8], F32)
        esink = sb("esink", [128, NA, 8], F32)
        ss = sb("ss", [128, 8], F32)
        sd = sb("sd", [128, 8], F32)
        rs = sb("rs", [128, 8], F32)
        ps = es.enter_context(nc.psum_tensor("ps", [128, 8, 512], F32))

        rot = {'A': [0, 1, 2, 3], 'B': [4, 5], 'C': [6, 7]}
        rpos = {'A': 0, 'B': 0, 'C': 0}

        def bank(cls):
            b = rot[cls][rpos[cls] % len(rot[cls])]
            rpos[cls] += 1
            return b

        tfpos = {'lo': 0, 'hi': 0}

        def tf_lo():
            i = tfpos['lo'] % 2
            tfpos['lo'] += 1
            return i

        def tf_hi():
            i = 2 + tfpos['hi'] % 2
            tfpos['hi'] += 1
            return i

        tbpos = [0]

        def tb_next(n=1):
            i = tbpos[0] % (4 // n)
            tbpos[0] += 1
            return i * n

        def col(name, j=0):
            c0 = lay[name] + j
            return colsb[:, c0:c0 + 1]

        def slab_id(L, k):
            return SID[(L, k)]

        seq = []
        for s in range(6):
            seq.append(('wmod', 0, s))
        for un in range(NSEQ * NU):
            for L in range(DEPTH):
                for k in layer_slabs(L):
                    if un == 0 and k == 'U0' and L + 1 < DEPTH:
                        for s in range(6):
                            seq.append(('wmod', L + 1, s))
                    seq.append(('w', L, k))
        wstate = {'loaded': 0, 'cur': 0}

        def slab_len(L, k):
            if k == 'KV':
                return 8 * 384
            if k.startswith('DG'):
                return 62 * 128
            return SLAB

        def slab_tags(L, k):
            sid = slab_id(L, k)
            if k == 'KV':
                return [f"wsc{sid}_{d}" for d in (0, 64, 128, 192, 256)]
            if k in ('P10', 'P11'):
                return [f"wsc{sid}_a", f"wsc{sid}_g"]
            return [f"wsc{sid}"]

        def record_load(i):
            ent = seq[i]
            slot = i % NSLOT
            if ent[0] == 'wmod':
                _, L, s = ent
                sid = slab_id(L, f'M{s}')
                S.op('sp', lambda e: e.dma_start(out=ring[:, slot, :], in_=wsc[sid]),
                     reads=[f"wsc{sid}"], writes=[f"ring{slot}"], dma=True)
            else:
                _, L, k = ent
                n = slab_len(L, k)
                sid = slab_id(L, k)
                S.op('sp', lambda e: e.dma_start(out=ring[:, slot, 0:n], in_=wsc[sid][:, 0:n]),
                     reads=slab_tags(L, k), writes=[f"ring{slot}"], dma=True)

        def prefetch():
            while wstate['loaded'] < len(seq) and wstate['loaded'] < wstate['cur'] + NSLOT:
                record_load(wstate['loaded'])
                wstate['loaded'] += 1

        def acquire(expect):
            i = wstate['cur']
            assert seq[i] == expect, (seq[i], expect)
            prefetch()
            return i % NSLOT

        def release():
            wstate['cur'] += 1
            prefetch()

        def load_x(b, u, xb):
            for tt in range(NT):
                S.op('act', lambda e, tt=tt: e.dma_start(
                    out=xs[:, xb * NT + tt, :], in_=x_d[b, u * T + tt * 128:u * T + (tt + 1) * 128, :]),
                    writes=[f"x{xb}_{tt}_{q}" for q in range(4)], dma=True)

        out_tags = []

        def _body():
            load_x(0, 0, 0)
            S.op('sp', lambda e: e.dma_start(out=colsb[:], in_=cols_d), writes=['colsb'], dma=True)
            S.op('sp', lambda e: e.dma_start(out=identf[:], in_=ident_d), writes=['identf'], dma=True)
            S.op('sp', lambda e: e.dma_start(out=DM[:], in_=dm_d), writes=['DM'], dma=True)
            S.op('sp', lambda e: e.dma_start(out=cTf[:], in_=cT_d), writes=['cTf'], dma=True)
            S.op('pool', lambda e: e.dma_start(out=rowsb[:].rearrange("p a n -> p (a n)"), in_=rows_d),
                 writes=['rowsb'], dma=True)
            _stage('dma0')

            def conv_slab(sid, src3, n):
                dst = wsc[sid][:, 0:8 * n].rearrange("p (kc n) -> p kc n", kc=8)
                S.op('pool', lambda e: e.dma_start(out=dst, in_=src3), writes=[f"wsc{sid}"], dma=True)

            def r3(ap2):
                return ap2.rearrange("(kc p) n -> p kc n", p=128)

            def convert_layer(L):
                j = L // 2
                if L % 2 == 0:
                    conv_slab(slab_id(L, 'Q'), r3(w_qkv_d[j][:, 0:1024]), 1024)
                    sid = slab_id(L, 'KV')
                    dst3 = wsc[sid][:, 0:8 * 384].rearrange("p (kc n) -> p kc n", kc=8)
                    pieces = [(0, 1024), (64, 1024), (128, 1088), (192, 1088)]
                    tags = [f"wsc{sid}"]
                    for (d0, s0) in pieces:
                        S.op('pool', lambda e, d0=d0, s0=s0: e.dma_start(
                            out=dst3[:, :, d0:d0 + 64], in_=r3(w_qkv_d[j][:, s0:s0 + 64])),
                            writes=[f"wsc{sid}_{d0}"], dma=True)
                    S.op('pool', lambda e: e.dma_start(out=dst3[:, :, 256:384], in_=r3(w_qkv_d[j][:, 1152:1280])),
                         writes=[f"wsc{sid}_256"], dma=True)
                    conv_slab(slab_id(L, 'O'), r3(w_o_d[j]), 1024)
                else:
                    for i in range(2):
                        sid = slab_id(L, f'P1{i}')
                        dst3 = wsc[sid].rearrange("p (kc n) -> p kc n", kc=8)
                        S.op('pool', lambda e, i=i, dst3=dst3: e.dma_start(
                            out=dst3[:, :, 0:512], in_=r3(w_pw1_d[j][:, i * 512:(i + 1) * 512])),
                            writes=[f"wsc{sid}_a"], dma=True)
                        S.op('pool', lambda e, i=i, dst3=dst3: e.dma_start(
                            out=dst3[:, :, 512:1024], in_=r3(w_pw1_d[j][:, 1024 + i * 512:1024 + (i + 1) * 512])),
                            writes=[f"wsc{sid}_g"], dma=True)
                    conv_slab(slab_id(L, 'P2'), r3(w_pw2_d[j]), 1024)
                for s in range(4):
                    conv_slab(slab_id(L, f'U{s}'), r3(w_up_d[L][:, s * 1024:(s + 1) * 1024]), 1024)
                for s in range(4):
                    sid = slab_id(L, f'D{s}')
                    dst = wsc[sid].rearrange("p (kc n) -> p kc n", kc=32)
                    src = w_down_d[L][:, s * 256:(s + 1) * 256].rearrange("(kc p) n -> p kc n", p=128)
                    S.op('pool', lambda e, dst=dst, src=src: e.dma_start(out=dst, in_=src),
                         writes=[f"wsc{sid}"], dma=True)

            def convert_wmod(L):
                for s_ in range(6):
                    conv_slab(slab_id(L, f'M{s_}'), r3(w_mod_d[L][:, s_ * 1024:(s_ + 1) * 1024]), 1024)

            convert_wmod(0)
            S.op('act', lambda e: e.copy(out=identb[:], in_=identf[:]), reads=['identf'], writes=['identb'])
            aTflat = aT[:].rearrange("p a t -> p (a t)")

            def build_diag(L):
                cw0 = lay[('wdw', L)]
                for k in range(4):
                    hb = k % 2
                    stg = aTflat[:, hb * 8192:(hb + 1) * 8192]
                    stags = [f"aT{hb * 16 + q}" for q in range(16)]
                    for ci in range(2):
                        c = 2 * k + ci
                        wcols = colsb[:, cw0 + c:cw0 + c + 8 * (CW - 1) + 1:8]
                        S.op('dve', lambda e, ci=ci, stg=stg, wcols=wcols: e.tensor_tensor(
                            out=stg[:, ci * CW * 128:(ci + 1) * CW * 128].rearrange("p (j n) -> p j n", j=CW),
                            in0=identb[:].unsqueeze(1).to_broadcast([128, CW, 128]),
                            in1=wcols.unsqueeze(2).to_broadcast([128, CW, 128]), op=ALU.mult),
                            reads=['identb', 'colsb'], writes=stags)
                    sid = slab_id(L, f'DG{k}')
                    S.op('pool', lambda e, sid=sid, stg=stg: e.dma_start(out=wsc[sid][:, 0:62 * 128], in_=stg[:, 0:62 * 128]),
                         reads=stags, writes=[f"wsc{sid}"], dma=True)

            for L_ in range(1, DEPTH, 2):
                build_diag(L_)

            convert_layer(0)
            for L_ in range(1, DEPTH):
                convert_wmod(L_)
                convert_layer(L_)
            _stage('conv0')

            S.op('dve', lambda e: e.memset(onesf[:], 1.0), writes=['onesf'])
            S.op('dve', lambda e: e.memset(onesb[:], 1.0), writes=['onesb'])
            S.op('act', lambda e: e.activation(out=csT[:].rearrange("p k b -> p (k b)"), in_=cTf[:], func=AF.Silu),
                 reads=['cTf'], writes=['csT'])
            for a in range(NA):
                L = 2 * a
                c0 = lay[('bq', L)]
                S.op('act', lambda e, a=a, c0=c0: e.mul(out=bq8[:, a, :], in_=colsb[:, c0:c0 + 8], mul=0.125),
                     reads=['colsb'], writes=['bq8'])
                c1 = lay[('sink', L)]
                S.op('act', lambda e, a=a, c1=c1: e.activation(out=esink[:, a, :], in_=colsb[:, c1:c1 + 8], func=AF.Exp),
                     reads=['colsb'], writes=['esink'])
            _stage('small')

            def bcast_tile(dst, dst_tag, colfn4, col_tags):
                for hf in range(2):
                    bk = bank('B')
                    ri = hf
                    S.op('dve', lambda e, hf=hf, ri=ri: e.tensor_tensor(
                        out=rep4[:, ri, :].rearrange("p (j n) -> p j n", j=4),
                        in0=identf[:].unsqueeze(1).to_broadcast([128, 4, 128]),
                        in1=colfn4(hf).unsqueeze(2).to_broadcast([128, 4, 128]), op=ALU.mult),
                        reads=['identf'] + col_tags, writes=[f"TF{4 + ri}"])
                    S.op('pe', lambda e, ri=ri, bk=bk: e.matmul(
                        ps[:, bk, :], lhsT=onesf[:], rhs=rep4[:, ri, :], start=True, stop=True),
                        reads=[f"TF{4 + ri}", 'onesf'], writes=[f"ps{bk}"])
                    S.op('act', lambda e, hf=hf, bk=bk: e.copy(out=dst[:, hf * 512:(hf + 1) * 512], in_=ps[:, bk, :]),
                         reads=[f"ps{bk}"], writes=[f"{dst_tag}{hf}"])

            gcount = [0]

            def mod_layer(L, spar):
                gstage = [(Gb[:, spar, 0, :], f'G{spar}_0_'), (Gb[:, spar, 1, :], f'G{spar}_1_')]
                for s in range(6):
                    slot = acquire(('wmod', L, s))
                    w3 = ring[:, slot, :].rearrange("p (kc n) -> p kc n", kc=8)
                    mi = (L * 6 + s) % 2
                    mrow = TF[0:NSEQ, 2 * mi:2 * mi + 2, :].rearrange("p a t -> p (a t)")
                    mtags = [f"TF{2 * mi}", f"TF{2 * mi + 1}"]
                    for ch in range(2):
                        bka = bank('A')
                        for kc in range(8):
                            S.op('pe', lambda e, ch=ch, kc=kc, w3=w3, bka=bka: e.matmul(
                                ps[0:NSEQ, bka, :], lhsT=csT[:, kc, :], rhs=w3[:, kc, ch * 512:(ch + 1) * 512],
                                start=(kc == 0), stop=(kc == 7)),
                                reads=[f"ring{slot}", 'csT'], writes=[f"ps{bka}"])
                        S.op('act', lambda e, ch=ch, bka=bka, mrow=mrow: e.copy(
                            out=mrow[:, ch * 512:(ch + 1) * 512], in_=ps[0:NSEQ, bka, :]),
                            reads=[f"ps{bka}"], writes=[mtags[ch]])
                    bk = bank('B')
                    for j in range(8):
                        S.op('pe', lambda e, j=j, bk=bk, mrow=mrow: e.transpose(
                            ps[:, bk, j * NSEQ:(j + 1) * NSEQ], mrow[:, j * 128:(j + 1) * 128], identf[0:NSEQ, 0:NSEQ]),
                            reads=mtags + ['identf'], writes=[f"ps{bk}"])
                    c0 = lay[('bmod', L)] + s * 8
                    S.op('dve', lambda e, L=L, s=s, bk=bk, c0=c0: e.tensor_tensor(
                        out=modT[:, L * 6 + s, :, :],
                        in0=ps[:, bk, 0:8 * NSEQ].rearrange("p (j b) -> p j b", j=8),
                        in1=colsb[:, c0:c0 + 8].unsqueeze(2).to_broadcast([128, 8, NSEQ]), op=ALU.add),
                        reads=[f"ps{bk}", 'colsb'], writes=[f"modT{L}_{s}"])
                    release()
                    if s % 2 == 1 and (L * 3 + s // 2 + 1) < DEPTH:
                        pass
                cm = lay[('nmix', L)]
                S.op('dve', lambda e, L=L, cm=cm: e.scalar_tensor_tensor(
                    out=A1[:, L, :, :], in0=modT[:, L * 6 + 1, :, :], scalar=1.0,
                    in1=colsb[:, cm:cm + 8].unsqueeze(2).to_broadcast([128, 8, NSEQ]), op0=ALU.add, op1=ALU.mult),
                    reads=[f"modT{L}_1", 'colsb'], writes=[f"A1_{L}"])
                cm2 = lay[('nmlp', L)]
                S.op('dve', lambda e, L=L, cm2=cm2: e.scalar_tensor_tensor(
                    out=A2[:, L, :, :], in0=modT[:, L * 6 + 4, :, :], scalar=1.0,
                    in1=colsb[:, cm2:cm2 + 8].unsqueeze(2).to_broadcast([128, 8, NSEQ]), op0=ALU.add, op1=ALU.mult),
                    reads=[f"modT{L}_4", 'colsb'], writes=[f"A2_{L}"])
                for b_ in range(NSEQ):
                    for wh, sec in ((0, 2), (1, 5)):
                        stg, stag = gstage[gcount[0] % len(gstage)]
                        gcount[0] += 1
                        bcast_tile(stg, stag, lambda hf, L=L, b_=b_, sec=sec: modT[:, L * 6 + sec, hf * 4:(hf + 1) * 4, b_],
                                   [f"modT{L}_{sec}"])
                        gi = (L * NSEQ + b_) * 2 + wh
                        S.op('act', lambda e, gi=gi, stg=stg: e.dma_start(out=gsc[gi], in_=stg),
                             reads=[f"{stag}0", f"{stag}1"], writes=[f"gsc{gi}"], dma=True)
            mod_layer(0, 1)
            _stage('mod')
            cfn = lay[('fn',)]
            bcast_tile(FN, 'FN', lambda hf: colsb[:, cfn + hf * 4:cfn + (hf + 1) * 4], ['colsb'])

            def load_gates(L, b, par):
                for wh in range(2):
                    gi = (L * NSEQ + b) * 2 + wh
                    S.op('act', lambda e, gi=gi, wh=wh: e.dma_start(out=Gb[:, par, wh, :], in_=gsc[gi]),
                         reads=[f"gsc{gi}"], writes=[f"G{par}_{wh}_0", f"G{par}_{wh}_1"], dma=True)
            _stage('fn')

            def xtags(xb, tt, qs=range(4)):
                return [f"x{xb}_{tt}_{q}" for q in qs]

            def norm_to_hT(xb, Acol, Scol, col_tags):
                for tt in range(NT):
                    xrow = xs[:, xb * NT + tt, :]
                    S.op('act', lambda e, tt=tt, xrow=xrow: e.activation(
                        out=ntm[:, tt, :], in_=xrow, func=AF.Square, accum_out=ss[:, tt:tt + 1]),
                        reads=xtags(xb, tt), writes=[f"ntm{tt}", f"ss{tt}"])
                    S.op('act', lambda e, tt=tt: e.activation(
                        out=sd[:, tt:tt + 1], in_=ss[:, tt:tt + 1], func=AF.Sqrt, scale=1.0 / D, bias=EPS),
                        reads=[f"ss{tt}"], writes=[f"sd{tt}"])
                    S.op('dve', lambda e, tt=tt: e.reciprocal(out=rs[:, tt:tt + 1], in_=sd[:, tt:tt + 1]),
                         reads=[f"sd{tt}"], writes=[f"rs{tt}"])
                    S.op('dve', lambda e, tt=tt, xrow=xrow: e.tensor_scalar(
                        out=ntm[:, tt, :], in0=xrow, scalar1=rs[:, tt:tt + 1], scalar2=None, op0=ALU.mult),
                        reads=xtags(xb, tt) + [f"rs{tt}"], writes=[f"ntm{tt}"])
                for kc in range(8):
                    bk = bank('B')
                    for tt in range(NT):
                        S.op('pe', lambda e, kc=kc, tt=tt, bk=bk: e.transpose(
                            ps[:, bk, tt * 128:(tt + 1) * 128], ntm[:, tt, kc * 128:(kc + 1) * 128], identf[:]),
                            reads=[f"ntm{tt}", 'identf'], writes=[f"ps{bk}"])
                    if kc % 2 == 0:
                        S.op('dve', lambda e, kc=kc, bk=bk: e.tensor_scalar(
                            out=hT[:, kc, :], in0=ps[:, bk, :], scalar1=Acol(kc), scalar2=Scol(kc),
                            op0=ALU.mult, op1=ALU.add),
                            reads=[f"ps{bk}"] + col_tags, writes=[f"hT{kc}"])
                    else:
                        S.op('act', lambda e, kc=kc, bk=bk: e.activation(
                            out=hT[:, kc, :], in_=ps[:, bk, :], func=AF.Identity, scale=Acol(kc), bias=Scol(kc)),
                            reads=[f"ps{bk}"] + col_tags, writes=[f"hT{kc}"])

            def proj_residual(xb, slot, src_tags_fn, lhs_fn, nk, par, bias_row, pr, tts=range(NT)):
                w3 = ring[:, slot, :].rearrange("p (kc n) -> p kc n", kc=nk)
                for tt in tts:
                    for hf in range(2):
                        bk = bank('A')
                        for kc in range(nk):
                            S.op('pe', lambda e, kc=kc, tt=tt, hf=hf, bk=bk: e.matmul(
                                ps[:, bk, :], lhsT=lhs_fn(kc, tt), rhs=w3[:, kc, hf * 512:(hf + 1) * 512],
                                start=(kc == 0), stop=False),
                                reads=[f"ring{slot}"] + src_tags_fn(kc), writes=[f"ps{bk}"])
                        S.op('pe', lambda e, hf=hf, bk=bk: e.matmul(
                            ps[:, bk, :], lhsT=onesb[pr:pr + 1, :], rhs=bias_row(hf), start=False, stop=True),
                            reads=['onesb', 'rowsb'], writes=[f"ps{bk}"])
                        ti = tf_hi()
                        S.op('dve', lambda e, hf=hf, bk=bk, ti=ti: e.tensor_tensor(
                            out=TF[:, ti, :], in0=ps[:, bk, :], in1=Gb[:, par, 0, hf * 512:(hf + 1) * 512], op=ALU.mult),
                            reads=[f"ps{bk}", f"G{par}_0_{hf}"], writes=[f"TF{ti}"])
                        xv = xs[:, xb * NT + tt, hf * 512:(hf + 1) * 512]
                        S.op('dve', lambda e, xv=xv, ti=ti: e.tensor_tensor(out=xv, in0=xv, in1=TF[:, ti, :], op=ALU.add),
                             reads=[f"TF{ti}"] + xtags(xb, tt, (2 * hf, 2 * hf + 1)),
                             writes=xtags(xb, tt, (2 * hf, 2 * hf + 1)))

            def attention(L, b, u, xb, par):
                a = L // 2
                pr, ri = 32 * (L % 3), L // 3
                kTa, Va = kT[a], Vt[a]
                if u > 0:
                    S.op('act', lambda e: e.copy(out=kTa[:, :, 0:128], in_=kTa[:, :, T:T + 128]),
                         reads=[f"kT{a}_3"], writes=[f"kT{a}_c"])
                    S.op('act', lambda e: e.copy(out=Va[:, 0, :], in_=Va[:, NT, :]),
                         reads=[f"V{a}_3"], writes=[f"V{a}_c"])
                slot = acquire(('w', L, 'Q'))
                w3 = ring[:, slot, :].rearrange("p (kc n) -> p kc n", kc=8)
                for c in range(8):
                    bk = bank('A')
                    for kc in range(8):
                        S.op('pe', lambda e, c=c, kc=kc, bk=bk, w3=w3: e.matmul(
                            ps[:, bk, :], lhsT=w3[:, kc, c * 128:(c + 1) * 128], rhs=hT[:, kc, :],
                            start=(kc == 0), stop=(kc == 7)),
                            reads=[f"ring{slot}", f"hT{kc}"], writes=[f"ps{bk}"])
                    S.op('act', lambda e, c=c, bk=bk: e.activation(
                        out=qT[:, c, :], in_=ps[:, bk, :], func=AF.Identity, scale=0.125, bias=bq8[:, a, c:c + 1]),
                        reads=[f"ps{bk}", 'bq8'], writes=[f"qT{c}"])
                release()
                _stage('attQ')
                slot = acquire(('w', L, 'KV'))
                w3 = ring[:, slot, 0:8 * 384].rearrange("p (kc n) -> p kc n", kc=8)
                for c2 in range(2):
                    bk = bank('A')
                    for kc in range(8):
                        S.op('pe', lambda e, c2=c2, kc=kc, bk=bk, w3=w3: e.matmul(
                            ps[:, bk, :], lhsT=w3[:, kc, c2 * 128:(c2 + 1) * 128], rhs=hT[:, kc, :],
                            start=(kc == 0), stop=(kc == 7)),
                            reads=[f"ring{slot}", f"hT{kc}"], writes=[f"ps{bk}"])
                    S.op('act', lambda e, c2=c2, bk=bk: e.activation(
                        out=kTa[:, c2, 128:128 + T], in_=ps[:, bk, :], func=AF.Identity,
                        bias=col(('bk2', L), c2)),
                        reads=[f"ps{bk}", 'colsb'], writes=[f"kT{a}_{i}" for i in range(NT)])
                for tt in range(NT):
                    bk = bank('A')
                    for kc in range(8):
                        S.op('pe', lambda e, tt=tt, kc=kc, bk=bk, w3=w3: e.matmul(
                            ps[:, bk, 0:128], lhsT=hT[:, kc, tt * 128:(tt + 1) * 128], rhs=w3[:, kc, 256:384],
                            start=(kc == 0), stop=False),
                            reads=[f"ring{slot}", f"hT{kc}"], writes=[f"ps{bk}"])
                    S.op('pe', lambda e, bk=bk: e.matmul(
                        ps[:, bk, 0:128], lhsT=onesb[pr:pr + 1, :], rhs=rowsb[pr:pr + 1, ri, 0:128],
                        start=False, stop=True),
                        reads=['onesb', 'rowsb'], writes=[f"ps{bk}"])
                    S.op('act', lambda e, tt=tt, bk=bk: e.copy(out=Va[:, 1 + tt, :], in_=ps[:, bk, 0:128]),
                         reads=[f"ps{bk}"], writes=[f"V{a}_{tt}"])
                release()
                _stage('attKV')
                def ktag(slot_i):
                    return f"kT{a}_c" if slot_i == 0 else f"kT{a}_{slot_i - 1}"

                def vtag(slot_i):
                    return f"V{a}_c" if slot_i == 0 else f"V{a}_{slot_i - 1}"

                pairs = [(i, hf, p2) for i in range(NT) for hf in range(2) for p2 in range(2)]
                pstate = {}
                hstate = {}

                def stage_a(pidx):
                    i, hf, p2 = pairs[pidx]
                    has_prev = not (u == 0 and i == 0)
                    kbs = [0, 1] if has_prev else [1]
                    c0 = hf * 4 + p2 * 2
                    kvh = c0 // 4
                    sbanks = (bank('A'), bank('A'))
                    for cq in range(2):
                        c_ = c0 + cq
                        for hh in range(2):
                            S3 = ps[:, sbanks[hh], :].rearrange("p (q k t) -> p q k t", q=2, k=2)
                            for kb in kbs:
                                S.op('pe', lambda e, hh=hh, kb=kb, c_=c_, cq=cq, S3=S3: e.matmul(
                                    S3[:, cq, kb, :],
                                    lhsT=kTa[hh * 64:(hh + 1) * 64, kvh, (i + kb) * 128:(i + kb + 1) * 128],
                                    rhs=qT[hh * 64:(hh + 1) * 64, c_, i * 128:(i + 1) * 128],
                                    start=True, stop=True),
                                    reads=[ktag(i + kb), f"qT{c_}"], writes=[f"ps{sbanks[hh]}"])
                    info = []
                    for cq in range(2):
                        c = c0 + cq
                        sviews = [ps[:, sbanks[hh], :].rearrange("p (q k t) -> p q k t", q=2, k=2)[:, cq, :, :]
                                  for hh in range(2)]
                        ti = tf_lo()
                        T4 = TF[:, ti, :].rearrange("p (h k t) -> p h k t", h=2, k=2)
                        for hh in range(2):
                            h = 2 * c + hh
                            if has_prev:
                                S.op('dve', lambda e, hh=hh, h=h, sv=sviews[hh], T4=T4: e.scalar_tensor_tensor(
                                    out=T4[:, hh, :, :], in0=DM[:].rearrange("p (k t) -> p k t", k=2),
                                    scalar=slopes[h], in1=sv, op0=ALU.mult, op1=ALU.add),
                                    reads=['DM', f"ps{sbanks[hh]}"], writes=[f"TF{ti}"])
                            else:
                                S.op('dve', lambda e, hh=hh, h=h, sv=sviews[hh], T4=T4: e.scalar_tensor_tensor(
                                    out=T4[:, hh, 1, :], in0=DM[:, 128:256],
                                    scalar=slopes[h], in1=sv[:, 1, :], op0=ALU.mult, op1=ALU.add),
                                    reads=['DM', f"ps{sbanks[hh]}"], writes=[f"TF{ti}"])
                        pi = tb_next(1)
                        P4 = TB[:, pi, :].rearrange("p (h k t) -> p h k t", h=2, k=2)
                        if has_prev:
                            S.op('act', lambda e, ti=ti, pi=pi: e.activation(out=TB[:, pi, :], in_=TF[:, ti, :], func=AF.Exp),
                                 reads=[f"TF{ti}"], writes=[f"TB{pi}"])
                        else:
                            S.op('act', lambda e, T4=T4, P4=P4: e.activation(out=P4[:, :, 1, :], in_=T4[:, :, 1, :], func=AF.Exp),
                                 reads=[f"TF{ti}"], writes=[f"TB{pi}"])
                        info.append((c, pi, P4))
                    pstate[pidx] = (kbs, kvh, info)

                def stage_b(pidx):
                    i, hf, p2 = pairs[pidx]
                    kbs, kvh, info = pstate.pop(pidx)
                    if p2 == 0:
                        hstate[(i, hf)] = (bank('C'), bank('B'))
                    nb, db = hstate[(i, hf)]
                    for (c, pi, P4) in info:
                        cc = c % 4
                        for hh in range(2):
                            for n_i, kb in enumerate(kbs):
                                st, sp_ = (n_i == 0), (n_i == len(kbs) - 1)
                                S.op('pe', lambda e, hh=hh, kb=kb, cc=cc, P4=P4, st=st, sp_=sp_: e.matmul(
                                    ps[hh * 64:(hh + 1) * 64, nb, cc * 128:(cc + 1) * 128],
                                    lhsT=Va[:, i + kb, kvh * 64:(kvh + 1) * 64], rhs=P4[:, hh, kb, :],
                                    start=st, stop=sp_),
                                    reads=[vtag(i + kb), f"TB{pi}"], writes=[f"ps{nb}"])
                                S.op('pe', lambda e, hh=hh, kb=kb, cc=cc, P4=P4, st=st, sp_=sp_: e.matmul(
                                    ps[hh * 64:(hh + 1) * 64, db, cc * 128:(cc + 1) * 128],
                                    lhsT=onesb[:, 0:64], rhs=P4[:, hh, kb, :], start=st, stop=sp_),
                                    reads=['onesb', f"TB{pi}"], writes=[f"ps{db}"])
                    if p2 == 1:
                        ri2 = tf_hi()
                        R3 = TF[:, ri2, :].rearrange("p (c t) -> p c t", c=4)
                        S.op('dve', lambda e, R3=R3: e.tensor_tensor(
                            out=R3, in0=ps[:, db, :].rearrange("p (c t) -> p c t", c=4),
                            in1=esink[:, a, hf * 4:(hf + 1) * 4].unsqueeze(2).to_broadcast([128, 4, 128]), op=ALU.add),
                            reads=[f"ps{db}", 'esink'], writes=[f"TF{ri2}"])
                        S.op('act', lambda e: e.activation(out=TF[:, ri2, :], in_=TF[:, ri2, :], func=AF.Ln),
                             reads=[f"TF{ri2}"], writes=[f"TF{ri2}"])
                        S.op('act', lambda e: e.activation(out=TF[:, ri2, :], in_=TF[:, ri2, :], func=AF.Exp, scale=-1.0),
                             reads=[f"TF{ri2}"], writes=[f"TF{ri2}"])
                        S.op('dve', lambda e, R3=R3: e.tensor_tensor(
                            out=hT[:, hf * 4:(hf + 1) * 4, i * 128:(i + 1) * 128],
                            in0=ps[:, nb, :].rearrange("p (c t) -> p c t", c=4), in1=R3, op=ALU.mult),
                            reads=[f"ps{nb}", f"TF{ri2}"],
                            writes=([f"hT{c_}" for c_ in range(hf * 4, hf * 4 + 4)] if i == 0 else []) +
                                   [f"oT{c_}_{i}" for c_ in range(hf * 4, hf * 4 + 4)])

                oslot = acquire(('w', L, 'O'))

                def oproj(tt_):
                    proj_residual(xb, oslot, lambda kc: [f"hT{kc}", f"oT{kc}_{tt_}"],
                                  lambda kc, tt: hT[:, kc, tt * 128:(tt + 1) * 128], 8, par,
                                  lambda hf: rowsb[pr:pr + 1, ri, 128 + hf * 512:128 + (hf + 1) * 512], pr,
                                  tts=[tt_])

                stage_a(0)
                for pidx in range(len(pairs)):
                    if pidx + 1 < len(pairs):
                        stage_a(pidx + 1)
                    stage_b(pidx)
                    if OPROJ_INTERLEAVE and pidx % 4 == 1 and pidx // 4 >= 1:
                        oproj(pidx // 4 - 1)
                if OPROJ_INTERLEAVE:
                    oproj(NT - 1)
                else:
                    for tt_ in range(NT):
                        oproj(tt_)
                release()
                _stage('attN')

            def conformer(L, b, u, xb, par):
                cv = L // 2
                pr, ri = 32 * (L % 3), L // 3
                uc = ucar[cv]
                sbk1, sbk2 = bank('B'), bank('B')
                pend = []

                def flush_stats():
                    while pend:
                        c, pi = pend.pop(0)
                        S.op('pe', lambda e, c=c, pi=pi: e.matmul(
                            ps[:, sbk1, :], lhsT=onesb[:], rhs=TB[:, pi, :], start=(c == 0), stop=(c == 7)),
                            reads=['onesb', f"TB{pi}"], writes=[f"ps{sbk1}"])
                        S.op('pe', lambda e, c=c, pi=pi: e.matmul(
                            ps[:, sbk2, :], lhsT=onesb[:], rhs=TB[:, pi + 1, :], start=(c == 0), stop=(c == 7)),
                            reads=['onesb', f"TB{pi + 1}"], writes=[f"ps{sbk2}"])

                for i2 in range(2):
                    slot = acquire(('w', L, f'P1{i2}'))
                    w3 = ring[:, slot, :].rearrange("p (kc n) -> p kc n", kc=8)
                    for cc in range(4):
                        c = i2 * 4 + cc
                        bka = bank('A')
                        bkg = bank('A')
                        for kc in range(8):
                            S.op('pe', lambda e, cc=cc, kc=kc, bka=bka, w3=w3: e.matmul(
                                ps[:, bka, :], lhsT=w3[:, kc, cc * 128:(cc + 1) * 128], rhs=hT[:, kc, :],
                                start=(kc == 0), stop=(kc == 7)),
                                reads=[f"ring{slot}", f"hT{kc}"], writes=[f"ps{bka}"])
                        for kc in range(8):
                            S.op('pe', lambda e, cc=cc, kc=kc, bkg=bkg, w3=w3: e.matmul(
                                ps[:, bkg, :], lhsT=w3[:, kc, 512 + cc * 128:512 + (cc + 1) * 128], rhs=hT[:, kc, :],
                                start=(kc == 0), stop=(kc == 7)),
                                reads=[f"ring{slot}", f"hT{kc}"], writes=[f"ps{bkg}"])
                        ti = tf_lo()
                        S.op('act', lambda e, c=c, bkg=bkg, ti=ti: e.activation(
                            out=TF[:, ti, :], in_=ps[:, bkg, :], func=AF.Sigmoid, bias=col(('b1g', L), c)),
                            reads=[f"ps{bkg}", 'colsb'], writes=[f"TF{ti}"])
                        ub = ub8[:, c, :]
                        if u > 0:
                            S.op('dve', lambda e, c=c, ub=ub: e.tensor_copy(out=ub[:, 0:CW - 1], in_=uc[:, c, :]),
                                 reads=[f"ucar{cv}_{c}"], writes=[f"ub{c}"])
                        else:
                            S.op('dve', lambda e, ub=ub: e.memset(ub[:, 0:CW - 1], 0.0), writes=[f"ub{c}"])
                        S.op('dve', lambda e, c=c, ub=ub, bka=bka, ti=ti: e.scalar_tensor_tensor(
                            out=ub[:, CW - 1:CW - 1 + T], in0=ps[:, bka, :], scalar=col(('b1a', L), c),
                            in1=TF[:, ti, :], op0=ALU.add, op1=ALU.mult),
                            reads=[f"ps{bka}", f"TF{ti}", 'colsb'], writes=[f"ub{c}"])
                        S.op('dve', lambda e, c=c, ub=ub: e.tensor_copy(out=uc[:, c, :], in_=ub[:, T:T + CW - 1]),
                             reads=[f"ub{c}"], writes=[f"ucar{cv}_{c}"])
                    release()
                    for kk in range(2):
                        slot = acquire(('w', L, f'DG{2 * i2 + kk}'))
                        dg = ring[:, slot, :]
                        for ci in range(2):
                            c = i2 * 4 + kk * 2 + ci
                            ub = ub8[:, c, :]
                            bk = bank('A')
                            for j in range(CW):
                                m = ci * CW + j
                                S.op('pe', lambda e, j=j, m=m, bk=bk, ub=ub, dg=dg: e.matmul(
                                    ps[:, bk, :], lhsT=dg[:, m * 128:(m + 1) * 128], rhs=ub[:, j:j + T],
                                    start=(j == 0), stop=(j == CW - 1)),
                                    reads=[f"ring{slot}", f"ub{c}"], writes=[f"ps{bk}"])
                            vc = aTf[:, c * T:(c + 1) * T]
                            vtags = [f"aT{2 * c}", f"aT{2 * c + 1}"]
                            S.op('act', lambda e, c=c, bk=bk, vc=vc: e.activation(
                                out=vc, in_=ps[:, bk, :], func=AF.Identity, bias=col(('bdw', L), c)),
                                reads=[f"ps{bk}", 'colsb'], writes=vtags)
                            pi = tb_next(2)
                            S.op('act', lambda e, c=c, bk=bk, pi=pi: e.activation(
                                out=TB[:, pi, :], in_=ps[:, bk, :], func=AF.Identity, bias=col(('bdw', L), c)),
                                reads=[f"ps{bk}", 'colsb'], writes=[f"TB{pi}"])
                            S.op('act', lambda e, c=c, bk=bk, pi=pi: e.activation(
                                out=TB[:, pi + 1, :], in_=ps[:, bk, :], func=AF.Square, bias=col(('bdw', L), c)),
                                reads=[f"ps{bk}", 'colsb'], writes=[f"TB{pi + 1}"])
                            flush_stats()
                            pend.append((c, pi))
                        release()
                flush_stats()
                S.op('act', lambda e: e.activation(out=TF[:, 4, :], in_=ps[:, sbk1, :], func=AF.Copy, scale=1.0 / D),
                     reads=[f"ps{sbk1}"], writes=['TF4'])
                S.op('dve', lambda e: e.tensor_tensor(out=TF[:, 5, :], in0=TF[:, 4, :], in1=TF[:, 4, :], op=ALU.mult),
                     reads=['TF4'], writes=['TF5'])
                S.op('dve', lambda e: e.scalar_tensor_tensor(
                    out=TF[:, 5, :], in0=ps[:, sbk2, :], scalar=1.0 / D, in1=TF[:, 5, :], op0=ALU.mult, op1=ALU.subtract),
                    reads=[f"ps{sbk2}", 'TF5'], writes=['TF5'])
                S.op('act', lambda e: e.activation(out=TF[:, 5, :], in_=TF[:, 5, :], func=AF.Sqrt, bias=EPS),
                     reads=['TF5'], writes=['TF5'])
                S.op('dve', lambda e: e.reciprocal(out=TF[:, 5, :], in_=TF[:, 5, :]), reads=['TF5'], writes=['TF5'])
                for c in range(8):
                    vc = aTf[:, c * T:(c + 1) * T]
                    vtags = [f"aT{2 * c}", f"aT{2 * c + 1}"]
                    zi = tf_hi()
                    S.op('dve', lambda e, vc=vc, zi=zi: e.tensor_tensor(out=TF[:, zi, :], in0=vc, in1=TF[:, 4, :], op=ALU.subtract),
                         reads=vtags + ['TF4'], writes=[f"TF{zi}"])
                    S.op('dve', lambda e, zi=zi: e.tensor_tensor(out=TF[:, zi, :], in0=TF[:, zi, :], in1=TF[:, 5, :], op=ALU.mult),
                         reads=[f"TF{zi}", 'TF5'], writes=[f"TF{zi}"])
                    S.op('act', lambda e, c=c, zi=zi: e.activation(
                        out=hT[:, c, :], in_=TF[:, zi, :], func=AF.Silu, scale=col(('lng', L), c), bias=col(('lnb', L), c)),
                        reads=[f"TF{zi}", 'colsb'], writes=[f"hT{c}"])
                slot = acquire(('w', L, 'P2'))
                proj_residual(xb, slot, lambda kc: [f"hT{kc}"], lambda kc, tt: hT[:, kc, tt * 128:(tt + 1) * 128], 8,
                              par, lambda hf: rowsb[pr:pr + 1, ri, hf * 512:(hf + 1) * 512], pr)
                release()

            def mlp(L, b, u, xb, par):
                for s in range(4):
                    slot = acquire(('w', L, f'U{s}'))
                    w3 = ring[:, slot, :].rearrange("p (kc n) -> p kc n", kc=8)
                    for f in range(8):
                        bk = bank('A')
                        for kc in range(8):
                            S.op('pe', lambda e, f=f, kc=kc, bk=bk, w3=w3: e.matmul(
                                ps[:, bk, :], lhsT=w3[:, kc, f * 128:(f + 1) * 128], rhs=hT[:, kc, :],
                                start=(kc == 0), stop=(kc == 7)),
                                reads=[f"ring{slot}", f"hT{kc}"], writes=[f"ps{bk}"])
                        ti = tf_lo()
                        S.op('act', lambda e, bk=bk, ti=ti: e.activation(out=TF[:, ti, :], in_=ps[:, bk, :], func=AF.Relu),
                             reads=[f"ps{bk}"], writes=[f"TF{ti}"])
                        ch = s * 8 + f
                        S.op('dve', lambda e, ch=ch, ti=ti: e.tensor_tensor(
                            out=aT[:, ch, :], in0=TF[:, ti, :], in1=TF[:, ti, :], op=ALU.mult),
                            reads=[f"TF{ti}"], writes=[f"aT{ch}"])
                    release()
                for s in range(4):
                    slot = acquire(('w', L, f'D{s}'))
                    w3 = ring[:, slot, :].rearrange("p (kc n) -> p kc n", kc=32)
                    for tt in range(NT):
                        bk = bank('A')
                        for kc in range(32):
                            S.op('pe', lambda e, tt=tt, kc=kc, bk=bk, w3=w3: e.matmul(
                                ps[:, bk, 0:256], lhsT=aT[:, kc, tt * 128:(tt + 1) * 128], rhs=w3[:, kc, :],
                                start=(kc == 0), stop=(kc == 31)),
                                reads=[f"ring{slot}", f"aT{kc}"], writes=[f"ps{bk}"])
                        ti = tf_hi()
                        S.op('dve', lambda e, s=s, bk=bk, ti=ti: e.tensor_tensor(
                            out=TF[:, ti, 0:256], in0=ps[:, bk, 0:256], in1=Gb[:, par, 1, s * 256:(s + 1) * 256], op=ALU.mult),
                            reads=[f"ps{bk}", f"G{par}_1_{s // 2}"], writes=[f"TF{ti}"])
                        xv = xs[:, xb * NT + tt, s * 256:(s + 1) * 256]
                        S.op('dve', lambda e, xv=xv, ti=ti: e.tensor_tensor(out=xv, in0=xv, in1=TF[:, ti, 0:256], op=ALU.add),
                             reads=[f"TF{ti}"] + xtags(xb, tt, (s,)), writes=xtags(xb, tt, (s,)))
                    release()

            un = 0
            ul_list = [(L_, b_) for b_ in range(NSEQ) for u_ in range(NU) for L_ in range(DEPTH)]
            ul = 0
            load_gates(ul_list[0][0], ul_list[0][1], 0)
            for b in range(NSEQ):
                for u in range(NU):
                    xb = un % NXB
                    if NXB == 1:
                        if un > 0:
                            load_x(b, u, xb)
                    elif un + 1 < NSEQ * NU:
                        nb_, nu_ = divmod(un + 1, NU)
                        load_x(nb_, nu_, (un + 1) % NXB)
                    for L in range(DEPTH):
                        par = ul % 2
                        nxt = ul_list[ul + 1] if ul + 1 < len(ul_list) else None
                        defer_gates = (un == 0 and L + 1 < DEPTH)
                        if nxt is not None and not defer_gates:
                            load_gates(nxt[0], nxt[1], (ul + 1) % 2)
                        ul += 1
                        _stage('gates')
                        norm_to_hT(xb, lambda kc, L=L, b=b: A1[:, L, kc, b:b + 1],
                                   lambda kc, L=L, b=b: modT[:, L * 6 + 0, kc, b:b + 1],
                                   [f"A1_{L}", f"modT{L}_0"])
                        _stage('norm1')
                        if L % 2 == 0:
                            attention(L, b, u, xb, par)
                        else:
                            conformer(L, b, u, xb, par)
                        if defer_gates:
                            mod_layer(L + 1, ul % 2)
                            load_gates(nxt[0], nxt[1], ul % 2)
                        _stage('mixer')
                        norm_to_hT(xb, lambda kc, L=L, b=b: A2[:, L, kc, b:b + 1],
                                   lambda kc, L=L, b=b: modT[:, L * 6 + 3, kc, b:b + 1],
                                   [f"A2_{L}", f"modT{L}_3"])
                        _stage('norm2')
                        mlp(L, b, u, xb, par)
                        _stage('mlp')
                    for tt in range(NT):
                        xrow = xs[:, xb * NT + tt, :]
                        S.op('act', lambda e, tt=tt, xrow=xrow: e.activation(
                            out=ntm[:, tt, :], in_=xrow, func=AF.Square, accum_out=ss[:, tt:tt + 1]),
                            reads=xtags(xb, tt), writes=[f"ntm{tt}", f"ss{tt}"])
                        S.op('act', lambda e, tt=tt: e.activation(
                            out=sd[:, tt:tt + 1], in_=ss[:, tt:tt + 1], func=AF.Sqrt, scale=1.0 / D, bias=EPS),
                            reads=[f"ss{tt}"], writes=[f"sd{tt}"])
                        S.op('dve', lambda e, tt=tt: e.reciprocal(out=rs[:, tt:tt + 1], in_=sd[:, tt:tt + 1]),
                             reads=[f"sd{tt}"], writes=[f"rs{tt}"])
                        S.op('dve', lambda e, tt=tt, xrow=xrow: e.scalar_tensor_tensor(
                            out=ntm[:, tt, :], in0=xrow, scalar=rs[:, tt:tt + 1], in1=FN[:], op0=ALU.mult, op1=ALU.mult),
                            reads=xtags(xb, tt) + [f"rs{tt}", 'FN0', 'FN1'], writes=[f"ntm{tt}"])
                        ot = f"out{un}_{tt}"
                        out_tags.append(ot)
                        S.op('pool', lambda e, tt=tt, b=b, u=u: e.dma_start(
                            out=out_d[b, u * T + tt * 128:u * T + (tt + 1) * 128, :], in_=ntm[:, tt, :]),
                            reads=[f"ntm{tt}"], writes=[ot], dma=True)
                    un += 1

        stopped = False
        try:
            _body()
        except StopBuild:
            stopped = True
        assert stopped or wstate['cur'] == len(seq), (wstate, len(seq))
        S.op('pool', lambda e: e.nop(), reads=out_tags)
        S.emit()
    return nc, S


_CACHE = {}


def kernel(**inputs):
    NCORES = 8
    inp = {k: np.asarray(v) for k, v in inputs.items()}
    B, SEQ, _ = inp['x'].shape
    DEPTH = inp['w_mod'].shape[0]
    NSEQ = B // NCORES
    key = (NSEQ, SEQ, DEPTH)
    if key not in _CACHE:
        _CACHE[key] = build_program(NSEQ, SEQ, DEPTH)[0]
    nc = _CACHE[key]
    cols, rows = prep_shared(inp, DEPTH)
    dm, ident = host_constants()
    f32c = lambda a: np.ascontiguousarray(np.asarray(a, dtype=np.float32))
    shared = dict(cols=cols, rows=rows.reshape(128, 2 * 1152), dm=dm, ident=ident,
                  w_mod=f32c(inp['w_mod']), w_qkv=f32c(inp['w_qkv']), w_o=f32c(inp['w_o']),
                  w_pw1=f32c(inp['w_pw1']), w_pw2=f32c(inp['w_pw2']),
                  w_up=f32c(inp['w_up']), w_down=f32c(inp['w_down']))
    x = f32c(inp['x'])
    c = f32c(inp['c'])
    in_maps = []
    for ci in range(NCORES):
        cc = c[ci * NSEQ:(ci + 1) * NSEQ]
        cT = np.ascontiguousarray(cc.reshape(NSEQ, 8, 128).transpose(2, 1, 0)).reshape(128, 8 * NSEQ)
        m = dict(shared)
        m['x'] = x[ci * NSEQ:(ci + 1) * NSEQ]
        m['cT'] = cT
        in_maps.append(m)
    res = run_bass_kernel_spmd(nc, in_maps, core_ids=list(range(NCORES)))
    out = np.concatenate([np.asarray(r['out']) for r in res.results], axis=0)
    return out.astype(np.float32, copy=False)
```

```python
import contextlib
import numpy as np
import concourse.bass as bass
import concourse.mybir as mybir
from concourse.bass_utils import run_bass_kernel_spmd

F32 = mybir.dt.float32
BF16 = mybir.dt.bfloat16
AF = mybir.ActivationFunctionType
ALU = mybir.AluOpType

ENGS = ['pe', 'act', 'dve', 'pool', 'sp']
CENGS = ['pe', 'act', 'dve', 'pool']
SEM_W = 4000
DMA_RING = 8

D = 1024
NH = 16
HD = 64
NMOD = 6
DFF = 4096
CW = 31
EPS = 1e-6
T = 512
NT = 4
NSLOT = 3
SLAB = 8192
NEG = -1.0e6
OPROJ_INTERLEAVE = False


class StopBuild(Exception):
    pass


def _stage(name):
    PHASE[0] = 'after_' + name


PHASE = ['init']


class Sched:
    def __init__(self, nc):
        self.nc = nc
        self.ops = {e: [] for e in ENGS}
        self.tags = {}
        self.known = {e: {f: -1 for f in CENGS} for e in ENGS}
        self.known_dma = {e: set() for e in ENGS}
        self.ndma = {}

    def op(self, eng, fn, reads=(), writes=(), dma=False):
        idx = len(self.ops[eng])
        me = (eng, idx)
        raw, oth = set(), set()
        for t in reads:
            st = self.tags.get(t)
            if st is not None and st[0] is not None:
                raw.add(st[0])
        for t in writes:
            st = self.tags.get(t)
            if st is not None:
                if st[0] is not None:
                    oth.add(st[0])
                oth.update(st[1])
        deps = set(raw)
        for d in oth:
            if d[0] == eng and eng == 'pe' and not dma:
                continue
            deps.add(d)
        deps.discard(me)
        waits = []
        kn = self.known[eng]
        kd = self.known_dma[eng]
        for d in sorted(deps, key=lambda z: -z[1]):
            f, j = d
            o = self.ops[f][j]
            if o['dma']:
                if d in kd:
                    continue
                kd.add(d)
            else:
                if kn[f] >= j:
                    continue
                kn[f] = j
            waits.append(d)
            o['needs_inc'] = True
            for g, v in o['known'].items():
                if v > kn[g]:
                    kn[g] = v
        rec = dict(fn=fn, waits=waits, needs_inc=False, dma=dma, known=dict(kn), dma_k=None, phase=PHASE[0])
        if dma:
            rec['dma_k'] = self.ndma.get(eng, 0)
            self.ndma[eng] = rec['dma_k'] + 1
        self.ops[eng].append(rec)
        for t in reads:
            st = self.tags.setdefault(t, [None, []])
            st[1].append(me)
        for t in writes:
            self.tags[t] = [me, []]
        return me

    def emit(self):
        nc = self.nc
        with contextlib.ExitStack() as es:
            for e in CENGS:
                n_inc = sum(1 for o in self.ops[e] if o['needs_inc'] and not o['dma'])
                nsem = max(1, (n_inc + SEM_W - 1) // SEM_W)
                sems = [es.enter_context(nc.semaphore(f"s_{e}_{i}")) for i in range(nsem)]
                c = 0
                for o in self.ops[e]:
                    if o['needs_inc'] and not o['dma']:
                        o['sem'] = (sems[c // SEM_W], c % SEM_W + 1)
                        c += 1
            dma_by_k = {}
            for e in ENGS:
                dl = [o for o in self.ops[e] if o['dma']]
                if not dl:
                    continue
                dsems = [es.enter_context(nc.semaphore(f"s_dma_{e}_{i}")) for i in range(DMA_RING)]
                for o in dl:
                    k = o['dma_k']
                    o['sem'] = (dsems[k % DMA_RING], 16 * (k // DMA_RING + 1))
                    dma_by_k[(e, k)] = o
            block = es.enter_context(nc.Block())

            def run(eng_name):
                def body(eng):
                    for o in self.ops[eng_name]:
                        for (f, j) in o['waits']:
                            s, v = self.ops[f][j]['sem']
                            eng.wait_ge(s, v)
                        if o['dma']:
                            k = o['dma_k']
                            if k >= DMA_RING:
                                s, v = dma_by_k[(eng_name, k - DMA_RING)]['sem']
                                eng.wait_ge(s, v)
                            ins = o['fn'](eng)
                            ins.then_inc(o['sem'][0], 16)
                        else:
                            ins = o['fn'](eng)
                            if o['needs_inc']:
                                ins.then_inc(o['sem'][0], 1)
                return body

            block.tensor(run('pe'))
            block.scalar(run('act'))
            block.vector(run('dve'))
            block.gpsimd(run('pool'))
            block.sync(run('sp'))

    def stats(self):
        return {e: (len(self.ops[e]), sum(len(o['waits']) for o in self.ops[e])) for e in ENGS}


def col_layout(depth):
    lay = {}
    n = 0

    def add(name, w):
        nonlocal n
        lay[name] = n
        n += w

    for L in range(depth):
        add(('nmix', L), 8)
        add(('nmlp', L), 8)
        add(('bmod', L), 48)
        if L % 2 == 0:
            add(('bq', L), 8)
            add(('bk2', L), 2)
            add(('sink', L), 8)
        else:
            add(('b1a', L), 8)
            add(('b1g', L), 8)
            add(('wdw', L), CW * 8)
            add(('bdw', L), 8)
            add(('lng', L), 8)
            add(('lnb', L), 8)
    add(('fn',), 8)
    return lay, n


def colsT(v):
    v = np.asarray(v, dtype=np.float32)
    return np.ascontiguousarray(v.reshape(-1, 128).T)


def alibi_slopes():
    return [float(np.float32(2.0 ** (-8.0 * (h + 1) / NH))) for h in range(NH)]


def host_constants():
    s = np.arange(128)[:, None]
    t = np.arange(128)[None, :]
    prev = np.where(t < s, -(128 + t - s), NEG).astype(np.float32)
    cur = np.where(t >= s, -(t - s), NEG).astype(np.float32)
    dm = np.concatenate([prev, cur], axis=1).astype(np.float32)
    ident = np.eye(128, dtype=np.float32)
    return dm, ident


def prep_shared(inp, depth):
    lay, ncol = col_layout(depth)
    cols = np.zeros((128, ncol), np.float32)
    rows = np.zeros((128, 2, 1152), np.float32)
    for L in range(depth):
        j = L // 2
        cols[:, lay[('nmix', L)]:lay[('nmix', L)] + 8] = colsT(inp['norm_mix'][L])
        cols[:, lay[('nmlp', L)]:lay[('nmlp', L)] + 8] = colsT(inp['norm_mlp'][L])
        cols[:, lay[('bmod', L)]:lay[('bmod', L)] + 48] = colsT(inp['b_mod'][L])
        pr, ri = 32 * (L % 3), L // 3
        if L % 2 == 0:
            bqkv = np.asarray(inp['b_qkv'][j], np.float32)
            cols[:, lay[('bq', L)]:lay[('bq', L)] + 8] = colsT(bqkv[:1024])
            k0, k1 = bqkv[1024:1088], bqkv[1088:1152]
            cols[:, lay[('bk2', L)]:lay[('bk2', L)] + 2] = colsT(np.concatenate([k0, k0, k1, k1]))
            sk = np.asarray(inp['sinks'][j], np.float32)
            cols[:, lay[('sink', L)]:lay[('sink', L)] + 8] = colsT(np.repeat(sk, 64))
            rows[pr, ri, 0:128] = bqkv[1152:1280]
            rows[pr, ri, 128:1152] = np.asarray(inp['b_o'][j], np.float32)
        else:
            b1 = np.asarray(inp['b_pw1'][j], np.float32)
            cols[:, lay[('b1a', L)]:lay[('b1a', L)] + 8] = colsT(b1[:1024])
            cols[:, lay[('b1g', L)]:lay[('b1g', L)] + 8] = colsT(b1[1024:])
            wd = np.asarray(inp['w_dw'][j], np.float32)
            cols[:, lay[('wdw', L)]:lay[('wdw', L)] + CW * 8] = colsT(wd.reshape(-1))
            cols[:, lay[('bdw', L)]:lay[('bdw', L)] + 8] = colsT(inp['b_dw'][j])
            cols[:, lay[('lng', L)]:lay[('lng', L)] + 8] = colsT(inp['conv_ln_g'][j])
            cols[:, lay[('lnb', L)]:lay[('lnb', L)] + 8] = colsT(inp['conv_ln_b'][j])
            rows[pr, ri, 0:1024] = np.asarray(inp['b_pw2'][j], np.float32)
    cols[:, lay[('fn',)]:lay[('fn',)] + 8] = colsT(inp['final_norm'])
    return cols, rows


def build_program(NSEQ=4, SEQ=2048, DEPTH=4, NXB=1):
    NU = SEQ // T
    NA = (DEPTH + 1) // 2
    NCV = DEPTH // 2
    lay, NCOL = col_layout(DEPTH)
    slopes = alibi_slopes()
    nc = bass.Bass("TRN2", target_bir_lowering=False)

    def din(name, shape, dt=F32):
        return nc.dram_tensor(name, shape, dt, kind="ExternalInput").ap()

    x_d = din("x", [NSEQ, SEQ, D])
    cT_d = din("cT", [128, 8 * NSEQ])
    cols_d = din("cols", [128, NCOL])
    rows_d = din("rows", [128, 2 * 1152])
    dm_d = din("dm", [128, 256])
    ident_d = din("ident", [128, 128])
    w_mod_d = din("w_mod", [DEPTH, D, NMOD * D])
    w_qkv_d = din("w_qkv", [NA, D, 1280])
    w_o_d = din("w_o", [NA, D, D])
    w_pw1_d = din("w_pw1", [max(NCV, 1), D, 2 * D])
    w_pw2_d = din("w_pw2", [max(NCV, 1), D, D])
    w_up_d = din("w_up", [DEPTH, D, DFF])
    w_down_d = din("w_down", [DEPTH, DFF, D])
    out_d = nc.dram_tensor("out", [NSEQ, SEQ, D], F32, kind="ExternalOutput").ap()
    def layer_slabs(L):
        if L % 2 == 0:
            names = ['Q', 'KV', 'O']
        else:
            names = ['P10', 'DG0', 'DG1', 'P11', 'DG2', 'DG3', 'P2']
        return names + [f'U{s_}' for s_ in range(4)] + [f'D{s_}' for s_ in range(4)]

    SID = {}
    for L_ in range(DEPTH):
        for nm_ in layer_slabs(L_):
            SID[(L_, nm_)] = len(SID)
    for L_ in range(DEPTH):
        for s_ in range(6):
            SID[(L_, f'M{s_}')] = len(SID)
    wsc = nc.dram_tensor("wsc", [len(SID), 128, SLAB], BF16, kind="Internal").ap()
    gsc = nc.dram_tensor("gsc", [DEPTH * NSEQ * 2, 128, D], F32, kind="Internal").ap()

    S = Sched(nc)
    with contextlib.ExitStack() as es:
        def sb(name, shape, dt):
            return es.enter_context(nc.sbuf_tensor(name, shape, dt))

        xs = sb("xs", [128, NXB * NT, D], F32)
        hT = sb("hT", [128, 8, T], BF16)
        aT = sb("aT", [128, 32, T], BF16)
        aTf = aT[:].rearrange("p a t -> p (a t)").bitcast(F32)
        ntm = sb("ntm", [128, NT, D], F32)
        ring = sb("ring", [128, NSLOT, SLAB], BF16)
        Gb = sb("Gb", [128, 2, 2, D], F32)
        FN = sb("FN", [128, D], F32)
        qT = sb("qT", [128, 8, T], BF16)
        kT = [sb(f"kT{a}", [128, 2, T + 128], BF16) for a in range(NA)]
        Vt = [sb(f"V{a}", [128, NT + 1, 128], BF16) for a in range(NA)]
        DM = sb("DM", [128, 256], F32)
        TF = sb("TF", [128, 6, T], F32)
        TB = sb("TB", [128, 4, T], BF16)
        ub8 = sb("ub8", [128, 8, T + 32], BF16)
        ucar = [sb(f"ucar{a}", [128, 8, CW - 1], BF16) for a in range(max(NCV, 1))]
        identb = sb("identb", [128, 128], BF16)
        colsb = sb("colsb", [128, NCOL], F32)
        rowsb = sb("rowsb", [128, 2, 1152], BF16)
        identf = sb("identf", [128, 128], F32)
        onesf = sb("onesf", [128, 128], F32)
        onesb = sb("onesb", [128, 128], BF16)
        rep4 = TF[:, 4:6, :]
        cTf = sb("cTf", [128, 8 * NSEQ], F32)
        csT = sb("csT", [128, 8, NSEQ], BF16)
        modT = sb("modT", [128, DEPTH * 6, 8, NSEQ], F32)
        A1 = sb("A1", [128, DEPTH, 8, NSEQ], F32)
        A2 = sb("A2", [128, DEPTH, 8, NSEQ], F32)
        bq8 = sb("bq8", [128, NA, 8], F32)
        esink = sb("esink", [128, NA, 8], F32)
        ss = sb("ss", [128, 8], F32)
        sd = sb("sd", [128, 8], F32)
        rs = sb("rs", [128, 8], F32)
        ps = es.enter_context(nc.psum_tensor("ps", [128, 8, 512], F32))

        rot = {'A': [0, 1, 2, 3], 'B': [4, 5], 'C': [6, 7]}
        rpos = {'A': 0, 'B': 0, 'C': 0}

        def bank(cls):
            b = rot[cls][rpos[cls] % len(rot[cls])]
            rpos[cls] += 1
            return b

        tfpos = {'lo': 0, 'hi': 0}

        def tf_lo():
            i = tfpos['lo'] % 2
            tfpos['lo'] += 1
            return i

        def tf_hi():
            i = 2 + tfpos['hi'] % 2
            tfpos['hi'] += 1
            return i

        tbpos = [0]

        def tb_next(n=1):
            i = tbpos[0] % (4 // n)
            tbpos[0] += 1
            return i * n

        def col(name, j=0):
            c0 = lay[name] + j
            return colsb[:, c0:c0 + 1]

        def slab_id(L, k):
            return SID[(L, k)]

        seq = []
        for s in range(6):
            seq.append(('wmod', 0, s))
        for un in range(NSEQ * NU):
            for L in range(DEPTH):
                for k in layer_slabs(L):
                    if un == 0 and k == 'U0' and L + 1 < DEPTH:
                        for s in range(6):
                            seq.append(('wmod', L + 1, s))
                    seq.append(('w', L, k))
        wstate = {'loaded': 0, 'cur': 0}

        def slab_len(L, k):
            if k == 'KV':
                return 8 * 384
            if k.startswith('DG'):
                return 62 * 128
            return SLAB

        def slab_tags(L, k):
            sid = slab_id(L, k)
            if k == 'KV':
                return [f"wsc{sid}_{d}" for d in (0, 64, 128, 192, 256)]
            if k in ('P10', 'P11'):
                return [f"wsc{sid}_a", f"wsc{sid}_g"]
            return [f"wsc{sid}"]

        def record_load(i):
            ent = seq[i]
            slot = i % NSLOT
            if ent[0] == 'wmod':
                _, L, s = ent
                sid = slab_id(L, f'M{s}')
                S.op('sp', lambda e: e.dma_start(out=ring[:, slot, :], in_=wsc[sid]),
                     reads=[f"wsc{sid}"], writes=[f"ring{slot}"], dma=True)
            else:
                _, L, k = ent
                n = slab_len(L, k)
                sid = slab_id(L, k)
                S.op('sp', lambda e: e.dma_start(out=ring[:, slot, 0:n], in_=wsc[sid][:, 0:n]),
                     reads=slab_tags(L, k), writes=[f"ring{slot}"], dma=True)

        def prefetch():
            while wstate['loaded'] < len(seq) and wstate['loaded'] < wstate['cur'] + NSLOT:
                record_load(wstate['loaded'])
                wstate['loaded'] += 1

        def acquire(expect):
            i = wstate['cur']
            assert seq[i] == expect, (seq[i], expect)
            prefetch()
            return i % NSLOT

        def release():
            wstate['cur'] += 1
            prefetch()

        def load_x(b, u, xb):
            for tt in range(NT):
                S.op('act', lambda e, tt=tt: e.dma_start(
                    out=xs[:, xb * NT + tt, :], in_=x_d[b, u * T + tt * 128:u * T + (tt + 1) * 128, :]),
                    writes=[f"x{xb}_{tt}_{q}" for q in range(4)], dma=True)

        out_tags = []

        def _body():
            load_x(0, 0, 0)
            S.op('sp', lambda e: e.dma_start(out=colsb[:], in_=cols_d), writes=['colsb'], dma=True)
            S.op('sp', lambda e: e.dma_start(out=identf[:], in_=ident_d), writes=['identf'], dma=True)
            S.op('sp', lambda e: e.dma_start(out=DM[:], in_=dm_d), writes=['DM'], dma=True)
            S.op('sp', lambda e: e.dma_start(out=cTf[:], in_=cT_d), writes=['cTf'], dma=True)
            S.op('pool', lambda e: e.dma_start(out=rowsb[:].rearrange("p a n -> p (a n)"), in_=rows_d),
                 writes=['rowsb'], dma=True)
            _stage('dma0')

            def conv_slab(sid, src3, n):
                dst = wsc[sid][:, 0:8 * n].rearrange("p (kc n) -> p kc n", kc=8)
                S.op('pool', lambda e: e.dma_start(out=dst, in_=src3), writes=[f"wsc{sid}"], dma=True)

            def r3(ap2):
                return ap2.rearrange("(kc p) n -> p kc n", p=128)

            def convert_layer(L):
                j = L // 2
                if L % 2 == 0:
                    conv_slab(slab_id(L, 'Q'), r3(w_qkv_d[j][:, 0:1024]), 1024)
                    sid = slab_id(L, 'KV')
                    dst3 = wsc[sid][:, 0:8 * 384].rearrange("p (kc n) -> p kc n", kc=8)
                    pieces = [(0, 1024), (64, 1024), (128, 1088), (192, 1088)]
                    tags = [f"wsc{sid}"]
                    for (d0, s0) in pieces:
                        S.op('pool', lambda e, d0=d0, s0=s0: e.dma_start(
                            out=dst3[:, :, d0:d0 + 64], in_=r3(w_qkv_d[j][:, s0:s0 + 64])),
                            writes=[f"wsc{sid}_{d0}"], dma=True)
                    S.op('pool', lambda e: e.dma_start(out=dst3[:, :, 256:384], in_=r3(w_qkv_d[j][:, 1152:1280])),
                         writes=[f"wsc{sid}_256"], dma=True)
                    conv_slab(slab_id(L, 'O'), r3(w_o_d[j]), 1024)
                else:
                    for i in range(2):
                        sid = slab_id(L, f'P1{i}')
                        dst3 = wsc[sid].rearrange("p (kc n) -> p kc n", kc=8)
                        S.op('pool', lambda e, i=i, dst3=dst3: e.dma_start(
                            out=dst3[:, :, 0:512], in_=r3(w_pw1_d[j][:, i * 512:(i + 1) * 512])),
                            writes=[f"wsc{sid}_a"], dma=True)
                        S.op('pool', lambda e, i=i, dst3=dst3: e.dma_start(
                            out=dst3[:, :, 512:1024], in_=r3(w_pw1_d[j][:, 1024 + i * 512:1024 + (i + 1) * 512])),
                            writes=[f"wsc{sid}_g"], dma=True)
                    conv_slab(slab_id(L, 'P2'), r3(w_pw2_d[j]), 1024)
                for s in range(4):
                    conv_slab(slab_id(L, f'U{s}'), r3(w_up_d[L][:, s * 1024:(s + 1) * 1024]), 1024)
                for s in range(4):
                    sid = slab_id(L, f'D{s}')
                    dst = wsc[sid].rearrange("p (kc n) -> p kc n", kc=32)
                    src = w_down_d[L][:, s * 256:(s + 1) * 256].rearrange("(kc p) n -> p kc n", p=128)
                    S.op('pool', lambda e, dst=dst, src=src: e.dma_start(out=dst, in_=src),
                         writes=[f"wsc{sid}"], dma=True)

            def convert_wmod(L):
                for s_ in range(6):
                    conv_slab(slab_id(L, f'M{s_}'), r3(w_mod_d[L][:, s_ * 1024:(s_ + 1) * 1024]), 1024)

            convert_wmod(0)
            S.op('act', lambda e: e.copy(out=identb[:], in_=identf[:]), reads=['identf'], writes=['identb'])
            aTflat = aT[:].rearrange("p a t -> p (a t)")

            def build_diag(L):
                cw0 = lay[('wdw', L)]
                for k in range(4):
                    hb = k % 2
                    stg = aTflat[:, hb * 8192:(hb + 1) * 8192]
                    stags = [f"aT{hb * 16 + q}" for q in range(16)]
                    for ci in range(2):
                        c = 2 * k + ci
                        wcols = colsb[:, cw0 + c:cw0 + c + 8 * (CW - 1) + 1:8]
                        S.op('dve', lambda e, ci=ci, stg=stg, wcols=wcols: e.tensor_tensor(
                            out=stg[:, ci * CW * 128:(ci + 1) * CW * 128].rearrange("p (j n) -> p j n", j=CW),
                            in0=identb[:].unsqueeze(1).to_broadcast([128, CW, 128]),
                            in1=wcols.unsqueeze(2).to_broadcast([128, CW, 128]), op=ALU.mult),
                            reads=['identb', 'colsb'], writes=stags)
                    sid = slab_id(L, f'DG{k}')
                    S.op('pool', lambda e, sid=sid, stg=stg: e.dma_start(out=wsc[sid][:, 0:62 * 128], in_=stg[:, 0:62 * 128]),
                         reads=stags, writes=[f"wsc{sid}"], dma=True)

            for L_ in range(1, DEPTH, 2):
                build_diag(L_)

            convert_layer(0)
            for L_ in range(1, DEPTH):
                convert_wmod(L_)
                convert_layer(L_)
            _stage('conv0')

            S.op('dve', lambda e: e.memset(onesf[:], 1.0), writes=['onesf'])
            S.op('dve', lambda e: e.memset(onesb[:], 1.0), writes=['onesb'])
            S.op('act', lambda e: e.activation(out=csT[:].rearrange("p k b -> p (k b)"), in_=cTf[:], func=AF.Silu),
                 reads=['cTf'], writes=['csT'])
            for a in range(NA):
                L = 2 * a
                c0 = lay[('bq', L)]
                S.op('act', lambda e, a=a, c0=c0: e.mul(out=bq8[:, a, :], in_=colsb[:, c0:c0 + 8], mul=0.125),
                     reads=['colsb'], writes=['bq8'])
                c1 = lay[('sink', L)]
                S.op('act', lambda e, a=a, c1=c1: e.activation(out=esink[:, a, :], in_=colsb[:, c1:c1 + 8], func=AF.Exp),
                     reads=['colsb'], writes=['esink'])
            _stage('small')

            def bcast_tile(dst, dst_tag, colfn4, col_tags):
                for hf in range(2):
                    bk = bank('B')
                    ri = hf
                    S.op('dve', lambda e, hf=hf, ri=ri: e.tensor_tensor(
                        out=rep4[:, ri, :].rearrange("p (j n) -> p j n", j=4),
                        in0=identf[:].unsqueeze(1).to_broadcast([128, 4, 128]),
                        in1=colfn4(hf).unsqueeze(2).to_broadcast([128, 4, 128]), op=ALU.mult),
                        reads=['identf'] + col_tags, writes=[f"TF{4 + ri}"])
                    S.op('pe', lambda e, ri=ri, bk=bk: e.matmul(
                        ps[:, bk, :], lhsT=onesf[:], rhs=rep4[:, ri, :], start=True, stop=True),
                        reads=[f"TF{4 + ri}", 'onesf'], writes=[f"ps{bk}"])
                    S.op('act', lambda e, hf=hf, bk=bk: e.copy(out=dst[:, hf * 512:(hf + 1) * 512], in_=ps[:, bk, :]),
                         reads=[f"ps{bk}"], writes=[f"{dst_tag}{hf}"])

            gcount = [0]

            def mod_layer(L, spar):
                gstage = [(Gb[:, spar, 0, :], f'G{spar}_0_'), (Gb[:, spar, 1, :], f'G{spar}_1_')]
                for s in range(6):
                    slot = acquire(('wmod', L, s))
                    w3 = ring[:, slot, :].rearrange("p (kc n) -> p kc n", kc=8)
                    mi = (L * 6 + s) % 2
                    mrow = TF[0:NSEQ, 2 * mi:2 * mi + 2, :].rearrange("p a t -> p (a t)")
                    mtags = [f"TF{2 * mi}", f"TF{2 * mi + 1}"]
                    for ch in range(2):
                        bka = bank('A')
                        for kc in range(8):
                            S.op('pe', lambda e, ch=ch, kc=kc, w3=w3, bka=bka: e.matmul(
                                ps[0:NSEQ, bka, :], lhsT=csT[:, kc, :], rhs=w3[:, kc, ch * 512:(ch + 1) * 512],
                                start=(kc == 0), stop=(kc == 7)),
                                reads=[f"ring{slot}", 'csT'], writes=[f"ps{bka}"])
                        S.op('act', lambda e, ch=ch, bka=bka, mrow=mrow: e.copy(
                            out=mrow[:, ch * 512:(ch + 1) * 512], in_=ps[0:NSEQ, bka, :]),
                            reads=[f"ps{bka}"], writes=[mtags[ch]])
                    bk = bank('B')
                    for j in range(8):
                        S.op('pe', lambda e, j=j, bk=bk, mrow=mrow: e.transpose(
                            ps[:, bk, j * NSEQ:(j + 1) * NSEQ], mrow[:, j * 128:(j + 1) * 128], identf[0:NSEQ, 0:NSEQ]),
                            reads=mtags + ['identf'], writes=[f"ps{bk}"])
                    c0 = lay[('bmod', L)] + s * 8
                    S.op('dve', lambda e, L=L, s=s, bk=bk, c0=c0: e.tensor_tensor(
                        out=modT[:, L * 6 + s, :, :],
                        in0=ps[:, bk, 0:8 * NSEQ].rearrange("p (j b) -> p j b", j=8),
                        in1=colsb[:, c0:c0 + 8].unsqueeze(2).to_broadcast([128, 8, NSEQ]), op=ALU.add),
                        reads=[f"ps{bk}", 'colsb'], writes=[f"modT{L}_{s}"])
                    release()
                    if s % 2 == 1 and (L * 3 + s // 2 + 1) < DEPTH:
                        pass
                cm = lay[('nmix', L)]
                S.op('dve', lambda e, L=L, cm=cm: e.scalar_tensor_tensor(
                    out=A1[:, L, :, :], in0=modT[:, L * 6 + 1, :, :], scalar=1.0,
                    in1=colsb[:, cm:cm + 8].unsqueeze(2).to_broadcast([128, 8, NSEQ]), op0=ALU.add, op1=ALU.mult),
                    reads=[f"modT{L}_1", 'colsb'], writes=[f"A1_{L}"])
                cm2 = lay[('nmlp', L)]
                S.op('dve', lambda e, L=L, cm2=cm2: e.scalar_tensor_tensor(
                    out=A2[:, L, :, :], in0=modT[:, L * 6 + 4, :, :], scalar=1.0,
                    in1=colsb[:, cm2:cm2 + 8].unsqueeze(2).to_broadcast([128, 8, NSEQ]), op0=ALU.add, op1=ALU.mult),
                    reads=[f"modT{L}_4", 'colsb'], writes=[f"A2_{L}"])
                for b_ in range(NSEQ):
                    for wh, sec in ((0, 2), (1, 5)):
                        stg, stag = gstage[gcount[0] % len(gstage)]
                        gcount[0] += 1
                        bcast_tile(stg, stag, lambda hf, L=L, b_=b_, sec=sec: modT[:, L * 6 + sec, hf * 4:(hf + 1) * 4, b_],
                                   [f"modT{L}_{sec}"])
                        gi = (L * NSEQ + b_) * 2 + wh
                        S.op('act', lambda e, gi=gi, stg=stg: e.dma_start(out=gsc[gi], in_=stg),
                             reads=[f"{stag}0", f"{stag}1"], writes=[f"gsc{gi}"], dma=True)
            mod_layer(0, 1)
            _stage('mod')
            cfn = lay[('fn',)]
            bcast_tile(FN, 'FN', lambda hf: colsb[:, cfn + hf * 4:cfn + (hf + 1) * 4], ['colsb'])

            def load_gates(L, b, par):
                for wh in range(2):
                    gi = (L * NSEQ + b) * 2 + wh
                    S.op('act', lambda e, gi=gi, wh=wh: e.dma_start(out=Gb[:, par, wh, :], in_=gsc[gi]),
                         reads=[f"gsc{gi}"], writes=[f"G{par}_{wh}_0", f"G{par}_{wh}_1"], dma=True)
            _stage('fn')

            def xtags(xb, tt, qs=range(4)):
                return [f"x{xb}_{tt}_{q}" for q in qs]

            def norm_to_hT(xb, Acol, Scol, col_tags):
                for tt in range(NT):
                    xrow = xs[:, xb * NT + tt, :]
                    S.op('act', lambda e, tt=tt, xrow=xrow: e.activation(
                        out=ntm[:, tt, :], in_=xrow, func=AF.Square, accum_out=ss[:, tt:tt + 1]),
                        reads=xtags(xb, tt), writes=[f"ntm{tt}", f"ss{tt}"])
                    S.op('act', lambda e, tt=tt: e.activation(
                        out=sd[:, tt:tt + 1], in_=ss[:, tt:tt + 1], func=AF.Sqrt, scale=1.0 / D, bias=EPS),
                        reads=[f"ss{tt}"], writes=[f"sd{tt}"])
                    S.op('dve', lambda e, tt=tt: e.reciprocal(out=rs[:, tt:tt + 1], in_=sd[:, tt:tt + 1]),
                         reads=[f"sd{tt}"], writes=[f"rs{tt}"])
                    S.op('dve', lambda e, tt=tt, xrow=xrow: e.tensor_scalar(
                        out=ntm[:, tt, :], in0=xrow, scalar1=rs[:, tt:tt + 1], scalar2=None, op0=ALU.mult),
                        reads=xtags(xb, tt) + [f"rs{tt}"], writes=[f"ntm{tt}"])
                for kc in range(8):
                    bk = bank('B')
                    for tt in range(NT):
                        S.op('pe', lambda e, kc=kc, tt=tt, bk=bk: e.transpose(
                            ps[:, bk, tt * 128:(tt + 1) * 128], ntm[:, tt, kc * 128:(kc + 1) * 128], identf[:]),
                            reads=[f"ntm{tt}", 'identf'], writes=[f"ps{bk}"])
                    if kc % 2 == 0:
                        S.op('dve', lambda e, kc=kc, bk=bk: e.tensor_scalar(
                            out=hT[:, kc, :], in0=ps[:, bk, :], scalar1=Acol(kc), scalar2=Scol(kc),
                            op0=ALU.mult, op1=ALU.add),
                            reads=[f"ps{bk}"] + col_tags, writes=[f"hT{kc}"])
                    else:
                        S.op('act', lambda e, kc=kc, bk=bk: e.activation(
                            out=hT[:, kc, :], in_=ps[:, bk, :], func=AF.Identity, scale=Acol(kc), bias=Scol(kc)),
                            reads=[f"ps{bk}"] + col_tags, writes=[f"hT{kc}"])

            def proj_residual(xb, slot, src_tags_fn, lhs_fn, nk, par, bias_row, pr, tts=range(NT)):
                w3 = ring[:, slot, :].rearrange("p (kc n) -> p kc n", kc=nk)
                for tt in tts:
                    for hf in range(2):
                        bk = bank('A')
                        for kc in range(nk):
                            S.op('pe', lambda e, kc=kc, tt=tt, hf=hf, bk=bk: e.matmul(
                                ps[:, bk, :], lhsT=lhs_fn(kc, tt), rhs=w3[:, kc, hf * 512:(hf + 1) * 512],
                                start=(kc == 0), stop=False),
                                reads=[f"ring{slot}"] + src_tags_fn(kc), writes=[f"ps{bk}"])
                        S.op('pe', lambda e, hf=hf, bk=bk: e.matmul(
                            ps[:, bk, :], lhsT=onesb[pr:pr + 1, :], rhs=bias_row(hf), start=False, stop=True),
                            reads=['onesb', 'rowsb'], writes=[f"ps{bk}"])
                        ti = tf_hi()
                        S.op('dve', lambda e, hf=hf, bk=bk, ti=ti: e.tensor_tensor(
                            out=TF[:, ti, :], in0=ps[:, bk, :], in1=Gb[:, par, 0, hf * 512:(hf + 1) * 512], op=ALU.mult),
                            reads=[f"ps{bk}", f"G{par}_0_{hf}"], writes=[f"TF{ti}"])
                        xv = xs[:, xb * NT + tt, hf * 512:(hf + 1) * 512]
                        S.op('dve', lambda e, xv=xv, ti=ti: e.tensor_tensor(out=xv, in0=xv, in1=TF[:, ti, :], op=ALU.add),
                             reads=[f"TF{ti}"] + xtags(xb, tt, (2 * hf, 2 * hf + 1)),
                             writes=xtags(xb, tt, (2 * hf, 2 * hf + 1)))

            def attention(L, b, u, xb, par):
                a = L // 2
                pr, ri = 32 * (L % 3), L // 3
                kTa, Va = kT[a], Vt[a]
                if u > 0:
                    S.op('act', lambda e: e.copy(out=kTa[:, :, 0:128], in_=kTa[:, :, T:T + 128]),
                         reads=[f"kT{a}_3"], writes=[f"kT{a}_c"])
                    S.op('act', lambda e: e.copy(out=Va[:, 0, :], in_=Va[:, NT, :]),
                         reads=[f"V{a}_3"], writes=[f"V{a}_c"])
                slot = acquire(('w', L, 'Q'))
                w3 = ring[:, slot, :].rearrange("p (kc n) -> p kc n", kc=8)
                for c in range(8):
                    bk = bank('A')
                    for kc in range(8):
                        S.op('pe', lambda e, c=c, kc=kc, bk=bk, w3=w3: e.matmul(
                            ps[:, bk, :], lhsT=w3[:, kc, c * 128:(c + 1) * 128], rhs=hT[:, kc, :],
                            start=(kc == 0), stop=(kc == 7)),
                            reads=[f"ring{slot}", f"hT{kc}"], writes=[f"ps{bk}"])
                    S.op('act', lambda e, c=c, bk=bk: e.activation(
                        out=qT[:, c, :], in_=ps[:, bk, :], func=AF.Identity, scale=0.125, bias=bq8[:, a, c:c + 1]),
                        reads=[f"ps{bk}", 'bq8'], writes=[f"qT{c}"])
                release()
                _stage('attQ')
                slot = acquire(('w', L, 'KV'))
                w3 = ring[:, slot, 0:8 * 384].rearrange("p (kc n) -> p kc n", kc=8)
                for c2 in range(2):
                    bk = bank('A')
                    for kc in range(8):
                        S.op('pe', lambda e, c2=c2, kc=kc, bk=bk, w3=w3: e.matmul(
                            ps[:, bk, :], lhsT=w3[:, kc, c2 * 128:(c2 + 1) * 128], rhs=hT[:, kc, :],
                            start=(kc == 0), stop=(kc == 7)),
                            reads=[f"ring{slot}", f"hT{kc}"], writes=[f"ps{bk}"])
                    S.op('act', lambda e, c2=c2, bk=bk: e.activation(
                        out=kTa[:, c2, 128:128 + T], in_=ps[:, bk, :], func=AF.Identity,
                        bias=col(('bk2', L), c2)),
                        reads=[f"ps{bk}", 'colsb'], writes=[f"kT{a}_{i}" for i in range(NT)])
                for tt in range(NT):
                    bk = bank('A')
                    for kc in range(8):
                        S.op('pe', lambda e, tt=tt, kc=kc, bk=bk, w3=w3: e.matmul(
                            ps[:, bk, 0:128], lhsT=hT[:, kc, tt * 128:(tt + 1) * 128], rhs=w3[:, kc, 256:384],
                            start=(kc == 0), stop=False),
                            reads=[f"ring{slot}", f"hT{kc}"], writes=[f"ps{bk}"])
                    S.op('pe', lambda e, bk=bk: e.matmul(
                        ps[:, bk, 0:128], lhsT=onesb[pr:pr + 1, :], rhs=rowsb[pr:pr + 1, ri, 0:128],
                        start=False, stop=True),
                        reads=['onesb', 'rowsb'], writes=[f"ps{bk}"])
                    S.op('act', lambda e, tt=tt, bk=bk: e.copy(out=Va[:, 1 + tt, :], in_=ps[:, bk, 0:128]),
                         reads=[f"ps{bk}"], writes=[f"V{a}_{tt}"])
                release()
                _stage('attKV')
                def ktag(slot_i):
                    return f"kT{a}_c" if slot_i == 0 else f"kT{a}_{slot_i - 1}"

                def vtag(slot_i):
                    return f"V{a}_c" if slot_i == 0 else f"V{a}_{slot_i - 1}"

                pairs = [(i, hf, p2) for i in range(NT) for hf in range(2) for p2 in range(2)]
                pstate = {}
                hstate = {}

                def stage_a(pidx):
                    i, hf, p2 = pairs[pidx]
                    has_prev = not (u == 0 and i == 0)
                    kbs = [0, 1] if has_prev else [1]
                    c0 = hf * 4 + p2 * 2
                    kvh = c0 // 4
                    sbanks = (bank('A'), bank('A'))
                    for cq in range(2):
                        c_ = c0 + cq
                        for hh in range(2):
                            S3 = ps[:, sbanks[hh], :].rearrange("p (q k t) -> p q k t", q=2, k=2)
                            for kb in kbs:
                                S.op('pe', lambda e, hh=hh, kb=kb, c_=c_, cq=cq, S3=S3: e.matmul(
                                    S3[:, cq, kb, :],
                                    lhsT=kTa[hh * 64:(hh + 1) * 64, kvh, (i + kb) * 128:(i + kb + 1) * 128],
                                    rhs=qT[hh * 64:(hh + 1) * 64, c_, i * 128:(i + 1) * 128],
                                    start=True, stop=True),
                                    reads=[ktag(i + kb), f"qT{c_}"], writes=[f"ps{sbanks[hh]}"])
                    info = []
                    for cq in range(2):
                        c = c0 + cq
                        sviews = [ps[:, sbanks[hh], :].rearrange("p (q k t) -> p q k t", q=2, k=2)[:, cq, :, :]
                                  for hh in range(2)]
                        ti = tf_lo()
                        T4 = TF[:, ti, :].rearrange("p (h k t) -> p h k t", h=2, k=2)
                        for hh in range(2):
                            h = 2 * c + hh
                            if has_prev:
                                S.op('dve', lambda e, hh=hh, h=h, sv=sviews[hh], T4=T4: e.scalar_tensor_tensor(
                                    out=T4[:, hh, :, :], in0=DM[:].rearrange("p (k t) -> p k t", k=2),
                                    scalar=slopes[h], in1=sv, op0=ALU.mult, op1=ALU.add),
                                    reads=['DM', f"ps{sbanks[hh]}"], writes=[f"TF{ti}"])
                            else:
                                S.op('dve', lambda e, hh=hh, h=h, sv=sviews[hh], T4=T4: e.scalar_tensor_tensor(
                                    out=T4[:, hh, 1, :], in0=DM[:, 128:256],
                                    scalar=slopes[h], in1=sv[:, 1, :], op0=ALU.mult, op1=ALU.add),
                                    reads=['DM', f"ps{sbanks[hh]}"], writes=[f"TF{ti}"])
                        pi = tb_next(1)
                        P4 = TB[:, pi, :].rearrange("p (h k t) -> p h k t", h=2, k=2)
                        if has_prev:
                            S.op('act', lambda e, ti=ti, pi=pi: e.activation(out=TB[:, pi, :], in_=TF[:, ti, :], func=AF.Exp),
                                 reads=[f"TF{ti}"], writes=[f"TB{pi}"])
                        else:
                            S.op('act', lambda e, T4=T4, P4=P4: e.activation(out=P4[:, :, 1, :], in_=T4[:, :, 1, :], func=AF.Exp),
                                 reads=[f"TF{ti}"], writes=[f"TB{pi}"])
                        info.append((c, pi, P4))
                    pstate[pidx] = (kbs, kvh, info)

                def stage_b(pidx):
                    i, hf, p2 = pairs[pidx]
                    kbs, kvh, info = pstate.pop(pidx)
                    if p2 == 0:
                        hstate[(i, hf)] = (bank('C'), bank('B'))
                    nb, db = hstate[(i, hf)]
                    for (c, pi, P4) in info:
                        cc = c % 4
                        for hh in range(2):
                            for n_i, kb in enumerate(kbs):
                                st, sp_ = (n_i == 0), (n_i == len(kbs) - 1)
                                S.op('pe', lambda e, hh=hh, kb=kb, cc=cc, P4=P4, st=st, sp_=sp_: e.matmul(
                                    ps[hh * 64:(hh + 1) * 64, nb, cc * 128:(cc + 1) * 128],
                                    lhsT=Va[:, i + kb, kvh * 64:(kvh + 1) * 64], rhs=P4[:, hh, kb, :],
                                    start=st, stop=sp_),
                                    reads=[vtag(i + kb), f"TB{pi}"], writes=[f"ps{nb}"])
                                S.op('pe', lambda e, hh=hh, kb=kb, cc=cc, P4=P4, st=st, sp_=sp_: e.matmul(
                                    ps[hh * 64:(hh + 1) * 64, db, cc * 128:(cc + 1) * 128],
                                    lhsT=onesb[:, 0:64], rhs=P4[:, hh, kb, :], start=st, stop=sp_),
                                    reads=['onesb', f"TB{pi}"], writes=[f"ps{db}"])
                    if p2 == 1:
                        ri2 = tf_hi()
                        R3 = TF[:, ri2, :].rearrange("p (c t) -> p c t", c=4)
                        S.op('dve', lambda e, R3=R3: e.tensor_tensor(
                            out=R3, in0=ps[:, db, :].rearrange("p (c t) -> p c t", c=4),
                            in1=esink[:, a, hf * 4:(hf + 1) * 4].unsqueeze(2).to_broadcast([128, 4, 128]), op=ALU.add),
                            reads=[f"ps{db}", 'esink'], writes=[f"TF{ri2}"])
                        S.op('act', lambda e: e.activation(out=TF[:, ri2, :], in_=TF[:, ri2, :], func=AF.Ln),
                             reads=[f"TF{ri2}"], writes=[f"TF{ri2}"])
                        S.op('act', lambda e: e.activation(out=TF[:, ri2, :], in_=TF[:, ri2, :], func=AF.Exp, scale=-1.0),
                             reads=[f"TF{ri2}"], writes=[f"TF{ri2}"])
                        S.op('dve', lambda e, R3=R3: e.tensor_tensor(
                            out=hT[:, hf * 4:(hf + 1) * 4, i * 128:(i + 1) * 128],
                            in0=ps[:, nb, :].rearrange("p (c t) -> p c t", c=4), in1=R3, op=ALU.mult),
                            reads=[f"ps{nb}", f"TF{ri2}"],
                            writes=([f"hT{c_}" for c_ in range(hf * 4, hf * 4 + 4)] if i == 0 else []) +
                                   [f"oT{c_}_{i}" for c_ in range(hf * 4, hf * 4 + 4)])

                oslot = acquire(('w', L, 'O'))

                def oproj(tt_):
                    proj_residual(xb, oslot, lambda kc: [f"hT{kc}", f"oT{kc}_{tt_}"],
                                  lambda kc, tt: hT[:, kc, tt * 128:(tt + 1) * 128], 8, par,
                                  lambda hf: rowsb[pr:pr + 1, ri, 128 + hf * 512:128 + (hf + 1) * 512], pr,
                                  tts=[tt_])

                stage_a(0)
                for pidx in range(len(pairs)):
                    if pidx + 1 < len(pairs):
                        stage_a(pidx + 1)
                    stage_b(pidx)
                    if OPROJ_INTERLEAVE and pidx % 4 == 1 and pidx // 4 >= 1:
                        oproj(pidx // 4 - 1)
                if OPROJ_INTERLEAVE:
                    oproj(NT - 1)
                else:
                    for tt_ in range(NT):
                        oproj(tt_)
                release()
                _stage('attN')

            def conformer(L, b, u, xb, par):
                cv = L // 2
                pr, ri = 32 * (L % 3), L // 3
                uc = ucar[cv]
                sbk1, sbk2 = bank('B'), bank('B')
                pend = []

                def flush_stats():
                    while pend:
                        c, pi = pend.pop(0)
                        S.op('pe', lambda e, c=c, pi=pi: e.matmul(
                            ps[:, sbk1, :], lhsT=onesb[:], rhs=TB[:, pi, :], start=(c == 0), stop=(c == 7)),
                            reads=['onesb', f"TB{pi}"], writes=[f"ps{sbk1}"])
                        S.op('pe', lambda e, c=c, pi=pi: e.matmul(
                            ps[:, sbk2, :], lhsT=onesb[:], rhs=TB[:, pi + 1, :], start=(c == 0), stop=(c == 7)),
                            reads=['onesb', f"TB{pi + 1}"], writes=[f"ps{sbk2}"])

                for i2 in range(2):
                    slot = acquire(('w', L, f'P1{i2}'))
                    w3 = ring[:, slot, :].rearrange("p (kc n) -> p kc n", kc=8)
                    for cc in range(4):
                        c = i2 * 4 + cc
                        bka = bank('A')
                        bkg = bank('A')
                        for kc in range(8):
                            S.op('pe', lambda e, cc=cc, kc=kc, bka=bka, w3=w3: e.matmul(
                                ps[:, bka, :], lhsT=w3[:, kc, cc * 128:(cc + 1) * 128], rhs=hT[:, kc, :],
                                start=(kc == 0), stop=(kc == 7)),
                                reads=[f"ring{slot}", f"hT{kc}"], writes=[f"ps{bka}"])
                        for kc in range(8):
                            S.op('pe', lambda e, cc=cc, kc=kc, bkg=bkg, w3=w3: e.matmul(
                                ps[:, bkg, :], lhsT=w3[:, kc, 512 + cc * 128:512 + (cc + 1) * 128], rhs=hT[:, kc, :],
                                start=(kc == 0), stop=(kc == 7)),
                                reads=[f"ring{slot}", f"hT{kc}"], writes=[f"ps{bkg}"])
                        ti = tf_lo()
                        S.op('act', lambda e, c=c, bkg=bkg, ti=ti: e.activation(
                            out=TF[:, ti, :], in_=ps[:, bkg, :], func=AF.Sigmoid, bias=col(('b1g', L), c)),
                            reads=[f"ps{bkg}", 'colsb'], writes=[f"TF{ti}"])
                        ub = ub8[:, c, :]
                        if u > 0:
                            S.op('dve', lambda e, c=c, ub=ub: e.tensor_copy(out=ub[:, 0:CW - 1], in_=uc[:, c, :]),
                                 reads=[f"ucar{cv}_{c}"], writes=[f"ub{c}"])
                        else:
                            S.op('dve', lambda e, ub=ub: e.memset(ub[:, 0:CW - 1], 0.0), writes=[f"ub{c}"])
                        S.op('dve', lambda e, c=c, ub=ub, bka=bka, ti=ti: e.scalar_tensor_tensor(
                            out=ub[:, CW - 1:CW - 1 + T], in0=ps[:, bka, :], scalar=col(('b1a', L), c),
                            in1=TF[:, ti, :], op0=ALU.add, op1=ALU.mult),
                            reads=[f"ps{bka}", f"TF{ti}", 'colsb'], writes=[f"ub{c}"])
                        S.op('dve', lambda e, c=c, ub=ub: e.tensor_copy(out=uc[:, c, :], in_=ub[:, T:T + CW - 1]),
                             reads=[f"ub{c}"], writes=[f"ucar{cv}_{c}"])
                    release()
                    for kk in range(2):
                        slot = acquire(('w', L, f'DG{2 * i2 + kk}'))
                        dg = ring[:, slot, :]
                        for ci in range(2):
                            c = i2 * 4 + kk * 2 + ci
                            ub = ub8[:, c, :]
                            bk = bank('A')
                            for j in range(CW):
                                m = ci * CW + j
                                S.op('pe', lambda e, j=j, m=m, bk=bk, ub=ub, dg=dg: e.matmul(
                                    ps[:, bk, :], lhsT=dg[:, m * 128:(m + 1) * 128], rhs=ub[:, j:j + T],
                                    start=(j == 0), stop=(j == CW - 1)),
                                    reads=[f"ring{slot}", f"ub{c}"], writes=[f"ps{bk}"])
                            vc = aTf[:, c * T:(c + 1) * T]
                            vtags = [f"aT{2 * c}", f"aT{2 * c + 1}"]
                            S.op('act', lambda e, c=c, bk=bk, vc=vc: e.activation(
                                out=vc, in_=ps[:, bk, :], func=AF.Identity, bias=col(('bdw', L), c)),
                                reads=[f"ps{bk}", 'colsb'], writes=vtags)
                            pi = tb_next(2)
                            S.op('act', lambda e, c=c, bk=bk, pi=pi: e.activation(
                                out=TB[:, pi, :], in_=ps[:, bk, :], func=AF.Identity, bias=col(('bdw', L), c)),
                                reads=[f"ps{bk}", 'colsb'], writes=[f"TB{pi}"])
                            S.op('act', lambda e, c=c, bk=bk, pi=pi: e.activation(
                                out=TB[:, pi + 1, :], in_=ps[:, bk, :], func=AF.Square, bias=col(('bdw', L), c)),
                                reads=[f"ps{bk}", 'colsb'], writes=[f"TB{pi + 1}"])
                            flush_stats()
                            pend.append((c, pi))
                        release()
                flush_stats()
                S.op('act', lambda e: e.activation(out=TF[:, 4, :], in_=ps[:, sbk1, :], func=AF.Copy, scale=1.0 / D),
                     reads=[f"ps{sbk1}"], writes=['TF4'])
                S.op('dve', lambda e: e.tensor_tensor(out=TF[:, 5, :], in0=TF[:, 4, :], in1=TF[:, 4, :], op=ALU.mult),
                     reads=['TF4'], writes=['TF5'])
                S.op('dve', lambda e: e.scalar_tensor_tensor(
                    out=TF[:, 5, :], in0=ps[:, sbk2, :], scalar=1.0 / D, in1=TF[:, 5, :], op0=ALU.mult, op1=ALU.subtract),
                    reads=[f"ps{sbk2}", 'TF5'], writes=['TF5'])
                S.op('act', lambda e: e.activation(out=TF[:, 5, :], in_=TF[:, 5, :], func=AF.Sqrt, bias=EPS),
                     reads=['TF5'], writes=['TF5'])
                S.op('dve', lambda e: e.reciprocal(out=TF[:, 5, :], in_=TF[:, 5, :]), reads=['TF5'], writes=['TF5'])
                for c in range(8):
                    vc = aTf[:, c * T:(c + 1) * T]
                    vtags = [f"aT{2 * c}", f"aT{2 * c + 1}"]
                    zi = tf_hi()
                    S.op('dve', lambda e, vc=vc, zi=zi: e.tensor_tensor(out=TF[:, zi, :], in0=vc, in1=TF[:, 4, :], op=ALU.subtract),
                         reads=vtags + ['TF4'], writes=[f"TF{zi}"])
                    S.op('dve', lambda e, zi=zi: e.tensor_tensor(out=TF[:, zi, :], in0=TF[:, zi, :], in1=TF[:, 5, :], op=ALU.mult),
                         reads=[f"TF{zi}", 'TF5'], writes=[f"TF{zi}"])
                    S.op('act', lambda e, c=c, zi=zi: e.activation(
                        out=hT[:, c, :], in_=TF[:, zi, :], func=AF.Silu, scale=col(('lng', L), c), bias=col(('lnb', L), c)),
                        reads=[f"TF{zi}", 'colsb'], writes=[f"hT{c}"])
                slot = acquire(('w', L, 'P2'))
                proj_residual(xb, slot, lambda kc: [f"hT{kc}"], lambda kc, tt: hT[:, kc, tt * 128:(tt + 1) * 128], 8,
                              par, lambda hf: rowsb[pr:pr + 1, ri, hf * 512:(hf + 1) * 512], pr)
                release()

            def mlp(L, b, u, xb, par):
                for s in range(4):
                    slot = acquire(('w', L, f'U{s}'))
                    w3 = ring[:, slot, :].rearrange("p (kc n) -> p kc n", kc=8)
                    for f in range(8):
                        bk = bank('A')
                        for kc in range(8):
                            S.op('pe', lambda e, f=f, kc=kc, bk=bk, w3=w3: e.matmul(
                                ps[:, bk, :], lhsT=w3[:, kc, f * 128:(f + 1) * 128], rhs=hT[:, kc, :],
                                start=(kc == 0), stop=(kc == 7)),
                                reads=[f"ring{slot}", f"hT{kc}"], writes=[f"ps{bk}"])
                        ti = tf_lo()
                        S.op('act', lambda e, bk=bk, ti=ti: e.activation(out=TF[:, ti, :], in_=ps[:, bk, :], func=AF.Relu),
                             reads=[f"ps{bk}"], writes=[f"TF{ti}"])
                        ch = s * 8 + f
                        S.op('dve', lambda e, ch=ch, ti=ti: e.tensor_tensor(
                            out=aT[:, ch, :], in0=TF[:, ti, :], in1=TF[:, ti, :], op=ALU.mult),
                            reads=[f"TF{ti}"], writes=[f"aT{ch}"])
                    release()
                for s in range(4):
                    slot = acquire(('w', L, f'D{s}'))
                    w3 = ring[:, slot, :].rearrange("p (kc n) -> p kc n", kc=32)
                    for tt in range(NT):
                        bk = bank('A')
                        for kc in range(32):
                            S.op('pe', lambda e, tt=tt, kc=kc, bk=bk, w3=w3: e.matmul(
                                ps[:, bk, 0:256], lhsT=aT[:, kc, tt * 128:(tt + 1) * 128], rhs=w3[:, kc, :],
                                start=(kc == 0), stop=(kc == 31)),
                                reads=[f"ring{slot}", f"aT{kc}"], writes=[f"ps{bk}"])
                        ti = tf_hi()
                        S.op('dve', lambda e, s=s, bk=bk, ti=ti: e.tensor_tensor(
                            out=TF[:, ti, 0:256], in0=ps[:, bk, 0:256], in1=Gb[:, par, 1, s * 256:(s + 1) * 256], op=ALU.mult),
                            reads=[f"ps{bk}", f"G{par}_1_{s // 2}"], writes=[f"TF{ti}"])
                        xv = xs[:, xb * NT + tt, s * 256:(s + 1) * 256]
                        S.op('dve', lambda e, xv=xv, ti=ti: e.tensor_tensor(out=xv, in0=xv, in1=TF[:, ti, 0:256], op=ALU.add),
                             reads=[f"TF{ti}"] + xtags(xb, tt, (s,)), writes=xtags(xb, tt, (s,)))
                    release()

            un = 0
            ul_list = [(L_, b_) for b_ in range(NSEQ) for u_ in range(NU) for L_ in range(DEPTH)]
            ul = 0
            load_gates(ul_list[0][0], ul_list[0][1], 0)
            for b in range(NSEQ):
                for u in range(NU):
                    xb = un % NXB
                    if NXB == 1:
                        if un > 0:
                            load_x(b, u, xb)
                    elif un + 1 < NSEQ * NU:
                        nb_, nu_ = divmod(un + 1, NU)
                        load_x(nb_, nu_, (un + 1) % NXB)
                    for L in range(DEPTH):
                        par = ul % 2
                        nxt = ul_list[ul + 1] if ul + 1 < len(ul_list) else None
                        defer_gates = (un == 0 and L + 1 < DEPTH)
                        if nxt is not None and not defer_gates:
                            load_gates(nxt[0], nxt[1], (ul + 1) % 2)
                        ul += 1
                        _stage('gates')
                        norm_to_hT(xb, lambda kc, L=L, b=b: A1[:, L, kc, b:b + 1],
                                   lambda kc, L=L, b=b: modT[:, L * 6 + 0, kc, b:b + 1],
                                   [f"A1_{L}", f"modT{L}_0"])
                        _stage('norm1')
                        if L % 2 == 0:
                            attention(L, b, u, xb, par)
                        else:
                            conformer(L, b, u, xb, par)
                        if defer_gates:
                            mod_layer(L + 1, ul % 2)
                            load_gates(nxt[0], nxt[1], ul % 2)
                        _stage('mixer')
                        norm_to_hT(xb, lambda kc, L=L, b=b: A2[:, L, kc, b:b + 1],
                                   lambda kc, L=L, b=b: modT[:, L * 6 + 3, kc, b:b + 1],
                                   [f"A2_{L}", f"modT{L}_3"])
                        _stage('norm2')
                        mlp(L, b, u, xb, par)
                        _stage('mlp')
                    for tt in range(NT):
                        xrow = xs[:, xb * NT + tt, :]
                        S.op('act', lambda e, tt=tt, xrow=xrow: e.activation(
                            out=ntm[:, tt, :], in_=xrow, func=AF.Square, accum_out=ss[:, tt:tt + 1]),
                            reads=xtags(xb, tt), writes=[f"ntm{tt}", f"ss{tt}"])
                        S.op('act', lambda e, tt=tt: e.activation(
                            out=sd[:, tt:tt + 1], in_=ss[:, tt:tt + 1], func=AF.Sqrt, scale=1.0 / D, bias=EPS),
                            reads=[f"ss{tt}"], writes=[f"sd{tt}"])
                        S.op('dve', lambda e, tt=tt: e.reciprocal(out=rs[:, tt:tt + 1], in_=sd[:, tt:tt + 1]),
                             reads=[f"sd{tt}"], writes=[f"rs{tt}"])
                        S.op('dve', lambda e, tt=tt, xrow=xrow: e.scalar_tensor_tensor(
                            out=ntm[:, tt, :], in0=xrow, scalar=rs[:, tt:tt + 1], in1=FN[:], op0=ALU.mult, op1=ALU.mult),
                            reads=xtags(xb, tt) + [f"rs{tt}", 'FN0', 'FN1'], writes=[f"ntm{tt}"])
                        ot = f"out{un}_{tt}"
                        out_tags.append(ot)
                        S.op('pool', lambda e, tt=tt, b=b, u=u: e.dma_start(
                            out=out_d[b, u * T + tt * 128:u * T + (tt + 1) * 128, :], in_=ntm[:, tt, :]),
                            reads=[f"ntm{tt}"], writes=[ot], dma=True)
                    un += 1

        stopped = False
        try:
            _body()
        except StopBuild:
            stopped = True
        assert stopped or wstate['cur'] == len(seq), (wstate, len(seq))
        S.op('pool', lambda e: e.nop(), reads=out_tags)
        S.emit()
    return nc, S


_CACHE = {}


def kernel(**inputs):
    NCORES = 8
    inp = {k: np.asarray(v) for k, v in inputs.items()}
    B, SEQ, _ = inp['x'].shape
    DEPTH = inp['w_mod'].shape[0]
    NSEQ = B // NCORES
    key = (NSEQ, SEQ, DEPTH)
    if key not in _CACHE:
        _CACHE[key] = build_program(NSEQ, SEQ, DEPTH)[0]
    nc = _CACHE[key]
    cols, rows = prep_shared(inp, DEPTH)
    dm, ident = host_constants()
    f32c = lambda a: np.ascontiguousarray(np.asarray(a, dtype=np.float32))
    shared = dict(cols=cols, rows=rows.reshape(128, 2 * 1152), dm=dm, ident=ident,
                  w_mod=f32c(inp['w_mod']), w_qkv=f32c(inp['w_qkv']), w_o=f32c(inp['w_o']),
                  w_pw1=f32c(inp['w_pw1']), w_pw2=f32c(inp['w_pw2']),
                  w_up=f32c(inp['w_up']), w_down=f32c(inp['w_down']))
    x = f32c(inp['x'])
    c = f32c(inp['c'])
    in_maps = []
    for ci in range(NCORES):
        cc = c[ci * NSEQ:(ci + 1) * NSEQ]
        cT = np.ascontiguousarray(cc.reshape(NSEQ, 8, 128).transpose(2, 1, 0)).reshape(128, 8 * NSEQ)
        m = dict(shared)
        m['x'] = x[ci * NSEQ:(ci + 1) * NSEQ]
        m['cT'] = cT
        in_maps.append(m)
    res = run_bass_kernel_spmd(nc, in_maps, core_ids=list(range(NCORES)))
    out = np.concatenate([np.asarray(r['out']) for r in res.results], axis=0)
    return out.astype(np.float32, copy=False)
```

```python
import contextlib
import numpy as np
import concourse.bass as bass
import concourse.mybir as mybir
from concourse.bass_utils import run_bass_kernel_spmd

F32 = mybir.dt.float32
BF16 = mybir.dt.bfloat16
AF = mybir.ActivationFunctionType
ALU = mybir.AluOpType

ENGS = ['pe', 'act', 'dve', 'pool', 'sp']
CENGS = ['pe', 'act', 'dve', 'pool']
SEM_W = 4000
DMA_RING = 8

D = 1024
NH = 16
HD = 64
NMOD = 6
DFF = 4096
CW = 31
EPS = 1e-6
T = 512
NT = 4
NSLOT = 3
SLAB = 8192
NEG = -1.0e6
OPROJ_INTERLEAVE = False


class StopBuild(Exception):
    pass


def _stage(name):
    PHASE[0] = 'after_' + name


PHASE = ['init']


class Sched:
    def __init__(self, nc):
        self.nc = nc
        self.ops = {e: [] for e in ENGS}
        self.tags = {}
        self.known = {e: {f: -1 for f in CENGS} for e in ENGS}
        self.known_dma = {e: set() for e in ENGS}
        self.ndma = {}

    def op(self, eng, fn, reads=(), writes=(), dma=False):
        idx = len(self.ops[eng])
        me = (eng, idx)
        raw, oth = set(), set()
        for t in reads:
            st = self.tags.get(t)
            if st is not None and st[0] is not None:
                raw.add(st[0])
        for t in writes:
            st = self.tags.get(t)
            if st is not None:
                if st[0] is not None:
                    oth.add(st[0])
                oth.update(st[1])
        deps = set(raw)
        for d in oth:
            if d[0] == eng and eng == 'pe' and not dma:
                continue
            deps.add(d)
        deps.discard(me)
        waits = []
        kn = self.known[eng]
        kd = self.known_dma[eng]
        for d in sorted(deps, key=lambda z: -z[1]):
            f, j = d
            o = self.ops[f][j]
            if o['dma']:
                if d in kd:
                    continue
                kd.add(d)
            else:
                if kn[f] >= j:
                    continue
                kn[f] = j
            waits.append(d)
            o['needs_inc'] = True
            for g, v in o['known'].items():
                if v > kn[g]:
                    kn[g] = v
        rec = dict(fn=fn, waits=waits, needs_inc=False, dma=dma, known=dict(kn), dma_k=None, phase=PHASE[0])
        if dma:
            rec['dma_k'] = self.ndma.get(eng, 0)
            self.ndma[eng] = rec['dma_k'] + 1
        self.ops[eng].append(rec)
        for t in reads:
            st = self.tags.setdefault(t, [None, []])
            st[1].append(me)
        for t in writes:
            self.tags[t] = [me, []]
        return me

    def emit(self):
        nc = self.nc
        with contextlib.ExitStack() as es:
            for e in CENGS:
                n_inc = sum(1 for o in self.ops[e] if o['needs_inc'] and not o['dma'])
                nsem = max(1, (n_inc + SEM_W - 1) // SEM_W)
                sems = [es.enter_context(nc.semaphore(f"s_{e}_{i}")) for i in range(nsem)]
                c = 0
                for o in self.ops[e]:
                    if o['needs_inc'] and not o['dma']:
                        o['sem'] = (sems[c // SEM_W], c % SEM_W + 1)
                        c += 1
            dma_by_k = {}
            for e in ENGS:
                dl = [o for o in self.ops[e] if o['dma']]
                if not dl:
                    continue
                dsems = [es.enter_context(nc.semaphore(f"s_dma_{e}_{i}")) for i in range(DMA_RING)]
                for o in dl:
                    k = o['dma_k']
                    o['sem'] = (dsems[k % DMA_RING], 16 * (k // DMA_RING + 1))
                    dma_by_k[(e, k)] = o
            block = es.enter_context(nc.Block())

            def run(eng_name):
                def body(eng):
                    for o in self.ops[eng_name]:
                        for (f, j) in o['waits']:
                            s, v = self.ops[f][j]['sem']
                            eng.wait_ge(s, v)
                        if o['dma']:
                            k = o['dma_k']
                            if k >= DMA_RING:
                                s, v = dma_by_k[(eng_name, k - DMA_RING)]['sem']
                                eng.wait_ge(s, v)
                            ins = o['fn'](eng)
                            ins.then_inc(o['sem'][0], 16)
                        else:
                            ins = o['fn'](eng)
                            if o['needs_inc']:
                                ins.then_inc(o['sem'][0], 1)
                return body

            block.tensor(run('pe'))
            block.scalar(run('act'))
            block.vector(run('dve'))
            block.gpsimd(run('pool'))
            block.sync(run('sp'))

    def stats(self):
        return {e: (len(self.ops[e]), sum(len(o['waits']) for o in self.ops[e])) for e in ENGS}


def col_layout(depth):
    lay = {}
    n = 0

    def add(name, w):
        nonlocal n
        lay[name] = n
        n += w

    for L in range(depth):
        add(('nmix', L), 8)
        add(('nmlp', L), 8)
        add(('bmod', L), 48)
        if L % 2 == 0:
            add(('bq', L), 8)
            add(('bk2', L), 2)
            add(('sink', L), 8)
        else:
            add(('b1a', L), 8)
            add(('b1g', L), 8)
            add(('wdw', L), CW * 8)
            add(('bdw', L), 8)
            add(('lng', L), 8)
            add(('lnb', L), 8)
    add(('fn',), 8)
    return lay, n


def colsT(v):
    v = np.asarray(v, dtype=np.float32)
    return np.ascontiguousarray(v.reshape(-1, 128).T)


def alibi_slopes():
    return [float(np.float32(2.0 ** (-8.0 * (h + 1) / NH))) for h in range(NH)]


def host_constants():
    s = np.arange(128)[:, None]
    t = np.arange(128)[None, :]
    prev = np.where(t < s, -(128 + t - s), NEG).astype(np.float32)
    cur = np.where(t >= s, -(t - s), NEG).astype(np.float32)
    dm = np.concatenate([prev, cur], axis=1).astype(np.float32)
    ident = np.eye(128, dtype=np.float32)
    return dm, ident


def prep_shared(inp, depth):
    lay, ncol = col_layout(depth)
    cols = np.zeros((128, ncol), np.float32)
    rows = np.zeros((128, 2, 1152), np.float32)
    for L in range(depth):
        j = L // 2
        cols[:, lay[('nmix', L)]:lay[('nmix', L)] + 8] = colsT(inp['norm_mix'][L])
        cols[:, lay[('nmlp', L)]:lay[('nmlp', L)] + 8] = colsT(inp['norm_mlp'][L])
        cols[:, lay[('bmod', L)]:lay[('bmod', L)] + 48] = colsT(inp['b_mod'][L])
        pr, ri = 32 * (L % 3), L // 3
        if L % 2 == 0:
            bqkv = np.asarray(inp['b_qkv'][j], np.float32)
            cols[:, lay[('bq', L)]:lay[('bq', L)] + 8] = colsT(bqkv[:1024])
            k0, k1 = bqkv[1024:1088], bqkv[1088:1152]
            cols[:, lay[('bk2', L)]:lay[('bk2', L)] + 2] = colsT(np.concatenate([k0, k0, k1, k1]))
            sk = np.asarray(inp['sinks'][j], np.float32)
            cols[:, lay[('sink', L)]:lay[('sink', L)] + 8] = colsT(np.repeat(sk, 64))
            rows[pr, ri, 0:128] = bqkv[1152:1280]
            rows[pr, ri, 128:1152] = np.asarray(inp['b_o'][j], np.float32)
        else:
            b1 = np.asarray(inp['b_pw1'][j], np.float32)
            cols[:, lay[('b1a', L)]:lay[('b1a', L)] + 8] = colsT(b1[:1024])
            cols[:, lay[('b1g', L)]:lay[('b1g', L)] + 8] = colsT(b1[1024:])
            wd = np.asarray(inp['w_dw'][j], np.float32)
            cols[:, lay[('wdw', L)]:lay[('wdw', L)] + CW * 8] = colsT(wd.reshape(-1))
            cols[:, lay[('bdw', L)]:lay[('bdw', L)] + 8] = colsT(inp['b_dw'][j])
            cols[:, lay[('lng', L)]:lay[('lng', L)] + 8] = colsT(inp['conv_ln_g'][j])
            cols[:, lay[('lnb', L)]:lay[('lnb', L)] + 8] = colsT(inp['conv_ln_b'][j])
            rows[pr, ri, 0:1024] = np.asarray(inp['b_pw2'][j], np.float32)
    cols[:, lay[('fn',)]:lay[('fn',)] + 8] = colsT(inp['final_norm'])
    return cols, rows


def build_program(NSEQ=4, SEQ=2048, DEPTH=4, NXB=1):
    NU = SEQ // T
    NA = (DEPTH + 1) // 2
    NCV = DEPTH // 2
    lay, NCOL = col_layout(DEPTH)
    slopes = alibi_slopes()
    nc = bass.Bass("TRN2", target_bir_lowering=False)

    def din(name, shape, dt=F32):
        return nc.dram_tensor(name, shape, dt, kind="ExternalInput").ap()

    x_d = din("x", [NSEQ, SEQ, D])
    cT_d = din("cT", [128, 8 * NSEQ])
    cols_d = din("cols", [128, NCOL])
    rows_d = din("rows", [128, 2 * 1152])
    dm_d = din("dm", [128, 256])
    ident_d = din("ident", [128, 128])
    w_mod_d = din("w_mod", [DEPTH, D, NMOD * D])
    w_qkv_d = din("w_qkv", [NA, D, 1280])
    w_o_d = din("w_o", [NA, D, D])
    w_pw1_d = din("w_pw1", [max(NCV, 1), D, 2 * D])
    w_pw2_d = din("w_pw2", [max(NCV, 1), D, D])
    w_up_d = din("w_up", [DEPTH, D, DFF])
    w_down_d = din("w_down", [DEPTH, DFF, D])
    out_d = nc.dram_tensor("out", [NSEQ, SEQ, D], F32, kind="ExternalOutput").ap()
    def layer_slabs(L):
        if L % 2 == 0:
            names = ['Q', 'KV', 'O']
        else:
            names = ['P10', 'DG0', 'DG1', 'P11', 'DG2', 'DG3', 'P2']
        return names + [f'U{s_}' for s_ in range(4)] + [f'D{s_}' for s_ in range(4)]

    SID = {}
    for L_ in range(DEPTH):
        for nm_ in layer_slabs(L_):
            SID[(L_, nm_)] = len(SID)
    for L_ in range(DEPTH):
        for s_ in range(6):
            SID[(L_, f'M{s_}')] = len(SID)
    wsc = nc.dram_tensor("wsc", [len(SID), 128, SLAB], BF16, kind="Internal").ap()
    gsc = nc.dram_tensor("gsc", [DEPTH * NSEQ * 2, 128, D], F32, kind="Internal").ap()

    S = Sched(nc)
    with contextlib.ExitStack() as es:
        def sb(name, shape, dt):
            return es.enter_context(nc.sbuf_tensor(name, shape, dt))

        xs = sb("xs", [128, NXB * NT, D], F32)
        hT = sb("hT", [128, 8, T], BF16)
        aT = sb("aT", [128, 32, T], BF16)
        aTf = aT[:].rearrange("p a t -> p (a t)").bitcast(F32)
        ntm = sb("ntm", [128, NT, D], F32)
        ring = sb("ring", [128, NSLOT, SLAB], BF16)
        Gb = sb("Gb", [128, 2, 2, D], F32)
        FN = sb("FN", [128, D], F32)
        qT = sb("qT", [128, 8, T], BF16)
        kT = [sb(f"kT{a}", [128, 2, T + 128], BF16) for a in range(NA)]
        Vt = [sb(f"V{a}", [128, NT + 1, 128], BF16) for a in range(NA)]
        DM = sb("DM", [128, 256], F32)
        TF = sb("TF", [128, 6, T], F32)
        TB = sb("TB", [128, 4, T], BF16)
        ub8 = sb("ub8", [128, 8, T + 32], BF16)
        ucar = [sb(f"ucar{a}", [128, 8, CW - 1], BF16) for a in range(max(NCV, 1))]
        identb = sb("identb", [128, 128], BF16)
        colsb = sb("colsb", [128, NCOL], F32)
        rowsb = sb("rowsb", [128, 2, 1152], BF16)
        identf = sb("identf", [128, 128], F32)
        onesf = sb("onesf", [128, 128], F32)
        onesb = sb("onesb", [128, 128], BF16)
        rep4 = TF[:, 4:6, :]
        cTf = sb("cTf", [128, 8 * NSEQ], F32)
        csT = sb("csT", [128, 8, NSEQ], BF16)
        modT = sb("modT", [128, DEPTH * 6, 8, NSEQ], F32)
        A1 = sb("A1", [128, DEPTH, 8, NSEQ], F32)
        A2 = sb("A2", [128, DEPTH, 8, NSEQ], F32)
        bq8 = sb("bq8", [128, NA, 8], F32)
        esink = sb("esink", [128, NA, 8], F32)
        ss = sb("ss", [128, 8], F32)
        sd = sb("sd", [128, 8], F32)
        rs = sb("rs", [128, 8], F32)
        ps = es.enter_context(nc.psum_tensor("ps", [128, 8, 512], F32))

        rot = {'A': [0, 1, 2, 3], 'B': [4, 5], 'C': [6, 7]}
        rpos = {'A': 0, 'B': 0, 'C': 0}

        def bank(cls):
            b = rot[cls][rpos[cls] % len(rot[cls])]
            rpos[cls] += 1
            return b

        tfpos = {'lo': 0, 'hi': 0}

        def tf_lo():
            i = tfpos['lo'] % 2
            tfpos['lo'] += 1
            return i

        def tf_hi():
            i = 2 + tfpos['hi'] % 2
            tfpos['hi'] += 1
            return i

        tbpos = [0]

        def tb_next(n=1):
            i = tbpos[0] % (4 // n)
            tbpos[0] += 1
            return i * n

        def col(name, j=0):
            c0 = lay[name] + j
            return colsb[:, c0:c0 + 1]

        def slab_id(L, k):
            return SID[(L, k)]

        seq = []
        for s in range(6):
            seq.append(('wmod', 0, s))
        for un in range(NSEQ * NU):
            for L in range(DEPTH):
                for k in layer_slabs(L):
                    if un == 0 and k == 'U0' and L + 1 < DEPTH:
                        for s in range(6):
                            seq.append(('wmod', L + 1, s))
                    seq.append(('w', L, k))
        wstate = {'loaded': 0, 'cur': 0}

        def slab_len(L, k):
            if k == 'KV':
                return 8 * 384
            if k.startswith('DG'):
                return 62 * 128
            return SLAB

        def slab_tags(L, k):
            sid = slab_id(L, k)
            if k == 'KV':
                return [f"wsc{sid}_{d}" for d in (0, 64, 128, 192, 256)]
            if k in ('P10', 'P11'):
                return [f"wsc{sid}_a", f"wsc{sid}_g"]
            return [f"wsc{sid}"]

        def record_load(i):
            ent = seq[i]
            slot = i % NSLOT
            if ent[0] == 'wmod':
                _, L, s = ent
                sid = slab_id(L, f'M{s}')
                S.op('sp', lambda e: e.dma_start(out=ring[:, slot, :], in_=wsc[sid]),
                     reads=[f"wsc{sid}"], writes=[f"ring{slot}"], dma=True)
            else:
                _, L, k = ent
                n = slab_len(L, k)
                sid = slab_id(L, k)
                S.op('sp', lambda e: e.dma_start(out=ring[:, slot, 0:n], in_=wsc[sid][:, 0:n]),
                     reads=slab_tags(L, k), writes=[f"ring{slot}"], dma=True)

        def prefetch():
            while wstate['loaded'] < len(seq) and wstate['loaded'] < wstate['cur'] + NSLOT:
                record_load(wstate['loaded'])
                wstate['loaded'] += 1

        def acquire(expect):
            i = wstate['cur']
            assert seq[i] == expect, (seq[i], expect)
            prefetch()
            return i % NSLOT

        def release():
            wstate['cur'] += 1
            prefetch()

        def load_x(b, u, xb):
            for tt in range(NT):
                S.op('act', lambda e, tt=tt: e.dma_start(
                    out=xs[:, xb * NT + tt, :], in_=x_d[b, u * T + tt * 128:u * T + (tt + 1) * 128, :]),
                    writes=[f"x{xb}_{tt}_{q}" for q in range(4)], dma=True)

        out_tags = []

        def _body():
            load_x(0, 0, 0)
            S.op('sp', lambda e: e.dma_start(out=colsb[:], in_=cols_d), writes=['colsb'], dma=True)
            S.op('sp', lambda e: e.dma_start(out=identf[:], in_=ident_d), writes=['identf'], dma=True)
            S.op('sp', lambda e: e.dma_start(out=DM[:], in_=dm_d), writes=['DM'], dma=True)
            S.op('sp', lambda e: e.dma_start(out=cTf[:], in_=cT_d), writes=['cTf'], dma=True)
            S.op('pool', lambda e: e.dma_start(out=rowsb[:].rearrange("p a n -> p (a n)"), in_=rows_d),
                 writes=['rowsb'], dma=True)
            _stage('dma0')

            def conv_slab(sid, src3, n):
                dst = wsc[sid][:, 0:8 * n].rearrange("p (kc n) -> p kc n", kc=8)
                S.op('pool', lambda e: e.dma_start(out=dst, in_=src3), writes=[f"wsc{sid}"], dma=True)

            def r3(ap2):
                return ap2.rearrange("(kc p) n -> p kc n", p=128)

            def convert_slab(L, nm):
                j = L // 2
                if nm == 'Q':
                    conv_slab(slab_id(L, 'Q'), r3(w_qkv_d[j][:, 0:1024]), 1024)
                elif nm == 'KV':
                    sid = slab_id(L, 'KV')
                    dst3 = wsc[sid][:, 0:8 * 384].rearrange("p (kc n) -> p kc n", kc=8)
                    pieces = [(0, 1024), (64, 1024), (128, 1088), (192, 1088)]
                    for (d0, s0) in pieces:
                        S.op('pool', lambda e, d0=d0, s0=s0: e.dma_start(
                            out=dst3[:, :, d0:d0 + 64], in_=r3(w_qkv_d[j][:, s0:s0 + 64])),
                            writes=[f"wsc{sid}_{d0}"], dma=True)
                    S.op('pool', lambda e: e.dma_start(out=dst3[:, :, 256:384], in_=r3(w_qkv_d[j][:, 1152:1280])),
                         writes=[f"wsc{sid}_256"], dma=True)
                elif nm == 'O':
                    conv_slab(slab_id(L, 'O'), r3(w_o_d[j]), 1024)
                elif nm in ('P10', 'P11'):
                    i = int(nm[2])
                    sid = slab_id(L, nm)
                    dst3 = wsc[sid].rearrange("p (kc n) -> p kc n", kc=8)
                    S.op('pool', lambda e: e.dma_start(
                        out=dst3[:, :, 0:512], in_=r3(w_pw1_d[j][:, i * 512:(i + 1) * 512])),
                        writes=[f"wsc{sid}_a"], dma=True)
                    S.op('pool', lambda e: e.dma_start(
                        out=dst3[:, :, 512:1024], in_=r3(w_pw1_d[j][:, 1024 + i * 512:1024 + (i + 1) * 512])),
                        writes=[f"wsc{sid}_g"], dma=True)
                elif nm == 'P2':
                    conv_slab(slab_id(L, 'P2'), r3(w_pw2_d[j]), 1024)
                elif nm[0] == 'U':
                    s_ = int(nm[1])
                    conv_slab(slab_id(L, nm), r3(w_up_d[L][:, s_ * 1024:(s_ + 1) * 1024]), 1024)
                elif nm[0] == 'D' and nm[1] != 'G':
                    s_ = int(nm[1])
                    sid = slab_id(L, nm)
                    dst = wsc[sid].rearrange("p (kc n) -> p kc n", kc=32)
                    src = w_down_d[L][:, s_ * 256:(s_ + 1) * 256].rearrange("(kc p) n -> p kc n", p=128)
                    S.op('pool', lambda e: e.dma_start(out=dst, in_=src), writes=[f"wsc{sid}"], dma=True)

            def convert_wmod(L):
                for s_ in range(6):
                    conv_slab(slab_id(L, f'M{s_}'), r3(w_mod_d[L][:, s_ * 1024:(s_ + 1) * 1024]), 1024)

            convert_wmod(0)
            S.op('act', lambda e: e.copy(out=identb[:], in_=identf[:]), reads=['identf'], writes=['identb'])
            aTflat = aT[:].rearrange("p a t -> p (a t)")

            def build_diag(L):
                cw0 = lay[('wdw', L)]
                for k in range(4):
                    hb = k % 2
                    stg = aTflat[:, hb * 8192:(hb + 1) * 8192]
                    stags = [f"aT{hb * 16 + q}" for q in range(16)]
                    for ci in range(2):
                        c = 2 * k + ci
                        wcols = colsb[:, cw0 + c:cw0 + c + 8 * (CW - 1) + 1:8]
                        S.op('dve', lambda e, ci=ci, stg=stg, wcols=wcols: e.tensor_tensor(
                            out=stg[:, ci * CW * 128:(ci + 1) * CW * 128].rearrange("p (j n) -> p j n", j=CW),
                            in0=identb[:].unsqueeze(1).to_broadcast([128, CW, 128]),
                            in1=wcols.unsqueeze(2).to_broadcast([128, CW, 128]), op=ALU.mult),
                            reads=['identb', 'colsb'], writes=stags)
                    sid = slab_id(L, f'DG{k}')
                    S.op('pool', lambda e, sid=sid, stg=stg: e.dma_start(out=wsc[sid][:, 0:62 * 128], in_=stg[:, 0:62 * 128]),
                         reads=stags, writes=[f"wsc{sid}"], dma=True)

            for L_ in range(1, DEPTH, 2):
                build_diag(L_)

            n_first = 6 + sum(len(layer_slabs(L_)) for L_ in range(DEPTH)) + 6 * (DEPTH - 1)
            for ent in seq[6:n_first]:
                if ent[0] == 'wmod':
                    conv_slab(slab_id(ent[1], f'M{ent[2]}'), r3(w_mod_d[ent[1]][:, ent[2] * 1024:(ent[2] + 1) * 1024]), 1024)
                else:
                    convert_slab(ent[1], ent[2])
            _stage('conv0')

            S.op('dve', lambda e: e.memset(onesf[:], 1.0), writes=['onesf'])
            S.op('dve', lambda e: e.memset(onesb[:], 1.0), writes=['onesb'])
            S.op('act', lambda e: e.activation(out=csT[:].rearrange("p k b -> p (k b)"), in_=cTf[:], func=AF.Silu),
                 reads=['cTf'], writes=['csT'])
            for a in range(NA):
                L = 2 * a
                c0 = lay[('bq', L)]
                S.op('act', lambda e, a=a, c0=c0: e.mul(out=bq8[:, a, :], in_=colsb[:, c0:c0 + 8], mul=0.125),
                     reads=['colsb'], writes=['bq8'])
                c1 = lay[('sink', L)]
                S.op('act', lambda e, a=a, c1=c1: e.activation(out=esink[:, a, :], in_=colsb[:, c1:c1 + 8], func=AF.Exp),
                     reads=['colsb'], writes=['esink'])
            _stage('small')

            def bcast_tile(dst, dst_tag, colfn4, col_tags):
                for hf in range(2):
                    bk = bank('B')
                    ri = hf
                    S.op('dve', lambda e, hf=hf, ri=ri: e.tensor_tensor(
                        out=rep4[:, ri, :].rearrange("p (j n) -> p j n", j=4),
                        in0=identf[:].unsqueeze(1).to_broadcast([128, 4, 128]),
                        in1=colfn4(hf).unsqueeze(2).to_broadcast([128, 4, 128]), op=ALU.mult),
                        reads=['identf'] + col_tags, writes=[f"TF{4 + ri}"])
                    S.op('pe', lambda e, ri=ri, bk=bk: e.matmul(
                        ps[:, bk, :], lhsT=onesf[:], rhs=rep4[:, ri, :], start=True, stop=True),
                        reads=[f"TF{4 + ri}", 'onesf'], writes=[f"ps{bk}"])
                    S.op('act', lambda e, hf=hf, bk=bk: e.copy(out=dst[:, hf * 512:(hf + 1) * 512], in_=ps[:, bk, :]),
                         reads=[f"ps{bk}"], writes=[f"{dst_tag}{hf}"])

            gcount = [0]

            def mod_layer(L, spar):
                gstage = [(Gb[:, spar, 0, :], f'G{spar}_0_'), (Gb[:, spar, 1, :], f'G{spar}_1_')]
                for s in range(6):
                    slot = acquire(('wmod', L, s))
                    w3 = ring[:, slot, :].rearrange("p (kc n) -> p kc n", kc=8)
                    mi = (L * 6 + s) % 2
                    mrow = TF[0:NSEQ, 2 * mi:2 * mi + 2, :].rearrange("p a t -> p (a t)")
                    mtags = [f"TF{2 * mi}", f"TF{2 * mi + 1}"]
                    for ch in range(2):
                        bka = bank('A')
                        for kc in range(8):
                            S.op('pe', lambda e, ch=ch, kc=kc, w3=w3, bka=bka: e.matmul(
                                ps[0:NSEQ, bka, :], lhsT=csT[:, kc, :], rhs=w3[:, kc, ch * 512:(ch + 1) * 512],
                                start=(kc == 0), stop=(kc == 7)),
                                reads=[f"ring{slot}", 'csT'], writes=[f"ps{bka}"])
                        S.op('act', lambda e, ch=ch, bka=bka, mrow=mrow: e.copy(
                            out=mrow[:, ch * 512:(ch + 1) * 512], in_=ps[0:NSEQ, bka, :]),
                            reads=[f"ps{bka}"], writes=[mtags[ch]])
                    bk = bank('B')
                    for j in range(8):
                        S.op('pe', lambda e, j=j, bk=bk, mrow=mrow: e.transpose(
                            ps[:, bk, j * NSEQ:(j + 1) * NSEQ], mrow[:, j * 128:(j + 1) * 128], identf[0:NSEQ, 0:NSEQ]),
                            reads=mtags + ['identf'], writes=[f"ps{bk}"])
                    c0 = lay[('bmod', L)] + s * 8
                    S.op('dve', lambda e, L=L, s=s, bk=bk, c0=c0: e.tensor_tensor(
                        out=modT[:, L * 6 + s, :, :],
                        in0=ps[:, bk, 0:8 * NSEQ].rearrange("p (j b) -> p j b", j=8),
                        in1=colsb[:, c0:c0 + 8].unsqueeze(2).to_broadcast([128, 8, NSEQ]), op=ALU.add),
                        reads=[f"ps{bk}", 'colsb'], writes=[f"modT{L}_{s}"])
                    release()
                    if s % 2 == 1 and (L * 3 + s // 2 + 1) < DEPTH:
                        pass
                cm = lay[('nmix', L)]
                S.op('dve', lambda e, L=L, cm=cm: e.scalar_tensor_tensor(
                    out=A1[:, L, :, :], in0=modT[:, L * 6 + 1, :, :], scalar=1.0,
                    in1=colsb[:, cm:cm + 8].unsqueeze(2).to_broadcast([128, 8, NSEQ]), op0=ALU.add, op1=ALU.mult),
                    reads=[f"modT{L}_1", 'colsb'], writes=[f"A1_{L}"])
                cm2 = lay[('nmlp', L)]
                S.op('dve', lambda e, L=L, cm2=cm2: e.scalar_tensor_tensor(
                    out=A2[:, L, :, :], in0=modT[:, L * 6 + 4, :, :], scalar=1.0,
                    in1=colsb[:, cm2:cm2 + 8].unsqueeze(2).to_broadcast([128, 8, NSEQ]), op0=ALU.add, op1=ALU.mult),
                    reads=[f"modT{L}_4", 'colsb'], writes=[f"A2_{L}"])
                for b_ in range(NSEQ):
                    for wh, sec in ((0, 2), (1, 5)):
                        stg, stag = gstage[gcount[0] % len(gstage)]
                        gcount[0] += 1
                        bcast_tile(stg, stag, lambda hf, L=L, b_=b_, sec=sec: modT[:, L * 6 + sec, hf * 4:(hf + 1) * 4, b_],
                                   [f"modT{L}_{sec}"])
                        gi = (L * NSEQ + b_) * 2 + wh
                        S.op('act', lambda e, gi=gi, stg=stg: e.dma_start(out=gsc[gi], in_=stg),
                             reads=[f"{stag}0", f"{stag}1"], writes=[f"gsc{gi}"], dma=True)
            mod_layer(0, 1)
            _stage('mod')
            cfn = lay[('fn',)]
            bcast_tile(FN, 'FN', lambda hf: colsb[:, cfn + hf * 4:cfn + (hf + 1) * 4], ['colsb'])

            def load_gates(L, b, par):
                for wh in range(2):
                    gi = (L * NSEQ + b) * 2 + wh
                    S.op('act', lambda e, gi=gi, wh=wh: e.dma_start(out=Gb[:, par, wh, :], in_=gsc[gi]),
                         reads=[f"gsc{gi}"], writes=[f"G{par}_{wh}_0", f"G{par}_{wh}_1"], dma=True)
            _stage('fn')

            def xtags(xb, tt, qs=range(4)):
                return [f"x{xb}_{tt}_{q}" for q in qs]

            def norm_to_hT(xb, Acol, Scol, col_tags):
                for tt in range(NT):
                    xrow = xs[:, xb * NT + tt, :]
                    S.op('act', lambda e, tt=tt, xrow=xrow: e.activation(
                        out=ntm[:, tt, :], in_=xrow, func=AF.Square, accum_out=ss[:, tt:tt + 1]),
                        reads=xtags(xb, tt), writes=[f"ntm{tt}", f"ss{tt}"])
                    S.op('act', lambda e, tt=tt: e.activation(
                        out=sd[:, tt:tt + 1], in_=ss[:, tt:tt + 1], func=AF.Sqrt, scale=1.0 / D, bias=EPS),
                        reads=[f"ss{tt}"], writes=[f"sd{tt}"])
                    S.op('dve', lambda e, tt=tt: e.reciprocal(out=rs[:, tt:tt + 1], in_=sd[:, tt:tt + 1]),
                         reads=[f"sd{tt}"], writes=[f"rs{tt}"])
                    S.op('dve', lambda e, tt=tt, xrow=xrow: e.tensor_scalar(
                        out=ntm[:, tt, :], in0=xrow, scalar1=rs[:, tt:tt + 1], scalar2=None, op0=ALU.mult),
                        reads=xtags(xb, tt) + [f"rs{tt}"], writes=[f"ntm{tt}"])
                for kc in range(8):
                    bk = bank('B')
                    for tt in range(NT):
                        S.op('pe', lambda e, kc=kc, tt=tt, bk=bk: e.transpose(
                            ps[:, bk, tt * 128:(tt + 1) * 128], ntm[:, tt, kc * 128:(kc + 1) * 128], identf[:]),
                            reads=[f"ntm{tt}", 'identf'], writes=[f"ps{bk}"])
                    if kc % 2 == 0:
                        S.op('dve', lambda e, kc=kc, bk=bk: e.tensor_scalar(
                            out=hT[:, kc, :], in0=ps[:, bk, :], scalar1=Acol(kc), scalar2=Scol(kc),
                            op0=ALU.mult, op1=ALU.add),
                            reads=[f"ps{bk}"] + col_tags, writes=[f"hT{kc}"])
                    else:
                        S.op('act', lambda e, kc=kc, bk=bk: e.activation(
                            out=hT[:, kc, :], in_=ps[:, bk, :], func=AF.Identity, scale=Acol(kc), bias=Scol(kc)),
                            reads=[f"ps{bk}"] + col_tags, writes=[f"hT{kc}"])

            def proj_residual(xb, slot, src_tags_fn, lhs_fn, nk, par, bias_row, pr, tts=range(NT)):
                w3 = ring[:, slot, :].rearrange("p (kc n) -> p kc n", kc=nk)
                for tt in tts:
                    for hf in range(2):
                        bk = bank('A')
                        for kc in range(nk):
                            S.op('pe', lambda e, kc=kc, tt=tt, hf=hf, bk=bk: e.matmul(
                                ps[:, bk, :], lhsT=lhs_fn(kc, tt), rhs=w3[:, kc, hf * 512:(hf + 1) * 512],
                                start=(kc == 0), stop=False),
                                reads=[f"ring{slot}"] + src_tags_fn(kc), writes=[f"ps{bk}"])
                        S.op('pe', lambda e, hf=hf, bk=bk: e.matmul(
                            ps[:, bk, :], lhsT=onesb[pr:pr + 1, :], rhs=bias_row(hf), start=False, stop=True),
                            reads=['onesb', 'rowsb'], writes=[f"ps{bk}"])
                        ti = tf_hi()
                        S.op('dve', lambda e, hf=hf, bk=bk, ti=ti: e.tensor_tensor(
                            out=TF[:, ti, :], in0=ps[:, bk, :], in1=Gb[:, par, 0, hf * 512:(hf + 1) * 512], op=ALU.mult),
                            reads=[f"ps{bk}", f"G{par}_0_{hf}"], writes=[f"TF{ti}"])
                        xv = xs[:, xb * NT + tt, hf * 512:(hf + 1) * 512]
                        S.op('dve', lambda e, xv=xv, ti=ti: e.tensor_tensor(out=xv, in0=xv, in1=TF[:, ti, :], op=ALU.add),
                             reads=[f"TF{ti}"] + xtags(xb, tt, (2 * hf, 2 * hf + 1)),
                             writes=xtags(xb, tt, (2 * hf, 2 * hf + 1)))

            def attention(L, b, u, xb, par):
                a = L // 2
                pr, ri = 32 * (L % 3), L // 3
                kTa, Va = kT[a], Vt[a]
                if u > 0:
                    S.op('act', lambda e: e.copy(out=kTa[:, :, 0:128], in_=kTa[:, :, T:T + 128]),
                         reads=[f"kT{a}_3"], writes=[f"kT{a}_c"])
                    S.op('act', lambda e: e.copy(out=Va[:, 0, :], in_=Va[:, NT, :]),
                         reads=[f"V{a}_3"], writes=[f"V{a}_c"])
                slot = acquire(('w', L, 'Q'))
                w3 = ring[:, slot, :].rearrange("p (kc n) -> p kc n", kc=8)
                for c in range(8):
                    bk = bank('A')
                    for kc in range(8):
                        S.op('pe', lambda e, c=c, kc=kc, bk=bk, w3=w3: e.matmul(
                            ps[:, bk, :], lhsT=w3[:, kc, c * 128:(c + 1) * 128], rhs=hT[:, kc, :],
                            start=(kc == 0), stop=(kc == 7)),
                            reads=[f"ring{slot}", f"hT{kc}"], writes=[f"ps{bk}"])
                    S.op('act', lambda e, c=c, bk=bk: e.activation(
                        out=qT[:, c, :], in_=ps[:, bk, :], func=AF.Identity, scale=0.125, bias=bq8[:, a, c:c + 1]),
                        reads=[f"ps{bk}", 'bq8'], writes=[f"qT{c}"])
                release()
                _stage('attQ')
                slot = acquire(('w', L, 'KV'))
                w3 = ring[:, slot, 0:8 * 384].rearrange("p (kc n) -> p kc n", kc=8)
                for c2 in range(2):
                    bk = bank('A')
                    for kc in range(8):
                        S.op('pe', lambda e, c2=c2, kc=kc, bk=bk, w3=w3: e.matmul(
                            ps[:, bk, :], lhsT=w3[:, kc, c2 * 128:(c2 + 1) * 128], rhs=hT[:, kc, :],
                            start=(kc == 0), stop=(kc == 7)),
                            reads=[f"ring{slot}", f"hT{kc}"], writes=[f"ps{bk}"])
                    S.op('act', lambda e, c2=c2, bk=bk: e.activation(
                        out=kTa[:, c2, 128:128 + T], in_=ps[:, bk, :], func=AF.Identity,
                        bias=col(('bk2', L), c2)),
                        reads=[f"ps{bk}", 'colsb'], writes=[f"kT{a}_{i}" for i in range(NT)])
                for tt in range(NT):
                    bk = bank('A')
                    for kc in range(8):
                        S.op('pe', lambda e, tt=tt, kc=kc, bk=bk, w3=w3: e.matmul(
                            ps[:, bk, 0:128], lhsT=hT[:, kc, tt * 128:(tt + 1) * 128], rhs=w3[:, kc, 256:384],
                            start=(kc == 0), stop=False),
                            reads=[f"ring{slot}", f"hT{kc}"], writes=[f"ps{bk}"])
                    S.op('pe', lambda e, bk=bk: e.matmul(
                        ps[:, bk, 0:128], lhsT=onesb[pr:pr + 1, :], rhs=rowsb[pr:pr + 1, ri, 0:128],
                        start=False, stop=True),
                        reads=['onesb', 'rowsb'], writes=[f"ps{bk}"])
                    S.op('act', lambda e, tt=tt, bk=bk: e.copy(out=Va[:, 1 + tt, :], in_=ps[:, bk, 0:128]),
                         reads=[f"ps{bk}"], writes=[f"V{a}_{tt}"])
                release()
                _stage('attKV')
                def ktag(slot_i):
                    return f"kT{a}_c" if slot_i == 0 else f"kT{a}_{slot_i - 1}"

                def vtag(slot_i):
                    return f"V{a}_c" if slot_i == 0 else f"V{a}_{slot_i - 1}"

                pairs = [(i, hf, p2) for i in range(NT) for hf in range(2) for p2 in range(2)]
                pstate = {}
                hstate = {}

                def stage_a(pidx):
                    i, hf, p2 = pairs[pidx]
                    has_prev = not (u == 0 and i == 0)
                    kbs = [0, 1] if has_prev else [1]
                    c0 = hf * 4 + p2 * 2
                    kvh = c0 // 4
                    sbanks = (bank('A'), bank('A'))
                    for cq in range(2):
                        c_ = c0 + cq
                        for hh in range(2):
                            S3 = ps[:, sbanks[hh], :].rearrange("p (q k t) -> p q k t", q=2, k=2)
                            for kb in kbs:
                                S.op('pe', lambda e, hh=hh, kb=kb, c_=c_, cq=cq, S3=S3: e.matmul(
                                    S3[:, cq, kb, :],
                                    lhsT=kTa[hh * 64:(hh + 1) * 64, kvh, (i + kb) * 128:(i + kb + 1) * 128],
                                    rhs=qT[hh * 64:(hh + 1) * 64, c_, i * 128:(i + 1) * 128],
                                    start=True, stop=True),
                                    reads=[ktag(i + kb), f"qT{c_}"], writes=[f"ps{sbanks[hh]}"])
                    info = []
                    for cq in range(2):
                        c = c0 + cq
                        sviews = [ps[:, sbanks[hh], :].rearrange("p (q k t) -> p q k t", q=2, k=2)[:, cq, :, :]
                                  for hh in range(2)]
                        ti = tf_lo()
                        T4 = TF[:, ti, :].rearrange("p (h k t) -> p h k t", h=2, k=2)
                        for hh in range(2):
                            h = 2 * c + hh
                            if has_prev:
                                S.op('dve', lambda e, hh=hh, h=h, sv=sviews[hh], T4=T4: e.scalar_tensor_tensor(
                                    out=T4[:, hh, :, :], in0=DM[:].rearrange("p (k t) -> p k t", k=2),
                                    scalar=slopes[h], in1=sv, op0=ALU.mult, op1=ALU.add),
                                    reads=['DM', f"ps{sbanks[hh]}"], writes=[f"TF{ti}"])
                            else:
                                S.op('dve', lambda e, hh=hh, h=h, sv=sviews[hh], T4=T4: e.scalar_tensor_tensor(
                                    out=T4[:, hh, 1, :], in0=DM[:, 128:256],
                                    scalar=slopes[h], in1=sv[:, 1, :], op0=ALU.mult, op1=ALU.add),
                                    reads=['DM', f"ps{sbanks[hh]}"], writes=[f"TF{ti}"])
                        pi = tb_next(1)
                        P4 = TB[:, pi, :].rearrange("p (h k t) -> p h k t", h=2, k=2)
                        if has_prev:
                            S.op('act', lambda e, ti=ti, pi=pi: e.activation(out=TB[:, pi, :], in_=TF[:, ti, :], func=AF.Exp),
                                 reads=[f"TF{ti}"], writes=[f"TB{pi}"])
                        else:
                            S.op('act', lambda e, T4=T4, P4=P4: e.activation(out=P4[:, :, 1, :], in_=T4[:, :, 1, :], func=AF.Exp),
                                 reads=[f"TF{ti}"], writes=[f"TB{pi}"])
                        info.append((c, pi, P4))
                    pstate[pidx] = (kbs, kvh, info)

                def stage_b(pidx):
                    i, hf, p2 = pairs[pidx]
                    kbs, kvh, info = pstate.pop(pidx)
                    if p2 == 0:
                        hstate[(i, hf)] = (bank('C'), bank('B'))
                    nb, db = hstate[(i, hf)]
                    for (c, pi, P4) in info:
                        cc = c % 4
                        for hh in range(2):
                            for n_i, kb in enumerate(kbs):
                                st, sp_ = (n_i == 0), (n_i == len(kbs) - 1)
                                S.op('pe', lambda e, hh=hh, kb=kb, cc=cc, P4=P4, st=st, sp_=sp_: e.matmul(
                                    ps[hh * 64:(hh + 1) * 64, nb, cc * 128:(cc + 1) * 128],
                                    lhsT=Va[:, i + kb, kvh * 64:(kvh + 1) * 64], rhs=P4[:, hh, kb, :],
                                    start=st, stop=sp_),
                                    reads=[vtag(i + kb), f"TB{pi}"], writes=[f"ps{nb}"])
                                S.op('pe', lambda e, hh=hh, kb=kb, cc=cc, P4=P4, st=st, sp_=sp_: e.matmul(
                                    ps[hh * 64:(hh + 1) * 64, db, cc * 128:(cc + 1) * 128],
                                    lhsT=onesb[:, 0:64], rhs=P4[:, hh, kb, :], start=st, stop=sp_),
                                    reads=['onesb', f"TB{pi}"], writes=[f"ps{db}"])
                    if p2 == 1:
                        ri2 = tf_hi()
                        R3 = TF[:, ri2, :].rearrange("p (c t) -> p c t", c=4)
                        S.op('dve', lambda e, R3=R3: e.tensor_tensor(
                            out=R3, in0=ps[:, db, :].rearrange("p (c t) -> p c t", c=4),
                            in1=esink[:, a, hf * 4:(hf + 1) * 4].unsqueeze(2).to_broadcast([128, 4, 128]), op=ALU.add),
                            reads=[f"ps{db}", 'esink'], writes=[f"TF{ri2}"])
                        S.op('act', lambda e: e.activation(out=TF[:, ri2, :], in_=TF[:, ri2, :], func=AF.Ln),
                             reads=[f"TF{ri2}"], writes=[f"TF{ri2}"])
                        S.op('act', lambda e: e.activation(out=TF[:, ri2, :], in_=TF[:, ri2, :], func=AF.Exp, scale=-1.0),
                             reads=[f"TF{ri2}"], writes=[f"TF{ri2}"])
                        S.op('dve', lambda e, R3=R3: e.tensor_tensor(
                            out=hT[:, hf * 4:(hf + 1) * 4, i * 128:(i + 1) * 128],
                            in0=ps[:, nb, :].rearrange("p (c t) -> p c t", c=4), in1=R3, op=ALU.mult),
                            reads=[f"ps{nb}", f"TF{ri2}"],
                            writes=([f"hT{c_}" for c_ in range(hf * 4, hf * 4 + 4)] if i == 0 else []) +
                                   [f"oT{c_}_{i}" for c_ in range(hf * 4, hf * 4 + 4)])

                oslot = acquire(('w', L, 'O'))

                def oproj(tt_):
                    proj_residual(xb, oslot, lambda kc: [f"hT{kc}", f"oT{kc}_{tt_}"],
                                  lambda kc, tt: hT[:, kc, tt * 128:(tt + 1) * 128], 8, par,
                                  lambda hf: rowsb[pr:pr + 1, ri, 128 + hf * 512:128 + (hf + 1) * 512], pr,
                                  tts=[tt_])

                stage_a(0)
                for pidx in range(len(pairs)):
                    if pidx + 1 < len(pairs):
                        stage_a(pidx + 1)
                    stage_b(pidx)
                    if OPROJ_INTERLEAVE and pidx % 4 == 1 and pidx // 4 >= 1:
                        oproj(pidx // 4 - 1)
                if OPROJ_INTERLEAVE:
                    oproj(NT - 1)
                else:
                    for tt_ in range(NT):
                        oproj(tt_)
                release()
                _stage('attN')

            def conformer(L, b, u, xb, par):
                cv = L // 2
                pr, ri = 32 * (L % 3), L // 3
                uc = ucar[cv]
                sbk1, sbk2 = bank('B'), bank('B')
                pend = []

                def flush_stats():
                    while pend:
                        c, pi = pend.pop(0)
                        S.op('pe', lambda e, c=c, pi=pi: e.matmul(
                            ps[:, sbk1, :], lhsT=onesb[:], rhs=TB[:, pi, :], start=(c == 0), stop=(c == 7)),
                            reads=['onesb', f"TB{pi}"], writes=[f"ps{sbk1}"])
                        S.op('pe', lambda e, c=c, pi=pi: e.matmul(
                            ps[:, sbk2, :], lhsT=onesb[:], rhs=TB[:, pi + 1, :], start=(c == 0), stop=(c == 7)),
                            reads=['onesb', f"TB{pi + 1}"], writes=[f"ps{sbk2}"])

                for i2 in range(2):
                    slot = acquire(('w', L, f'P1{i2}'))
                    w3 = ring[:, slot, :].rearrange("p (kc n) -> p kc n", kc=8)
                    for cc in range(4):
                        c = i2 * 4 + cc
                        bka = bank('A')
                        bkg = bank('A')
                        for kc in range(8):
                            S.op('pe', lambda e, cc=cc, kc=kc, bka=bka, w3=w3: e.matmul(
                                ps[:, bka, :], lhsT=w3[:, kc, cc * 128:(cc + 1) * 128], rhs=hT[:, kc, :],
                                start=(kc == 0), stop=(kc == 7)),
                                reads=[f"ring{slot}", f"hT{kc}"], writes=[f"ps{bka}"])
                        for kc in range(8):
                            S.op('pe', lambda e, cc=cc, kc=kc, bkg=bkg, w3=w3: e.matmul(
                                ps[:, bkg, :], lhsT=w3[:, kc, 512 + cc * 128:512 + (cc + 1) * 128], rhs=hT[:, kc, :],
                                start=(kc == 0), stop=(kc == 7)),
                                reads=[f"ring{slot}", f"hT{kc}"], writes=[f"ps{bkg}"])
                        ti = tf_lo()
                        S.op('act', lambda e, c=c, bkg=bkg, ti=ti: e.activation(
                            out=TF[:, ti, :], in_=ps[:, bkg, :], func=AF.Sigmoid, bias=col(('b1g', L), c)),
                            reads=[f"ps{bkg}", 'colsb'], writes=[f"TF{ti}"])
                        ub = ub8[:, c, :]
                        if u > 0:
                            S.op('dve', lambda e, c=c, ub=ub: e.tensor_copy(out=ub[:, 0:CW - 1], in_=uc[:, c, :]),
                                 reads=[f"ucar{cv}_{c}"], writes=[f"ub{c}"])
                        else:
                            S.op('dve', lambda e, ub=ub: e.memset(ub[:, 0:CW - 1], 0.0), writes=[f"ub{c}"])
                        S.op('dve', lambda e, c=c, ub=ub, bka=bka, ti=ti: e.scalar_tensor_tensor(
                            out=ub[:, CW - 1:CW - 1 + T], in0=ps[:, bka, :], scalar=col(('b1a', L), c),
                            in1=TF[:, ti, :], op0=ALU.add, op1=ALU.mult),
                            reads=[f"ps{bka}", f"TF{ti}", 'colsb'], writes=[f"ub{c}"])
                        S.op('dve', lambda e, c=c, ub=ub: e.tensor_copy(out=uc[:, c, :], in_=ub[:, T:T + CW - 1]),
                             reads=[f"ub{c}"], writes=[f"ucar{cv}_{c}"])
                    release()
                    for kk in range(2):
                        slot = acquire(('w', L, f'DG{2 * i2 + kk}'))
                        dg = ring[:, slot, :]
                        for ci in range(2):
                            c = i2 * 4 + kk * 2 + ci
                            ub = ub8[:, c, :]
                            bk = bank('A')
                            for j in range(CW):
                                m = ci * CW + j
                                S.op('pe', lambda e, j=j, m=m, bk=bk, ub=ub, dg=dg: e.matmul(
                                    ps[:, bk, :], lhsT=dg[:, m * 128:(m + 1) * 128], rhs=ub[:, j:j + T],
                                    start=(j == 0), stop=(j == CW - 1)),
                                    reads=[f"ring{slot}", f"ub{c}"], writes=[f"ps{bk}"])
                            vc = aTf[:, c * T:(c + 1) * T]
                            vtags = [f"aT{2 * c}", f"aT{2 * c + 1}"]
                            S.op('act', lambda e, c=c, bk=bk, vc=vc: e.activation(
                                out=vc, in_=ps[:, bk, :], func=AF.Identity, bias=col(('bdw', L), c)),
                                reads=[f"ps{bk}", 'colsb'], writes=vtags)
                            pi = tb_next(2)
                            S.op('act', lambda e, c=c, bk=bk, pi=pi: e.activation(
                                out=TB[:, pi, :], in_=ps[:, bk, :], func=AF.Identity, bias=col(('bdw', L), c)),
                                reads=[f"ps{bk}", 'colsb'], writes=[f"TB{pi}"])
                            S.op('act', lambda e, c=c, bk=bk, pi=pi: e.activation(
                                out=TB[:, pi + 1, :], in_=ps[:, bk, :], func=AF.Square, bias=col(('bdw', L), c)),
                                reads=[f"ps{bk}", 'colsb'], writes=[f"TB{pi + 1}"])
                            flush_stats()
                            pend.append((c, pi))
                        release()
                flush_stats()
                S.op('act', lambda e: e.activation(out=TF[:, 4, :], in_=ps[:, sbk1, :], func=AF.Copy, scale=1.0 / D),
                     reads=[f"ps{sbk1}"], writes=['TF4'])
                S.op('dve', lambda e: e.tensor_tensor(out=TF[:, 5, :], in0=TF[:, 4, :], in1=TF[:, 4, :], op=ALU.mult),
                     reads=['TF4'], writes=['TF5'])
                S.op('dve', lambda e: e.scalar_tensor_tensor(
                    out=TF[:, 5, :], in0=ps[:, sbk2, :], scalar=1.0 / D, in1=TF[:, 5, :], op0=ALU.mult, op1=ALU.subtract),
                    reads=[f"ps{sbk2}", 'TF5'], writes=['TF5'])
                S.op('act', lambda e: e.activation(out=TF[:, 5, :], in_=TF[:, 5, :], func=AF.Sqrt, bias=EPS),
                     reads=['TF5'], writes=['TF5'])
                S.op('dve', lambda e: e.reciprocal(out=TF[:, 5, :], in_=TF[:, 5, :]), reads=['TF5'], writes=['TF5'])
                for c in range(8):
                    vc = aTf[:, c * T:(c + 1) * T]
                    vtags = [f"aT{2 * c}", f"aT{2 * c + 1}"]
                    zi = tf_hi()
                    S.op('dve', lambda e, vc=vc, zi=zi: e.tensor_tensor(out=TF[:, zi, :], in0=vc, in1=TF[:, 4, :], op=ALU.subtract),
                         reads=vtags + ['TF4'], writes=[f"TF{zi}"])
                    S.op('dve', lambda e, zi=zi: e.tensor_tensor(out=TF[:, zi, :], in0=TF[:, zi, :], in1=TF[:, 5, :], op=ALU.mult),
                         reads=[f"TF{zi}", 'TF5'], writes=[f"TF{zi}"])
                    S.op('act', lambda e, c=c, zi=zi: e.activation(
                        out=hT[:, c, :], in_=TF[:, zi, :], func=AF.Silu, scale=col(('lng', L), c), bias=col(('lnb', L), c)),
                        reads=[f"TF{zi}", 'colsb'], writes=[f"hT{c}"])
                slot = acquire(('w', L, 'P2'))
                proj_residual(xb, slot, lambda kc: [f"hT{kc}"], lambda kc, tt: hT[:, kc, tt * 128:(tt + 1) * 128], 8,
                              par, lambda hf: rowsb[pr:pr + 1, ri, hf * 512:(hf + 1) * 512], pr)
                release()

            def mlp(L, b, u, xb, par):
                for s in range(4):
                    slot = acquire(('w', L, f'U{s}'))
                    w3 = ring[:, slot, :].rearrange("p (kc n) -> p kc n", kc=8)
                    for f in range(8):
                        bk = bank('A')
                        for kc in range(8):
                            S.op('pe', lambda e, f=f, kc=kc, bk=bk, w3=w3: e.matmul(
                                ps[:, bk, :], lhsT=w3[:, kc, f * 128:(f + 1) * 128], rhs=hT[:, kc, :],
                                start=(kc == 0), stop=(kc == 7)),
                                reads=[f"ring{slot}", f"hT{kc}"], writes=[f"ps{bk}"])
                        ti = tf_lo()
                        S.op('act', lambda e, bk=bk, ti=ti: e.activation(out=TF[:, ti, :], in_=ps[:, bk, :], func=AF.Relu),
                             reads=[f"ps{bk}"], writes=[f"TF{ti}"])
                        ch = s * 8 + f
                        S.op('dve', lambda e, ch=ch, ti=ti: e.tensor_tensor(
                            out=aT[:, ch, :], in0=TF[:, ti, :], in1=TF[:, ti, :], op=ALU.mult),
                            reads=[f"TF{ti}"], writes=[f"aT{ch}"])
                    release()
                for s in range(4):
                    slot = acquire(('w', L, f'D{s}'))
                    w3 = ring[:, slot, :].rearrange("p (kc n) -> p kc n", kc=32)
                    for tt in range(NT):
                        bk = bank('A')
                        for kc in range(32):
                            S.op('pe', lambda e, tt=tt, kc=kc, bk=bk, w3=w3: e.matmul(
                                ps[:, bk, 0:256], lhsT=aT[:, kc, tt * 128:(tt + 1) * 128], rhs=w3[:, kc, :],
                                start=(kc == 0), stop=(kc == 31)),
                                reads=[f"ring{slot}", f"aT{kc}"], writes=[f"ps{bk}"])
                        ti = tf_hi()
                        S.op('dve', lambda e, s=s, bk=bk, ti=ti: e.tensor_tensor(
                            out=TF[:, ti, 0:256], in0=ps[:, bk, 0:256], in1=Gb[:, par, 1, s * 256:(s + 1) * 256], op=ALU.mult),
                            reads=[f"ps{bk}", f"G{par}_1_{s // 2}"], writes=[f"TF{ti}"])
                        xv = xs[:, xb * NT + tt, s * 256:(s + 1) * 256]
                        S.op('dve', lambda e, xv=xv, ti=ti: e.tensor_tensor(out=xv, in0=xv, in1=TF[:, ti, 0:256], op=ALU.add),
                             reads=[f"TF{ti}"] + xtags(xb, tt, (s,)), writes=xtags(xb, tt, (s,)))
                    release()

            un = 0
            ul_list = [(L_, b_) for b_ in range(NSEQ) for u_ in range(NU) for L_ in range(DEPTH)]
            ul = 0
            load_gates(ul_list[0][0], ul_list[0][1], 0)
            for b in range(NSEQ):
                for u in range(NU):
                    xb = un % NXB
                    if NXB == 1:
                        if un > 0:
                            load_x(b, u, xb)
                    elif un + 1 < NSEQ * NU:
                        nb_, nu_ = divmod(un + 1, NU)
                        load_x(nb_, nu_, (un + 1) % NXB)
                    for L in range(DEPTH):
                        par = ul % 2
                        nxt = ul_list[ul + 1] if ul + 1 < len(ul_list) else None
                        defer_gates = (un == 0 and L + 1 < DEPTH)
                        if nxt is not None and not defer_gates:
                            load_gates(nxt[0], nxt[1], (ul + 1) % 2)
                        ul += 1
                        _stage('gates')
                        norm_to_hT(xb, lambda kc, L=L, b=b: A1[:, L, kc, b:b + 1],
                                   lambda kc, L=L, b=b: modT[:, L * 6 + 0, kc, b:b + 1],
                                   [f"A1_{L}", f"modT{L}_0"])
                        _stage('norm1')
                        if L % 2 == 0:
                            attention(L, b, u, xb, par)
                        else:
                            conformer(L, b, u, xb, par)
                        if defer_gates:
                            mod_layer(L + 1, ul % 2)
                            load_gates(nxt[0], nxt[1], ul % 2)
                        _stage('mixer')
                        norm_to_hT(xb, lambda kc, L=L, b=b: A2[:, L, kc, b:b + 1],
                                   lambda kc, L=L, b=b: modT[:, L * 6 + 3, kc, b:b + 1],
                                   [f"A2_{L}", f"modT{L}_3"])
                        _stage('norm2')
                        mlp(L, b, u, xb, par)
                        _stage('mlp')
                    for tt in range(NT):
                        xrow = xs[:, xb * NT + tt, :]
                        S.op('act', lambda e, tt=tt, xrow=xrow: e.activation(
                            out=ntm[:, tt, :], in_=xrow, func=AF.Square, accum_out=ss[:, tt:tt + 1]),
                            reads=xtags(xb, tt), writes=[f"ntm{tt}", f"ss{tt}"])
                        S.op('act', lambda e, tt=tt: e.activation(
                            out=sd[:, tt:tt + 1], in_=ss[:, tt:tt + 1], func=AF.Sqrt, scale=1.0 / D, bias=EPS),
                            reads=[f"ss{tt}"], writes=[f"sd{tt}"])
                        S.op('dve', lambda e, tt=tt: e.reciprocal(out=rs[:, tt:tt + 1], in_=sd[:, tt:tt + 1]),
                             reads=[f"sd{tt}"], writes=[f"rs{tt}"])
                        S.op('dve', lambda e, tt=tt, xrow=xrow: e.scalar_tensor_tensor(
                            out=ntm[:, tt, :], in0=xrow, scalar=rs[:, tt:tt + 1], in1=FN[:], op0=ALU.mult, op1=ALU.mult),
                            reads=xtags(xb, tt) + [f"rs{tt}", 'FN0', 'FN1'], writes=[f"ntm{tt}"])
                        ot = f"out{un}_{tt}"
                        out_tags.append(ot)
                        S.op('pool', lambda e, tt=tt, b=b, u=u: e.dma_start(
                            out=out_d[b, u * T + tt * 128:u * T + (tt + 1) * 128, :], in_=ntm[:, tt, :]),
                            reads=[f"ntm{tt}"], writes=[ot], dma=True)
                    un += 1

        stopped = False
        try:
            _body()
        except StopBuild:
            stopped = True
        assert stopped or wstate['cur'] == len(seq), (wstate, len(seq))
        S.op('pool', lambda e: e.nop(), reads=out_tags)
        S.emit()
    return nc, S


_CACHE = {}


def kernel(**inputs):
    NCORES = 8
    inp = {k: np.asarray(v) for k, v in inputs.items()}
    B, SEQ, _ = inp['x'].shape
    DEPTH = inp['w_mod'].shape[0]
    NSEQ = B // NCORES
    key = (NSEQ, SEQ, DEPTH)
    if key not in _CACHE:
        _CACHE[key] = build_program(NSEQ, SEQ, DEPTH)[0]
    nc = _CACHE[key]
    cols, rows = prep_shared(inp, DEPTH)
    dm, ident = host_constants()
    f32c = lambda a: np.ascontiguousarray(np.asarray(a, dtype=np.float32))
    shared = dict(cols=cols, rows=rows.reshape(128, 2 * 1152), dm=dm, ident=ident,
                  w_mod=f32c(inp['w_mod']), w_qkv=f32c(inp['w_qkv']), w_o=f32c(inp['w_o']),
                  w_pw1=f32c(inp['w_pw1']), w_pw2=f32c(inp['w_pw2']),
                  w_up=f32c(inp['w_up']), w_down=f32c(inp['w_down']))
    x = f32c(inp['x'])
    c = f32c(inp['c'])
    in_maps = []
    for ci in range(NCORES):
        cc = c[ci * NSEQ:(ci + 1) * NSEQ]
        cT = np.ascontiguousarray(cc.reshape(NSEQ, 8, 128).transpose(2, 1, 0)).reshape(128, 8 * NSEQ)
        m = dict(shared)
        m['x'] = x[ci * NSEQ:(ci + 1) * NSEQ]
        m['cT'] = cT
        in_maps.append(m)
    res = run_bass_kernel_spmd(nc, in_maps, core_ids=list(range(NCORES)))
    out = np.concatenate([np.asarray(r['out']) for r in res.results], axis=0)
    return out.astype(np.float32, copy=False)
```
